# Optimizing a Trainium2 kernel written in Bass

```python
import math
import jax
import jax.numpy as jnp
from jax import lax
import numpy as np

D_MODEL = 1024
BATCH = 16
SEQ = 256
DEPTH = 4
DEC_BATCH = 2
DEC_SEQ = 1024
PAST_LEN = 256

F32 = jnp.float32
GRID_W = 64
N_ADA = 9
D_FF = 2816
GROUP_W = D_MODEL // 4
MIX_W = 4 * GROUP_W
CHUNK = 64
Q_BLOCK = 128
RMS_EPS = 1e-6

HG_HEADS = 4
HG_DK = GROUP_W // HG_HEADS
HG_DV = GROUP_W // HG_HEADS
HY_W = GROUP_W
HY_EMB = 33
HY_BANDS = (HY_EMB - 1) // 2
HY_FH = 64
HY_TARGET = 1e-2
HY_FAST = 0.3
HY_SLOW = 1.5
MLA_HEADS = 4
MLA_NOPE = 64
MLA_ROPE = 32
MLA_V = GROUP_W // MLA_HEADS
MLA_Q_LORA = 256
MLA_KV_LORA = 128
ROPE_BASE = 10000.0
GD_HEADS = 4
GD_DK = 64
GD_DV = GROUP_W // GD_HEADS

HG_COLS = 5 * GROUP_W
HY_COLS = 3 * HY_W
MLA_COLS = MLA_Q_LORA + MLA_KV_LORA + MLA_ROPE
GD_COLS = 4 * GROUP_W + 4 * GD_HEADS
IN_COLS = HG_COLS + HY_COLS + MLA_COLS + GD_COLS

kernel_name = 'hybrid_diffusion_hgrn2_hyena_mla_gdn_step'


def rms_norm(x, g):
    xf = x.astype(F32)
    y = xf * lax.rsqrt(jnp.mean(xf * xf, axis=-1, keepdims=True) + RMS_EPS)
    return (y * g.astype(F32)).astype(x.dtype)


def l2_norm(x):
    return x * lax.rsqrt(jnp.sum(x * x, axis=-1, keepdims=True) + 1e-6)


def heads(x, n):
    B, T, _ = x.shape
    return x.reshape(B, T, n, -1).transpose(0, 2, 1, 3)


def merge_heads(x):
    B, n, T, d = x.shape
    return x.transpose(0, 2, 1, 3).reshape(B, T, n * d)


def head_norm_gate(o, g_norm, gate):
    o = rms_norm(o.transpose(0, 2, 1, 3), g_norm)
    B, T, H, d = o.shape
    return o.reshape(B, T, H * d) * jax.nn.silu(gate)


def conv3_centred(x, w):
    xp = jnp.pad(x, ((0, 0), (1, 1), (0, 0)))
    return xp[:, :-2] * w[0] + xp[:, 1:-1] * w[1] + xp[:, 2:] * w[2]


def swiglu(h, w_gu, w_down):
    gate, up = jnp.split(h @ w_gu, 2, axis=-1)
    return (jax.nn.silu(gate) * up) @ w_down


def adaln(cond, w, b):
    return (jax.nn.silu(cond) @ w + b).reshape(cond.shape[0], N_ADA, D_MODEL)


def rope_tables(rows):
    T = rows * GRID_W
    row = jnp.repeat(jnp.arange(rows, dtype=F32), GRID_W)
    col = (jnp.arange(T) % GRID_W).astype(F32)
    pairs = MLA_ROPE // 4
    inv = ROPE_BASE ** (-jnp.arange(pairs, dtype=F32) / pairs)
    ang = jnp.concatenate([row[:, None] * inv, col[:, None] * inv], axis=-1)
    return jnp.cos(ang), jnp.sin(ang)


def apply_rope(x, cos, sin):
    xn, xr = x[..., :MLA_NOPE], x[..., MLA_NOPE:]
    x1, x2 = jnp.split(xr, 2, axis=-1)
    return jnp.concatenate([xn, x1 * cos - x2 * sin, x2 * cos + x1 * sin], axis=-1)


def attention(q, k, v, scale):
    B, H, T, dq = q.shape
    nb = T // Q_BLOCK
    qb = q.reshape(B, H, nb, Q_BLOCK, dq).transpose(2, 0, 1, 3, 4)

    def block(qi):
        s = jnp.einsum('bhqd,bhkd->bhqk', qi, k).astype(F32) * scale
        p = jax.nn.softmax(s, axis=-1)
        return jnp.einsum('bhqk,bhkd->bhqd', p.astype(v.dtype), v)

    o = lax.map(block, qb)
    return o.transpose(1, 2, 0, 3, 4).reshape(B, H, T, -1)


def hgrn2_chunk_scan(q, k, v, log_f, s0):
    B, H, T, dk = q.shape
    n = T // CHUNK
    q, k, v, log_f = (a.reshape(B, H, n, CHUNK, a.shape[-1]) for a in (q, k, v, log_f))
    b = jnp.cumsum(log_f, axis=3)
    causal = jnp.tril(jnp.ones((CHUNK, CHUNK), bool))
    rel = jnp.where(causal[:, :, None], b[:, :, :, :, None, :] - b[:, :, :, None, :, :], -jnp.inf)
    scores = jnp.einsum('bhnik,bhnjk,bhnijk->bhnij', q, k, jnp.exp(rel))
    o_intra = jnp.einsum('bhnij,bhnjv->bhniv', scores, v)
    q_in = q * jnp.exp(b)
    k_out = k * jnp.exp(b[:, :, :, -1:] - b)
    decay = jnp.exp(b[:, :, :, -1])

    def step(s, xs):
        qi, ki, vi, di = xs
        o = jnp.einsum('bhck,bhkv->bhcv', qi, s)
        s = s * di[..., None] + jnp.einsum('bhck,bhcv->bhkv', ki, vi)
        return s, o

    xs = tuple(jnp.moveaxis(a, 2, 0) for a in (q_in, k_out, v, decay))
    s_fin, o_inter = lax.scan(step, s0, xs)
    o = o_intra + jnp.moveaxis(o_inter, 0, 2)
    return o.reshape(B, H, T, -1), s_fin


def hgrn2_mixer(u, lb, g_norm, s0):
    q, i, g, z_f, z_b = jnp.split(u, 5, axis=-1)
    q = heads(q, HG_HEADS) * HG_DK ** -0.5
    v = heads(i, HG_HEADS)
    out = 0.0
    finals = []
    for d, z in enumerate((z_f, z_b)):
        log_f = heads(jnp.logaddexp(jnp.log(lb[d]), jnp.log1p(-lb[d]) + jax.nn.log_sigmoid(z)), HG_HEADS)
        k = -jnp.expm1(log_f)
        if d == 0:
            o, s = hgrn2_chunk_scan(q, k, v, log_f, s0[:, d])
        else:
            o, s = hgrn2_chunk_scan(jnp.flip(q, 2), jnp.flip(k, 2), jnp.flip(v, 2), jnp.flip(log_f, 2), s0[:, d])
            o = jnp.flip(o, 2)
        out = out + o
        finals.append(s)
    return head_norm_gate(out, g_norm.reshape(HG_HEADS, HG_DV), g), jnp.stack(finals, axis=1)


def hyena_filter(L, w1, b1, freq, w2, b2, w3):
    w1, b1, freq, w2, b2, w3 = (a.astype(F32) for a in (w1, b1, freq, w2, b2, w3))
    pos = jnp.arange(L, dtype=F32)
    t = pos / (L - 1)
    bands = jnp.linspace(1e-4, HY_BANDS - 1, HY_BANDS, dtype=F32)
    ang = (2.0 * math.pi / L) * pos[:, None] * bands[None, :]
    z = jnp.concatenate([t[:, None], jnp.cos(ang), -jnp.sin(ang)], axis=-1)
    h = jnp.sin(freq * (z @ w1 + b1))
    h = jnp.sin(freq * (h @ w2 + b2))
    h = h @ w3
    max_decay = math.log(HY_TARGET) / HY_FAST
    min_decay = math.log(HY_TARGET) / HY_SLOW
    deltas = jnp.linspace(min_decay, max_decay, HY_W, dtype=F32)
    window = jnp.exp(-t[:, None] * jnp.abs(deltas)[None, :])
    h_f, h_b = h[:, :HY_W] * window, h[:, HY_W:] * window
    return jnp.concatenate([h_f, jnp.zeros((1, HY_W), F32), jnp.flip(h_b[1:], axis=0)], axis=0)


def hyena_mixer(u, conv_w, conv_b, w1, b1, freq, w2, b2, w3, skip):
    B, T, _ = u.shape
    uc = conv3_centred(u, conv_w) + conv_b
    x0, x1, v = jnp.split(uc, 3, axis=-1)
    z = x1 * v
    filt = hyena_filter(T, w1, b1, freq, w2, b2, w3)
    y = jnp.fft.irfft(jnp.fft.rfft(z, n=2 * T, axis=1) * jnp.fft.rfft(filt, n=2 * T, axis=0)[None], n=2 * T, axis=1)[:, :T]
    return x0 * (y + z * skip)


def mla_keys_values(ckv, krope, w_kv_up):
    B, S, _ = ckv.shape
    kv = heads(ckv @ w_kv_up, MLA_HEADS)
    k_nope, v = kv[..., :MLA_NOPE], kv[..., MLA_NOPE:]
    k = jnp.concatenate([k_nope, jnp.broadcast_to(krope[:, None], (B, MLA_HEADS, S, MLA_ROPE))], axis=-1)
    return k, v


def mla_mixer(u, q_norm_a, w_q_up, kv_norm_a, w_kv_up, qk_norm, rope, ctx):
    cq, ckv, krope = jnp.split(u, [MLA_Q_LORA, MLA_Q_LORA + MLA_KV_LORA], axis=-1)
    q = heads(rms_norm(cq, q_norm_a) @ w_q_up, MLA_HEADS)
    ckv = rms_norm(ckv, kv_norm_a)
    k, v = mla_keys_values(ckv, krope, w_kv_up)
    q = rms_norm(q, qk_norm[0])
    k = rms_norm(k, qk_norm[1])
    if ctx is not None:
        q = apply_rope(q, rope[0], rope[1])
        k = apply_rope(k, rope[0], rope[1])
        kc, vc = mla_keys_values(ctx[0], ctx[1], w_kv_up)
        k = jnp.concatenate([k, rms_norm(kc, qk_norm[1])], axis=2)
        v = jnp.concatenate([v, vc], axis=2)
    o = attention(q, k, v, (MLA_NOPE + MLA_ROPE) ** -0.5)
    return merge_heads(o), ckv, krope


def gdn_chunk_scan(q, k, v, beta, log_a, s0):
    B, H, T, dk = q.shape
    n = T // CHUNK
    q, k, v = (a.reshape(B, H, n, CHUNK, a.shape[-1]) for a in (q, k, v))
    beta, log_a = (a.reshape(B, H, n, CHUNK) for a in (beta, log_a))
    G = jnp.cumsum(log_a, axis=-1)
    incl = jnp.tril(jnp.ones((CHUNK, CHUNK), bool))
    strict = jnp.tril(jnp.ones((CHUNK, CHUNK), F32), -1)
    decay = jnp.exp(jnp.where(incl, G[..., :, None] - G[..., None, :], -jnp.inf))
    kb = k * beta[..., None]
    lmat = jnp.einsum('bhnik,bhnjk->bhnij', kb, k) * decay * strict
    eye = jnp.eye(CHUNK, dtype=F32)
    tmat = lax.linalg.triangular_solve(eye + lmat, jnp.broadcast_to(eye, lmat.shape), left_side=True, lower=True, unit_diagonal=True)
    u_w = tmat @ (v * beta[..., None])
    w_w = tmat @ (kb * jnp.exp(G)[..., None])
    attn = jnp.einsum('bhnik,bhnjk->bhnij', q, k) * decay
    q_in = q * jnp.exp(G)[..., None]
    k_out = k * jnp.exp(G[..., -1:] - G)[..., None]
    a_last = jnp.exp(G[..., -1])

    def step(s, xs):
        qi, ki, wi, ui, ai, di = xs
        v_new = ui - wi @ s
        o = qi @ s + ai @ v_new
        s = s * di[..., None, None] + jnp.swapaxes(ki, -1, -2) @ v_new
        return s, o

    xs = tuple(jnp.moveaxis(a, 2, 0) for a in (q_in, k_out, w_w, u_w, attn, a_last))
    s_fin, o = lax.scan(step, s0, xs)
    return jnp.moveaxis(o, 0, 2).reshape(B, H, T, -1), s_fin


def gdn_mixer(u, conv_w, a_log, dt_bias, g_norm, s0):
    B, T, _ = u.shape
    qkv, g, ab = jnp.split(u, [3 * GROUP_W, 4 * GROUP_W], axis=-1)
    qkv = jax.nn.silu(conv3_centred(qkv, conv_w))
    q, k, v = jnp.split(qkv, 3, axis=-1)
    q = l2_norm(heads(q, GD_HEADS)) * GD_DK ** -0.5
    k = l2_norm(heads(k, GD_HEADS))
    v = heads(v, GD_HEADS)
    a = ab[..., :2 * GD_HEADS].reshape(B, T, 2, GD_HEADS)
    bb = ab[..., 2 * GD_HEADS:].reshape(B, T, 2, GD_HEADS)
    a_log, dt_bias = a_log.astype(F32), dt_bias.astype(F32)
    out = 0.0
    finals = []
    for d in range(2):
        log_a = (-jnp.exp(a_log[d]) * jax.nn.softplus(a[:, :, d] + dt_bias[d])).transpose(0, 2, 1)
        beta = jax.nn.sigmoid(bb[:, :, d]).transpose(0, 2, 1)
        if d == 0:
            o, s = gdn_chunk_scan(q, k, v, beta, log_a, s0[:, d])
        else:
            o, s = gdn_chunk_scan(jnp.flip(q, 2), jnp.flip(k, 2), jnp.flip(v, 2), jnp.flip(beta, 2), jnp.flip(log_a, 2), s0[:, d])
            o = jnp.flip(o, 2)
        out = out + o
        finals.append(s)
    return head_norm_gate(out, g_norm, g), jnp.stack(finals, axis=1)


def trunk_layer(x, ada, lb, rope, ctx, l, P):
    m = [ada[:, i, None, :] for i in range(N_ADA)]
    h = rms_norm(x, P['norm_ffn'][l, 0]) * (1.0 + m[1]) + m[0]
    x = x + 0.5 * m[2] * swiglu(h, P['w_ffn_gu'][l, 0], P['w_ffn_down'][l, 0])
    h = rms_norm(x, P['norm_mix'][l]) * (1.0 + m[4]) + m[3]
    u = (h @ P['w_in'][l]).astype(F32)
    u_hg, u_hy, u_mla, u_gd = jnp.split(u, [HG_COLS, HG_COLS + HY_COLS, HG_COLS + HY_COLS + MLA_COLS], axis=-1)
    B = x.shape[0]
    if ctx is None:
        mla_ctx = None
        s_hg0 = jnp.zeros((B, 2, HG_HEADS, HG_DK, HG_DV), F32)
        s_gd0 = jnp.zeros((B, 2, GD_HEADS, GD_DK, GD_DV), F32)
    else:
        mla_ctx = (ctx[0].astype(F32), ctx[1].astype(F32))
        s_hg0, s_gd0 = ctx[2].astype(F32), ctx[3].astype(F32)
    o_hg, s_hg = hgrn2_mixer(u_hg, lb, P['hgrn_norm'][l], s_hg0)
    o_hy = hyena_mixer(u_hy, P['hy_conv_w'][l], P['hy_conv_b'][l], P['hy_w1'][l], P['hy_b1'][l], P['hy_freq'][l],
                       P['hy_w2'][l], P['hy_b2'][l], P['hy_w3'][l], P['hy_skip'][l])
    o_mla, ckv, krope = mla_mixer(u_mla, P['mla_q_norm_a'][l], P['mla_w_q_up'][l], P['mla_kv_norm_a'][l],
                                  P['mla_w_kv_up'][l], P['mla_qk_norm'][l], rope, mla_ctx)
    o_gd, s_gd = gdn_mixer(u_gd, P['gdn_conv_w'][l], P['gdn_a_log'][l], P['gdn_dt_bias'][l], P['gdn_norm'][l], s_gd0)
    o = jnp.concatenate([o_hg, o_hy, o_mla, o_gd], axis=-1).astype(x.dtype) @ P['w_out'][l]
    x = x + m[5] * o
    h = rms_norm(x, P['norm_ffn'][l, 1]) * (1.0 + m[7]) + m[6]
    x = x + 0.5 * m[8] * swiglu(h, P['w_ffn_gu'][l, 1], P['w_ffn_down'][l, 1])
    return x, (ckv, krope, s_hg, s_gd)


def setup_inputs(seed: int = 0) -> dict:
    key = jax.random.key(seed)
    ks = iter(jax.random.split(key, 48))

    def nrm(shape, s):
        return s * jax.random.normal(next(ks), shape, F32)

    def gain(shape):
        return 1.0 + nrm(shape, 0.02)

    dt = jnp.exp(jax.random.uniform(next(ks), (DEPTH, 2, GD_HEADS), F32, math.log(1e-3), math.log(1e-1)))
    a_init = jax.random.uniform(next(ks), (DEPTH, 2, GD_HEADS), F32, 1.0, 16.0)
    return {
        'x_prompt': nrm((BATCH, SEQ, D_MODEL), 1.0),
        'x_sample': nrm((DEC_BATCH, DEC_SEQ, D_MODEL), 1.0),
        'cache_mla_ckv': nrm((DEC_BATCH, DEPTH, PAST_LEN, MLA_KV_LORA), 1.0),
        'cache_mla_krope': nrm((DEC_BATCH, DEPTH, PAST_LEN, MLA_ROPE), 1.0),
        'state_hgrn': nrm((DEC_BATCH, DEPTH, 2, HG_HEADS, HG_DK, HG_DV), 0.3),
        'state_gdn': nrm((DEC_BATCH, DEPTH, 2, GD_HEADS, GD_DK, GD_DV), 0.1),
        'c': nrm((DEC_BATCH, D_MODEL), 1.0),
        'c_ctx': nrm((D_MODEL,), 1.0),
        'w_ada': nrm((DEPTH, D_MODEL, N_ADA * D_MODEL), 0.5 * D_MODEL ** -0.5),
        'b_ada': nrm((DEPTH, N_ADA * D_MODEL), 0.02),
        'norm_ffn': gain((DEPTH, 2, D_MODEL)),
        'w_ffn_gu': nrm((DEPTH, 2, D_MODEL, 2 * D_FF), D_MODEL ** -0.5),
        'w_ffn_down': nrm((DEPTH, 2, D_FF, D_MODEL), D_FF ** -0.5),
        'norm_mix': gain((DEPTH, D_MODEL)),
        'w_in': nrm((DEPTH, D_MODEL, IN_COLS), D_MODEL ** -0.5),
        'w_out': nrm((DEPTH, MIX_W, D_MODEL), MIX_W ** -0.5),
        'hgrn_lb': nrm((DEPTH, 2, GROUP_W), 0.1),
        'hgrn_norm': gain((DEPTH, GROUP_W)),
        'hy_conv_w': nrm((DEPTH, 3, HY_COLS), 0.5),
        'hy_conv_b': nrm((DEPTH, HY_COLS), 0.02),
        'hy_w1': nrm((DEPTH, HY_EMB, HY_FH), HY_EMB ** -0.5),
        'hy_b1': nrm((DEPTH, HY_FH), 0.1),
        'hy_freq': gain((DEPTH, HY_FH)),
        'hy_w2': nrm((DEPTH, HY_FH, HY_FH), HY_FH ** -0.5),
        'hy_b2': nrm((DEPTH, HY_FH), 0.1),
        'hy_w3': nrm((DEPTH, HY_FH, 2 * HY_W), 0.01),
        'hy_skip': nrm((DEPTH, HY_W), 0.5),
        'mla_q_norm_a': gain((DEPTH, MLA_Q_LORA)),
        'mla_w_q_up': nrm((DEPTH, MLA_Q_LORA, MLA_HEADS * (MLA_NOPE + MLA_ROPE)), MLA_Q_LORA ** -0.5),
        'mla_kv_norm_a': gain((DEPTH, MLA_KV_LORA)),
        'mla_w_kv_up': nrm((DEPTH, MLA_KV_LORA, MLA_HEADS * (MLA_NOPE + MLA_V)), MLA_KV_LORA ** -0.5),
        'mla_qk_norm': gain((DEPTH, 2, MLA_NOPE + MLA_ROPE)),
        'gdn_conv_w': nrm((DEPTH, 3, 3 * GROUP_W), 0.5),
        'gdn_a_log': jnp.log(a_init),
        'gdn_dt_bias': dt + jnp.log(-jnp.expm1(-dt)),
        'gdn_norm': gain((DEPTH, GD_DV)),
    }


def reference(x_prompt, x_sample, cache_mla_ckv, cache_mla_krope, state_hgrn, state_gdn, c, c_ctx,
              w_ada, b_ada, norm_ffn, w_ffn_gu, w_ffn_down, norm_mix, w_in, w_out, hgrn_lb, hgrn_norm,
              hy_conv_w, hy_conv_b, hy_w1, hy_b1, hy_freq, hy_w2, hy_b2, hy_w3, hy_skip,
              mla_q_norm_a, mla_w_q_up, mla_kv_norm_a, mla_w_kv_up, mla_qk_norm,
              gdn_conv_w, gdn_a_log, gdn_dt_bias, gdn_norm):
    P = dict(norm_ffn=norm_ffn, w_ffn_gu=w_ffn_gu, w_ffn_down=w_ffn_down, norm_mix=norm_mix, w_in=w_in,
             w_out=w_out, hgrn_norm=hgrn_norm, hy_conv_w=hy_conv_w, hy_conv_b=hy_conv_b, hy_w1=hy_w1,
             hy_b1=hy_b1, hy_freq=hy_freq, hy_w2=hy_w2, hy_b2=hy_b2, hy_w3=hy_w3, hy_skip=hy_skip,
             mla_q_norm_a=mla_q_norm_a, mla_w_q_up=mla_w_q_up, mla_kv_norm_a=mla_kv_norm_a,
             mla_w_kv_up=mla_w_kv_up, mla_qk_norm=mla_qk_norm, gdn_conv_w=gdn_conv_w, gdn_a_log=gdn_a_log,
             gdn_dt_bias=gdn_dt_bias, gdn_norm=gdn_norm)
    lb_all = jnp.cumsum(jax.nn.softmax(hgrn_lb.astype(F32), axis=0), axis=0)
    lb_all = lb_all - lb_all[0]
    rows = x_sample.shape[1] // GRID_W
    rope = rope_tables(rows)
    ctx_cond = c_ctx[None]
    y_prompt, y_sample = x_prompt, x_sample
    ctx_states = []
    for l in range(DEPTH):
        y_prompt, st = trunk_layer(y_prompt, adaln(ctx_cond, w_ada[l], b_ada[l]), lb_all[l], None, None, l, P)
        ctx_states.append(st)
        cached = (cache_mla_ckv[:, l], cache_mla_krope[:, l], state_hgrn[:, l], state_gdn[:, l])
        y_sample, _ = trunk_layer(y_sample, adaln(c, w_ada[l], b_ada[l]), lb_all[l], rope, cached, l, P)
    new_ckv = jnp.stack([s[0] for s in ctx_states], axis=1)
    new_krope = jnp.stack([s[1] for s in ctx_states], axis=1)
    new_hg = jnp.stack([s[2] for s in ctx_states], axis=1)
    new_gd = jnp.stack([s[3] for s in ctx_states], axis=1)
    return (y_prompt, y_sample, new_ckv, new_krope, new_hg, new_gd)
```

```python
import contextlib
import math
import numpy as np
import concourse.bass as bass
import concourse.mybir as mybir
from concourse.bass_utils import run_bass_kernel_spmd

F32 = mybir.dt.float32
BF16 = mybir.dt.bfloat16
AF = mybir.ActivationFunctionType
ALU = mybir.AluOpType

D = 1024
NT = 1024
DEPTH = 4
DFF = 2816
NF = 22
EPS = 1e-6
NCORES = 8
TILE_COLS = 4096
NWSLOT = 3
TILES_PER_LAYER = 48
ADA_Q = 1152
MAGIC = 12582912.0


class Prog:
    ENG = ('pe', 'act', 'dve', 'pool', 'sp')

    def __init__(self, nc):
        self.nc = nc
        self.ops = {e: [] for e in self.ENG}
        self.count = {e: 0 for e in self.ENG}
        self.waited = {e: {} for e in self.ENG}
        self.res = {}
        self.dsem_count = {}
        self.sem_names = []

    @staticmethod
    def _norm(rs):
        return [(r,) if isinstance(r, str) else tuple(r) for r in rs]

    def _conf(self, r):
        d = self.res.setdefault(r[0], {})
        out = []
        for k, v in d.items():
            n = min(len(k), len(r))
            if k[:n] == r[:n]:
                out.append(v)
        return d, out

    def _deps(self, reads, writes):
        toks = []
        for r in reads:
            _, cs = self._conf(r)
            for v in cs:
                if v[0] is not None:
                    toks.append(v[0])
        for w in writes:
            _, cs = self._conf(w)
            for v in cs:
                if v[0] is not None:
                    toks.append(v[0])
                toks.extend(v[1])
        return toks

    def _commit(self, reads, writes, tok):
        for r in reads:
            d, _ = self._conf(r)
            d.setdefault(r, [None, []])[1].append(tok)
        for w in writes:
            d, _ = self._conf(w)
            for k in list(d.keys()):
                if len(k) >= len(w) and k[:len(w)] == w:
                    del d[k]
            d[w] = [tok, []]

    def _waits(self, eng, toks, sync_self=True):
        best = {}
        for (sk, val) in toks:
            if sk == eng and not sync_self:
                continue
            if val > best.get(sk, 0):
                best[sk] = val
        out = []
        for sk, val in best.items():
            if self.waited[eng].get(sk, 0) >= val:
                continue
            self.waited[eng][sk] = val
            out.append((sk, val))
        return out

    def op(self, eng, fn, reads=(), writes=(), sync_self=True):
        reads, writes = self._norm(reads), self._norm(writes)
        waits = self._waits(eng, self._deps(reads, writes), sync_self)
        self.count[eng] += 1
        self.ops[eng].append((waits, fn, (eng, 1)))
        self._commit(reads, writes, (eng, self.count[eng]))
        if eng not in self.sem_names:
            self.sem_names.append(eng)

    def dma(self, eng, fn, slot, reads=(), writes=()):
        reads, writes = self._norm(reads), self._norm(writes)
        toks = self._deps(reads, writes)
        sk = 'd_' + slot
        prev = self.dsem_count.get(sk, 0)
        if prev:
            toks.append((sk, prev))
        waits = self._waits(eng, toks, True)
        self.dsem_count[sk] = prev + 16
        self.ops[eng].append((waits, fn, (sk, 16)))
        self._commit(reads, writes, (sk, prev + 16))
        if sk not in self.sem_names:
            self.sem_names.append(sk)

    def barrier(self):
        toks = [(e, c) for e, c in self.count.items() if c]
        toks += [(k, v) for k, v in self.dsem_count.items() if not k.startswith('d_ws')]
        for eng in self.ENG:
            waits = self._waits(eng, toks, True)
            if waits:
                self.ops[eng].append((waits, None, None))
        self.res = {k: v for k, v in self.res.items() if k == 'ws'}

    def emit(self, stack):
        nc = self.nc
        sems = {n: stack.enter_context(nc.semaphore('s_' + n)) for n in self.sem_names}
        block = stack.enter_context(nc.Block())
        engobj = {'pe': 'tensor', 'act': 'scalar', 'dve': 'vector', 'pool': 'gpsimd', 'sp': 'sync'}

        def mk(ename):
            def body(e):
                for waits, fn, inc in self.ops[ename]:
                    for sk, val in waits:
                        e.wait_ge(sems[sk], val)
                    if fn is not None:
                        fn(e).then_inc(sems[inc[0]], inc[1])
            return body

        for ename in self.ENG:
            if self.ops[ename]:
                getattr(block, engobj[ename])(mk(ename))


class Packer:
    def __init__(self):
        self.items = {}
        self.ncols = 0

    def add(self, name, parts, cols):
        self.items[name] = (self.ncols, parts, cols)
        self.ncols += cols

    def pack(self, vals):
        out = np.zeros((128, self.ncols), np.float32)
        for name, (o, p, c) in self.items.items():
            a = np.asarray(vals[name], np.float32).reshape(p, c)
            out[:p, o:o + c] = a
        return out


def fm(v, parts=128):
    v = np.asarray(v, np.float32)
    return v.reshape(-1, parts).T


LW = Packer()
for _n, _p, _c in [
    ('bada', 128, 72), ('g_ffn0', 128, 8), ('g_mix', 128, 8), ('g_ffn1', 128, 8),
    ('hy_cw', 128, 18), ('hy_cb', 128, 6), ('hy_w1', 33, 64), ('hy_b1', 64, 1), ('hy_fr', 64, 1),
    ('hy_w2', 64, 64), ('hy_b2', 64, 1), ('hy_w3', 64, 512), ('hy_skip', 128, 2),
    ('m_qa', 128, 2), ('m_wq', 128, 768), ('m_kva', 128, 1), ('m_wk', 128, 384), ('m_wv', 128, 256),
    ('m_gq', 96, 1), ('m_gk', 96, 1),
    ('hg_lbraw', 64, 32), ('hg_gn', 64, 4),
    ('gd_cw', 64, 36), ('gd_alog', 16, 1), ('gd_dtb', 16, 1), ('gd_gn', 64, 1),
]:
    LW.add(_n, _p, _c)

CW = Packer()
for _n, _p, _c in [
    ('cond', 128, 8), ('carry', 128, 1), ('ncarry', 128, 1),
    ('ident', 128, 128), ('ones', 128, 128), ('rm', 96, 96), ('e32', 32, 96),
    ('m32', 32, 64), ('m64', 64, 256), ('mf', 16, 1), ('mb', 16, 1),
]:
    CW.add(_n, _p, _c)
CB = Packer()
for _n, _p, _c in [
    ('cosq', 96, 1024), ('sinq', 96, 1024), ('ek', 8, 1280), ('eq', 8, 1024),
    ('zemb', 33, 1024), ('winf', 128, 2048), ('winb', 128, 2048), ('sel', 32, 16 * 64),
]:
    CB.add(_n, _p, _c)


def w_in_colmap():
    HG, HY, MLA = 1280, 768, 416
    hg0, hy0, ml0, gd0 = 0, HG, HG + HY, HG + HY + MLA
    tiles = []
    def rng(a, n):
        return list(range(a, a + n))
    tiles.append(rng(hg0, 512))
    tiles.append(rng(hg0 + 512, 512))
    tiles.append(rng(hg0 + 1024, 256))
    tiles.append(rng(hy0, 512))
    tiles.append(rng(hy0 + 512, 256))
    tiles.append(rng(gd0, 512))
    tiles.append(rng(gd0 + 512, 512))
    tiles.append(rng(gd0 + 1024, 16) + rng(ml0, 416))
    return tiles


def pack_wstream(inp):
    out = np.zeros((DEPTH * TILES_PER_LAYER, 128, TILE_COLS), np.float32)
    cm = w_in_colmap()
    t = 0
    for l in range(DEPTH):
        def ffn(i):
            nonlocal t
            wgu = inp['w_ffn_gu'][l, i].reshape(8, 128, 2 * DFF)
            for g in range(11):
                v = out[t].reshape(128, 8, 512)
                v[:, :, 0:256] = wgu[:, :, g * 256:(g + 1) * 256].transpose(1, 0, 2)
                v[:, :, 256:512] = wgu[:, :, DFF + g * 256:DFF + (g + 1) * 256].transpose(1, 0, 2)
                t += 1
            wd = inp['w_ffn_down'][l, i].reshape(NF, 128, 8, 128)
            for dc in range(8):
                out[t][:, :NF * 128] = wd[:, :, dc, :].transpose(1, 0, 2).reshape(128, NF * 128)
                t += 1
        ffn(0)
        win = inp['w_in'][l].reshape(8, 128, -1)
        for cols in cm:
            v = out[t].reshape(128, 8, 512)
            v[:, :, :len(cols)] = win[:, :, cols].transpose(1, 0, 2)
            t += 1
        wo = inp['w_out'][l].reshape(8, 128, 1024)
        for h in range(2):
            out[t].reshape(128, 8, 512)[:] = wo[:, :, h * 512:(h + 1) * 512].transpose(1, 0, 2)
            t += 1
        ffn(1)
    assert t == DEPTH * TILES_PER_LAYER
    return out


def pack_layer_small(inp, l):
    v = {}
    v['bada'] = fm(inp['b_ada'][l])
    v['g_ffn0'] = fm(inp['norm_ffn'][l, 0])
    v['g_mix'] = fm(inp['norm_mix'][l])
    v['g_ffn1'] = fm(inp['norm_ffn'][l, 1])
    cw = inp['hy_conv_w'][l]
    v['hy_cw'] = np.stack([fm(cw[k]) for k in range(3)], axis=2).reshape(128, 18)
    v['hy_cb'] = fm(inp['hy_conv_b'][l])
    v['hy_w1'] = inp['hy_w1'][l]
    v['hy_b1'] = inp['hy_b1'][l].reshape(64, 1)
    v['hy_fr'] = inp['hy_freq'][l].reshape(64, 1)
    v['hy_w2'] = inp['hy_w2'][l]
    v['hy_b2'] = inp['hy_b2'][l].reshape(64, 1)
    v['hy_w3'] = inp['hy_w3'][l]
    v['hy_skip'] = fm(inp['hy_skip'][l])
    v['m_qa'] = fm(inp['mla_q_norm_a'][l])
    v['m_wq'] = inp['mla_w_q_up'][l].reshape(2, 128, 384).transpose(1, 0, 2).reshape(128, 768)
    v['m_kva'] = inp['mla_kv_norm_a'][l].reshape(128, 1)
    wkv = inp['mla_w_kv_up'][l].reshape(128, 4, 128)
    wk = np.zeros((128, 4, 96), np.float32)
    wk[:, :, :64] = wkv[:, :, :64]
    v['m_wk'] = wk.reshape(128, 384)
    v['m_wv'] = wkv[:, :, 64:].reshape(128, 256)
    v['m_gq'] = inp['mla_qk_norm'][l, 0].reshape(96, 1)
    v['m_gk'] = inp['mla_qk_norm'][l, 1].reshape(96, 1)
    lb = inp['hgrn_lb'].reshape(4, 2, 4, 64)
    v['hg_lbraw'] = lb.transpose(3, 0, 1, 2).reshape(64, 32)
    v['hg_gn'] = inp['hgrn_norm'][l].reshape(4, 64).T
    gw = inp['gdn_conv_w'][l].reshape(3, 12, 64)
    v['gd_cw'] = gw.transpose(2, 1, 0).reshape(64, 36)
    al = np.zeros((16, 1), np.float32); al[:8, 0] = inp['gdn_a_log'][l].reshape(8)
    db = np.zeros((16, 1), np.float32); db[:8, 0] = inp['gdn_dt_bias'][l].reshape(8)
    v['gd_alog'] = al
    v['gd_dtb'] = db
    v['gd_gn'] = inp['gdn_norm'][l].reshape(64, 1)
    return LW.pack(v)


def rope_np(rows, gw=64):
    T = rows * gw
    row = np.repeat(np.arange(rows, dtype=np.float32), gw)
    col = (np.arange(T) % gw).astype(np.float32)
    pairs = 8
    inv = (10000.0 ** (-np.arange(pairs, dtype=np.float32) / pairs)).astype(np.float32)
    ang = np.concatenate([row[:, None] * inv, col[:, None] * inv], axis=-1)
    return np.cos(ang), np.sin(ang)


def core_consts(is_sample, cond):
    v = {}
    v['cond'] = fm(cond)
    v['carry'] = np.full((128, 1), 1.0 if is_sample else 0.0, np.float32)
    v['ncarry'] = np.full((128, 1), 0.0 if is_sample else -1.0, np.float32)
    v['ident'] = np.eye(128, dtype=np.float32)
    v['ones'] = np.ones((128, 128), np.float32)
    cosq = np.ones((96, 1024), np.float32); sinq = np.zeros((96, 1024), np.float32)
    if is_sample:
        c, s = rope_np(16)
        cosq[64:80] = c.T; cosq[80:96] = c.T; sinq[64:80] = s.T; sinq[80:96] = s.T
    v['cosq'], v['sinq'] = cosq, sinq
    ek = np.zeros((8, 1280), np.float32); eq = np.zeros((8, 1024), np.float32)
    for g in range(4):
        ek[g, g * 256:(g + 1) * 256] = 1.0
    ek[4, 1024:] = 1.0
    if not is_sample:
        eq[:5, :] = -30000.0
        for g in range(4):
            eq[g, g * 256:(g + 1) * 256] = 0.0
    v['ek'], v['eq'] = ek, eq
    rm = np.zeros((96, 96), np.float32)
    for i in range(16):
        rm[80 + i, 64 + i] = -1.0
        rm[64 + i, 80 + i] = 1.0
    v['rm'] = rm
    e32 = np.zeros((32, 96), np.float32)
    for i in range(32):
        e32[i, 64 + i] = 1.0
    v['e32'] = e32
    L = 1024 if is_sample else 256
    pos = np.arange(L, dtype=np.float32)
    tt = pos / np.float32(L - 1)
    bands = np.linspace(1e-4, 15, 16, dtype=np.float32)
    ang = (np.float32(2.0 * math.pi / L)) * pos[:, None] * bands[None, :]
    z = np.concatenate([tt[:, None], np.cos(ang), -np.sin(ang)], axis=-1).astype(np.float32)
    z = np.tile(z, (1024 // L, 1))
    v['zemb'] = z.T
    deltas = np.linspace(math.log(1e-2) / 1.5, math.log(1e-2) / 0.3, 256, dtype=np.float32)
    win = np.exp(-tt[:, None] * np.abs(deltas)[None, :]).astype(np.float32)
    winb = win.copy(); winb[0] = 0.0
    win = np.tile(win, (1024 // L, 1)); winb = np.tile(winb, (1024 // L, 1))
    v['winf'] = win.reshape(8, 128, 256).transpose(1, 0, 2).reshape(128, 2048)
    v['winb'] = winb.reshape(8, 128, 256).transpose(1, 0, 2).reshape(128, 2048)
    j32 = np.arange(32)[:, None]; i32 = np.arange(32)[None, :]
    v['m32'] = np.concatenate([(j32 <= i32), (j32 >= i32)], axis=1).astype(np.float32)
    j64 = np.arange(64)[:, None]; i64 = np.arange(64)[None, :]
    v['m64'] = np.concatenate([(j64 <= i64), (j64 < i64), (j64 >= i64), (j64 > i64)], axis=1).astype(np.float32)
    mf = np.zeros((16, 1), np.float32); mf[0:4] = 1.0
    mb = np.zeros((16, 1), np.float32); mb[4:8] = 1.0
    v['mf'], v['mb'] = mf, mb
    sel = np.zeros((32, 16, 64), np.float32)
    for r in range(16):
        sel[r, r, :] = 1.0
    v['sel'] = sel.reshape(32, 1024)
    return CW.pack({k: v[k] for k in CW.items}), CB.pack({k: v[k] for k in CB.items})


def dft_tiles(is_sample):
    L = 1024 if is_sample else 256
    t = np.arange(L, dtype=np.float64)[:, None]
    f = np.arange(L, dtype=np.float64)[None, :]
    ang = math.pi * (2 * f + 1) * t / (2 * L)
    C = np.cos(ang); S = np.sin(ang)
    nb = 1024 // L
    def bd(M):
        out = np.zeros((1024, 1024), np.float32)
        for b in range(nb):
            out[b * L:(b + 1) * L, b * L:(b + 1) * L] = M
        return out
    mats = [bd(C), bd(S), bd(C.T / L), bd(S.T / L)]
    tiles = np.zeros((8, 128, TILE_COLS), np.float32)
    k = 0
    for M in mats:
        Mr = M.reshape(8, 128, 1024)
        for h in range(2):
            tiles[k].reshape(128, 8, 512)[:] = Mr[:, :, h * 512:(h + 1) * 512].transpose(1, 0, 2)
            k += 1
    return tiles


class B:
    def __init__(self, nc, st, depth=DEPTH, stages=None, dbg=None):
        self.nc, self.st = nc, st
        self.depth = depth
        self.stages = stages or ('ffn0', 'hg', 'hy', 'gd', 'mla', 'ffn1')
        self.dbg = dbg or {}
        self.P = Prog(nc)
        self.bankctr = 0
        self.wtile = 0
        self.uid = 0

    def sb(self, name, shape, dt=F32):
        return self.st.enter_context(self.nc.sbuf_tensor(name, shape, dt))

    nbank = 8

    def bank(self):
        b = self.bankctr % self.nbank
        self.bankctr += 1
        return b

    def mm(self, out, lhsT, rhs, start, stop, r, w, sync_self=False):
        self.P.op('pe', lambda e: e.matmul(out, lhsT=lhsT, rhs=rhs, start=start, stop=stop), r, w, sync_self=sync_self)

    def tr(self, out, in_, ident, r, w):
        self.P.op('pe', lambda e: e.transpose(out, in_, ident), r, w, sync_self=False)

    def act(self, out, in_, func, r, w, bias=0.0, scale=1.0):
        self.P.op('act', lambda e: e.activation(out=out, in_=in_, func=func, bias=bias, scale=scale), r, w)

    def tt(self, eng, out, in0, in1, op, r, w):
        self.P.op(eng, lambda e: e.tensor_tensor(out=out, in0=in0, in1=in1, op=op), r, w)

    def ts(self, eng, out, in0, s1, s2, op0, op1, r, w):
        if s2 is None:
            self.P.op(eng, lambda e: e.tensor_scalar(out=out, in0=in0, scalar1=s1, scalar2=None, op0=op0), r, w)
        else:
            self.P.op(eng, lambda e: e.tensor_scalar(out=out, in0=in0, scalar1=s1, scalar2=s2, op0=op0, op1=op1), r, w)

    def stt(self, eng, out, in0, scalar, in1, op0, op1, r, w):
        self.P.op(eng, lambda e: e.scalar_tensor_tensor(out=out, in0=in0, scalar=scalar, in1=in1, op0=op0, op1=op1), r, w)

    def cp(self, eng, out, in_, r, w):
        if eng == 'act':
            self.P.op('act', lambda e: e.copy(out=out, in_=in_), r, w)
        else:
            self.P.op(eng, lambda e: e.tensor_copy(out=out, in_=in_), r, w)

    def recip(self, out, in_, r, w):
        self.P.op('dve', lambda e: e.reciprocal(out=out, in_=in_), r, w)

    def memset(self, eng, ap, val, w):
        self.P.op(eng, lambda e: e.memset(ap, val), [], w)

    def rsqrt(self, out, in_, scale, eps_ap, r, w):
        self.act(out, in_, AF.Ln, r + ['misc'], w, bias=eps_ap, scale=scale)
        self.act(out, out, AF.Exp, w, w, scale=-0.5)

    def arena_reset(self, keep16=0, keep32=0):
        self.P.barrier()
        self.a32 = keep32
        self.a16 = keep16
        self.uid += 1

    def f32(self, name, parts, cols):
        o = self.a32
        self.a32 += cols
        assert self.a32 <= self.A32, (name, self.a32)
        return self.ar32[0:parts, o:o + cols], f'a{self.uid}.{name}'

    def b16in32(self, name, parts, cols):
        o = self.a32
        self.a32 += cols // 2
        assert self.a32 <= self.A32, (name, self.a32)
        return self.ar32[0:parts, o:o + cols // 2].bitcast(BF16), f'a{self.uid}.{name}'

    def b16(self, name, parts, cols):
        o = self.a16
        self.a16 += cols
        assert self.a16 <= self.A16, (name, self.a16)
        return self.ar16[0:parts, o:o + cols], f'a{self.uid}.{name}'

    def prefetch(self, k):
        if not self.do_prefetch:
            return
        for _ in range(k):
            self.pref.append(self.next_wtile(_force=True))

    def next_wtile(self, src=None, f32=False, plain=False, _force=False):
        if src is None and self.pref and not _force:
            return self.pref.pop(0)
        if src is None:
            t = self.wtile
            self.wtile += 1
            src_ap = self.wstream[t]
        else:
            src_ap = src
        s = self.wslot_ctr % NWSLOT
        self.wslot_ctr += 1
        dst = self.wsl[:, s, :]
        rn = ('ws', s)
        if f32:
            dst = dst.bitcast(F32)
            self.P.dma('sp', lambda e: e.dma_start(out=dst, in_=src_ap), f'ws{s}', writes=[rn])
        elif plain:
            self.P.dma('sp', lambda e: e.dma_start(out=dst, in_=src_ap), f'ws{s}', writes=[rn])
        else:
            self.P.dma('pool', lambda e: e.dma_start(out=dst, in_=src_ap), f'ws{s}', writes=[rn])
        return dst, rn

    def build(self):
        nc, P = self.nc, self.P
        dr = lambda n, s, k: nc.dram_tensor(n, s, F32, kind=k).ap()
        self.xT_d = dr('xT', [8, 128, NT], 'ExternalInput')
        self.cw_d = dr('cw', [128, CW.ncols], 'ExternalInput')
        self.cb_d = dr('cb', [128, CB.ncols], 'ExternalInput')
        self.lw_d = dr('lw', [DEPTH, 128, LW.ncols], 'ExternalInput')
        self.wstream = dr('wstream', [DEPTH * TILES_PER_LAYER, 128, TILE_COLS], 'ExternalInput')
        self.wada_d = dr('wada', [DEPTH, 18, 128, TILE_COLS], 'ExternalInput')
        self.dft_d = nc.dram_tensor('dft', [8, 128, TILE_COLS], BF16, kind='ExternalInput').ap()
        self.ctxkv_d = dr('ctxkv', [DEPTH, 128, 256], 'ExternalInput')
        self.ctxkr_d = dr('ctxkr', [DEPTH, 32, 256], 'ExternalInput')
        self.hgst_d = dr('hgst', [DEPTH, 64, 2048], 'ExternalInput')
        self.gdst_d = dr('gdst', [DEPTH, 64, 2048], 'ExternalInput')
        self.yT_d = dr('yT', [8, 128, NT], 'ExternalOutput')
        self.ckvT_d = dr('ckvT', [DEPTH, 128, NT], 'ExternalOutput')
        self.krT_d = dr('krT', [DEPTH, 32, NT], 'ExternalOutput')
        self.hgo_d = dr('hgo', [DEPTH, 64, 2048], 'ExternalOutput')
        self.gdo_d = dr('gdo', [DEPTH, 64, 2048], 'ExternalOutput')
        self.dbg_d = {}
        for n, shp in self.dbg.items():
            self.dbg_d[n] = dr('dbg_' + n, list(shp), 'ExternalOutput')

        self.xT = self.sb('xT_sb', [128, 8, NT])
        self.h = self.sb('h_sb', [128, 8, NT], BF16)
        self.wsl = self.sb('wsl', [128, NWSLOT, TILE_COLS], BF16)
        self.cw = self.sb('cw_sb', [128, CW.ncols])
        self.lw = self.sb('lw_sb', [128, LW.ncols])
        self.A32, self.A16 = 18304, 24576
        self.ar32 = self.sb('ar32', [128, self.A32])
        self.ar16 = self.sb('ar16', [128, self.A16], BF16)
        self.misc = self.sb('misc', [128, 512])
        self.cb16 = self.sb('cb16', [128, 1024], BF16)
        self.ps = self.st.enter_context(nc.psum_tensor('ps', [128, 8, 512], F32))
        self.wslot_ctr = 0
        self.a32 = self.a16 = 0
        self.pref = []
        self.do_prefetch = (self.depth == DEPTH and all(x in self.stages for x in ('ffn0', 'hg', 'hy', 'gd', 'mla', 'ffn1')))

        self.ada = self.misc[:, 0:72]
        self.s2 = self.misc[:, 72:88].rearrange("p (k t) -> p k t", t=2)
        self.modA = self.misc[:, 96:120].rearrange("p (s k) -> p s k", k=8)
        self.modB = self.misc[:, 120:144].rearrange("p (s k) -> p s k", k=8)
        self.modG = self.misc[:, 144:168].rearrange("p (s k) -> p s k", k=8)
        self.epsc = self.misc[:, 168:169]
        self.eps6 = self.misc[:, 169:170]
        self.negpi = self.misc[:, 170:171]
        self.lball = self.misc[0:64, 176:208]
        self.adaraw = self.misc[:, 288:360]

        for kc in range(8):
            P.dma('sp', (lambda kc: lambda e: e.dma_start(out=self.xT[:, kc, :], in_=self.xT_d[kc]))(kc), 'init%d' % (kc % 2), writes=[('xT', kc)])
        P.dma('sp', lambda e: e.dma_start(out=self.cw[:], in_=self.cw_d), 'init', writes=['cw'])
        self.memset('dve', self.misc[:], 0.0, ['misc'])
        self.memset('dve', self.epsc, EPS, ['misc'])
        self.memset('dve', self.eps6, 1e-6, ['misc'])
        self.memset('dve', self.negpi, -math.pi, ['misc'])
        self.memset('dve', self.misc[:, 171:172], math.log(0.125), ['misc'])
        P.barrier()
        self.ones16 = self.cb16[:, 0:128]
        self.id16 = self.cb16[:, 128:256]
        self.cp('dve', self.ones16, self.c('ones'), ['cw'], ['cb16'])
        self.cp('dve', self.id16, self.c('ident'), ['cw'], ['cb16'])
        self.act(self.s2[:, :, 0], self.c('cond'), AF.Silu, ['cw'], ['misc'])
        self.cp('dve', self.s2[:, :, 1], self.s2[:, :, 0], ['misc'], ['misc'])
        self.s16 = self.cb16[:, 256:264]
        self.blk16 = self.cb16[:, 384:512]
        self.memset('dve', self.blk16, 0.0, ['cb16'])
        self.memset('dve', self.blk16[0:64, 0:64], 1.0, ['cb16'])
        self.memset('dve', self.blk16[64:128, 64:128], 1.0, ['cb16'])
        self.cp('dve', self.s16, self.s2[:, :, 0], ['misc'], ['cb16'])
        P.barrier()

        for l in range(self.depth):
            self.layer(l)

        P.barrier()
        for kc in range(8):
            P.dma('sp', (lambda kc: lambda e: e.dma_start(out=self.yT_d[kc], in_=self.xT[:, kc, :]))(kc), 'out%d' % (kc % 2), reads=[('xT', kc)])
        P.barrier()
        P.emit(self.st)

    def c(self, name):
        o, p, c = CW.items[name]
        return self.cw[0:p, o:o + c]

    def cbig(self, name, dst, dn):
        o, p, c = CB.items[name]
        self.P.dma('sp', lambda e: e.dma_start(out=dst, in_=self.cb_d[0:p, o:o + c]), 'cb', writes=[dn])

    def w(self, name):
        o, p, c = LW.items[name]
        return self.lw[0:p, o:o + c]

    def dump(self, name, ap, r):
        if name in self.dbg_d:
            self.P.dma('pool', lambda e: e.dma_start(out=self.dbg_d[name], in_=ap), 'dbg', reads=r)

    def layer(self, l):
        P = self.P
        self.arena_reset()
        P.dma('sp', lambda e: e.dma_start(out=self.lw[:], in_=self.lw_d[l]), 'lw', writes=['lw'])
        self.adaln(l)
        if l == 0:
            self.dump('misc', self.misc[:], ['misc', 'ada', 'mod'])
        if 'ffn0' in self.stages:
            self.ffn(l, 0)
        else:
            self.wtile += 19
        self.mixer(l)
        if 'ffn1' in self.stages:
            self.ffn(l, 2)
        else:
            self.wtile += 19
        if l == 0:
            self.dump('x_l0', self.xT[:], ['xT'])

    def ada_mm(self, l, tmp, tn):
        P = self.P
        identb = self.c('ident').unsqueeze(1).to_broadcast([128, 4, 128])
        for t in range(18):
            wt, wn = self.next_wtile(self.wada_d[l, t])
            wv = wt.rearrange("p (k c) -> p k c", c=512)
            b = self.bank()
            for kc in range(8):
                self.mm(self.ps[:, b, :], self.s16[:, kc:kc + 1].to_broadcast([128, 128]), wv[:, kc, :], kc == 0, kc == 7,
                        [wn, 'cb16'], [('ps', b)])
            tv = tmp[:, (t % 2) * 512:(t % 2 + 1) * 512].rearrange("p (j c) -> p j c", c=128)
            self.tt('dve', tv, self.ps[:, b, :].rearrange("p (j c) -> p j c", c=128), identb, ALU.mult, [('ps', b), 'cw'], [(tn, t % 2)])
            P.op('dve', (lambda tv, t: lambda e: e.tensor_reduce(out=self.adaraw[:, 4 * t:4 * t + 4], in_=tv, axis=mybir.AxisListType.X, op=ALU.add))(tv, t),
                 [(tn, t % 2)], ['adaraw'])
            yield

    def adaln(self, l):
        if l == 0 or 'mla' not in self.stages:
            tmp, tn = self.f32('ada_tmp', 128, 1024)
            for _ in self.ada_mm(l, tmp, tn):
                pass
        self.tt('dve', self.ada, self.adaraw, self.w('bada'), ALU.add, ['adaraw', 'lw'], ['ada'])
        adav = self.ada.rearrange("p (i k) -> p i k", k=8)
        gains = ['g_ffn0', 'g_mix', 'g_ffn1']
        for s in range(3):
            self.stt('dve', self.modA[:, s, :], adav[:, 3 * s + 1, :], 1.0, self.w(gains[s]), ALU.add, ALU.mult, ['ada', 'lw'], [('mod', s)])
            self.cp('dve', self.modB[:, s, :], adav[:, 3 * s, :], ['ada'], [('mod', s)])
            self.ts('dve', self.modG[:, s, :], adav[:, 3 * s + 2, :], 1.0 if s == 1 else 0.5, None, ALU.mult, None, ['ada'], [('mod', s)])

    def norm_mod(self, s):
        sq2 = [self.b16('sq0', 128, NT), self.b16('sq1', 128, NT)]
        rstd, rn = self.f32('rstd', 128, NT)
        tmp, tn = self.f32('nm_tmp', 128, 2 * NT)
        banks = [self.bank(), self.bank()]
        for kc in range(8):
            sq, sqn = sq2[kc % 2]
            self.act(sq, self.xT[:, kc, :], AF.Square, [('xT', kc)], [sqn])
            for blk in range(2):
                self.mm(self.ps[:, banks[blk], :], self.ones16, sq[:, blk * 512:(blk + 1) * 512], kc == 0, kc == 7,
                        [sqn, 'cb16'], [('ps', banks[blk])])
        for blk in range(2):
            self.rsqrt(rstd[:, blk * 512:(blk + 1) * 512], self.ps[:, banks[blk], :], 1.0 / D, self.epsc,
                       [('ps', banks[blk]), 'misc'], [(rn, blk)])
        for kc in range(8):
            t = tmp[:, (kc % 2) * NT:(kc % 2 + 1) * NT]
            self.stt('dve', t, self.xT[:, kc, :], self.modA[:, s, kc:kc + 1], rstd, ALU.mult, ALU.mult,
                     [('xT', kc), ('mod', s), rn], [(tn, kc % 2)])
            self.act(self.h[:, kc, :], t, AF.Identity, [(tn, kc % 2), ('mod', s)], [('h', kc)], bias=self.modB[:, s, kc:kc + 1])

    def ffn(self, l, s):
        self.arena_reset()
        self.norm_mod(s)
        if l == 0 and s == 0:
            self.dump('h0', self.h[:], ['h'])
        actb, an = self.b16('ffn_act', 128, NF * NT)
        actv = actb.rearrange("p (f t) -> p f t", t=NT)
        sg, sgn = self.f32('ffn_sg', 128, 4 * 512)
        k = 0
        for g in range(11):
            wt, wn = self.next_wtile()
            wv = wt.rearrange("p (k c) -> p k c", c=512)
            for ff in range(2):
                f = 2 * g + ff
                for blk in range(2):
                    bg, bu = self.bank(), self.bank()
                    for kc in range(8):
                        self.mm(self.ps[:, bg, :], wv[:, kc, ff * 128:(ff + 1) * 128], self.h[:, kc, blk * 512:(blk + 1) * 512],
                                kc == 0, kc == 7, [wn, ('h', kc)], [('ps', bg)])
                    for kc in range(8):
                        self.mm(self.ps[:, bu, :], wv[:, kc, 256 + ff * 128:256 + (ff + 1) * 128], self.h[:, kc, blk * 512:(blk + 1) * 512],
                                kc == 0, kc == 7, [wn, ('h', kc)], [('ps', bu)])
                    sgt = sg[:, (k % 4) * 512:(k % 4 + 1) * 512]
                    self.act(sgt, self.ps[:, bg, :], AF.Silu, [('ps', bg)], [(sgn, k % 4)])
                    self.tt('dve', actv[:, f, blk * 512:(blk + 1) * 512], sgt, self.ps[:, bu, :], ALU.mult,
                            [(sgn, k % 4), ('ps', bu)], [(an, f, blk)])
                    k += 1
        if l == 0 and s == 0:
            self.dump('act0', actv, [an])
        for dc in range(8):
            wt, wn = self.next_wtile()
            wv = wt[:, 0:NF * 128].rearrange("p (f c) -> p f c", c=128)
            for blk in range(2):
                b = self.bank()
                for f in range(NF):
                    self.mm(self.ps[:, b, :], wv[:, f, :], actv[:, f, blk * 512:(blk + 1) * 512], f == 0, f == NF - 1,
                            [wn, (an, f, blk)], [('ps', b)])
                xs = self.xT[:, dc, blk * 512:(blk + 1) * 512]
                self.stt('dve', xs, self.ps[:, b, :], self.modG[:, s, dc:dc + 1], xs, ALU.mult, ALU.add,
                         [('ps', b), ('mod', s), ('xT', dc)], [('xT', dc)])
        if s == 0:
            self.prefetch(3)
        elif l + 1 < self.depth:
            self.prefetch(2)

    def mixer(self, l):
        self.keep16 = 8 * NT
        self.arena_reset(self.keep16)
        self.norm_mod(1)
        self.omix = self.ar16[:, 0:8 * NT].rearrange("p (k t) -> p k t", t=NT)
        self.memset('pool', self.ar16[:, 0:8 * NT], 0.0, ['omix'])
        if 'hg' in self.stages:
            self.hgrn(l)
        else:
            self.wtile += 3
        if 'hy' in self.stages:
            self.hyena(l)
        else:
            self.wtile += 2
        if 'gd' in self.stages:
            self.gdn(l)
        else:
            self.wtile += 2
            self.arena_reset(self.keep16)
            self.abmla = self.next_wtile()
        if 'mla' in self.stages:
            g_mla = self.mla(l)
            next(g_mla)
            alive = [g_mla]
            if l + 1 < self.depth:
                alive.append(self.ada_mm(l + 1, *self.ada_tmp))
            while alive:
                for gg in list(alive):
                    try:
                        next(gg)
                    except StopIteration:
                        alive.remove(gg)
            self.prefetch(2)
        self.arena_reset(self.keep16)
        for hh in range(2):
            wt, wn = self.next_wtile()
            wv = wt.rearrange("p (k c) -> p k c", c=512)
            for dcl in range(4):
                dc = hh * 4 + dcl
                for blk in range(2):
                    b = self.bank()
                    for kc in range(8):
                        self.mm(self.ps[:, b, :], wv[:, kc, dcl * 128:(dcl + 1) * 128], self.omix[:, kc, blk * 512:(blk + 1) * 512],
                                kc == 0, kc == 7, [wn, 'omix'], [('ps', b)])
                    xs = self.xT[:, dc, blk * 512:(blk + 1) * 512]
                    self.stt('dve', xs, self.ps[:, b, :], self.modG[:, 1, dc:dc + 1], xs, ALU.mult, ALU.add,
                             [('ps', b), ('mod', 1), ('xT', dc)], [('xT', dc)])
        self.prefetch(2)

    def proj(self, wv, wn, c0, m, dst, dn, scale=None):
        if m == 64:
            c1 = min(c0, 384)
            off, mm_ = c0 - c1, 128
        else:
            c1, off, mm_ = c0, 0, m
        dn = (dn,) if isinstance(dn, str) else tuple(dn)
        for blk in range(2):
            b = self.bank()
            for kc in range(8):
                self.mm(self.ps[0:mm_, b, :], wv[:, kc, c1:c1 + mm_], self.h[:, kc, blk * 512:(blk + 1) * 512], kc == 0, kc == 7,
                        [wn, ('h', kc)], [('ps', b)])
            self.cp('act' if blk == 0 else 'dve', dst[0:m, blk * 512:(blk + 1) * 512], self.ps[off:off + m, b, :], [('ps', b)], [dn + (blk,)])

    def hgrn_lb(self):
        raw = self.w('hg_lbraw')
        e = self.misc[0:64, 240:272]
        ssum = self.misc[0:64, 272:280]
        self.act(e, raw, AF.Exp, ['lw'], ['lbe'])
        self.tt('dve', ssum, e[:, 0:8], e[:, 8:16], ALU.add, ['lbe'], ['lbs'])
        self.tt('dve', ssum, ssum, e[:, 16:24], ALU.add, ['lbe', 'lbs'], ['lbs'])
        self.tt('dve', ssum, ssum, e[:, 24:32], ALU.add, ['lbe', 'lbs'], ['lbs'])
        self.recip(ssum, ssum, ['lbs'], ['lbs'])
        lb = self.lball
        self.memset('dve', lb[:, 0:8], 0.0, ['lb'])
        for li in range(1, 4):
            self.tt('dve', e[:, li * 8:(li + 1) * 8], e[:, li * 8:(li + 1) * 8], ssum, ALU.mult, ['lbe', 'lbs'], ['lbe'])
            self.tt('dve', lb[:, li * 8:(li + 1) * 8], lb[:, (li - 1) * 8:li * 8], e[:, li * 8:(li + 1) * 8], ALU.add, ['lbe', 'lb'], ['lb'])
        self.ts('dve', self.misc[0:64, 208:240], lb, -1.0, 1.0, ALU.mult, ALU.add, ['lb'], ['lb'])

    def hgrn(self, l):
        P = self.P
        self.arena_reset(self.keep16)
        lb2 = self.misc[:, 360:376]
        oml2 = self.misc[:, 376:392]
        if l == 0:
            self.hgrn_lb()
            for hl in range(2):
                for (dst, src) in ((lb2, self.lball), (oml2, self.misc[0:64, 208:240])):
                    self.cp('dve', dst[hl * 64:(hl + 1) * 64, :].rearrange("k (a p) -> k a p", p=2),
                            src.rearrange("k (a p h) -> k a p h", p=2, h=2)[:, :, :, hl], ['lb'], ['lb2'])
        tiles = [self.next_wtile() for _ in range(3)]
        tv = [(t.rearrange("p (k c) -> p k c", c=512), n) for t, n in tiles]
        st, stn = self.f32('hg_st', 128, 1024)
        stv = st.rearrange("p (s d q v) -> p s d q v", s=4, d=2, q=2)
        for hl in range(2):
            srcv = self.hgst_d[l].rearrange("k (s d q h v) -> k s d q h v", s=4, d=2, q=2, h=2)[:, :, :, :, hl, :]
            P.dma('sp', (lambda hl, srcv: lambda e: e.dma_start(out=stv[hl * 64:(hl + 1) * 64], in_=srcv))(hl, srcv), 'hgin', writes=[stn])
        gn2, gn2n = self.f32('hg_gn2', 128, 2)
        for hl in range(2):
            self.cp('dve', gn2[hl * 64:(hl + 1) * 64, :], self.w('hg_gn').rearrange("k (q h) -> k q h", h=2)[:, :, hl], ['lw'], [gn2n])
        q, qn = self.f32('hg_q', 128, NT)
        g2 = [self.f32('hg_g0', 128, NT), self.f32('hg_g1', 128, NT)]
        z, zn = self.f32('hg_z', 128, NT)
        KK, kkn = self.f32('hg_kk', 128, NT)
        Pz, pzn = self.f32('hg_pz', 128, NT + 32)
        E, en = self.f32('hg_e', 128, NT)
        X, xn = self.f32('hg_x', 128, NT)
        og, ogn = self.f32('hg_og', 128, NT)
        rsB, rsBn = self.f32('hg_rsB', 128, NT)
        Dm2 = [self.f32('hg_D0', 128, 2048), self.f32('hg_D1', 128, 2048)]
        dd2 = [self.f32('hg_dd0', 128, 64), self.f32('hg_dd1', 128, 64)]
        S, sn = self.f32('hg_S', 128, 64)
        sqB, sqBn = self.b16in32('hg_sqB', 128, NT)
        qe2 = [self.b16('hg_qe0', 128, NT), self.b16('hg_qe1', 128, NT)]
        ke, ken = self.b16('hg_ke', 128, NT)
        ko, kon = self.b16('hg_ko', 128, NT)
        kt, ktn = self.b16('hg_kt', 32, 2048)
        vt, vtn = self.b16('hg_vt', 32, 4096)
        AT2 = [self.b16('hg_AT0', 32, 2 * NT), self.b16('hg_AT1', 32, 2 * NT)]
        Sa, san = self.b16('hg_Sa', 128, 2048)
        ktv = kt.rearrange("p (n k) -> p n k", k=64)
        vtv = vt.rearrange("p (n c) -> p n c", c=128)
        Sav = Sa.rearrange("p (n v) -> p n v", v=64)
        ones_b = self.c('ones')[:, 0:1].to_broadcast([128, NT])
        self.memset('dve', Pz[:, 0:1], 0.0, [pzn])
        self.memset('dve', Pz[:, NT + 1:NT + 32], 0.0, [pzn])
        m32 = self.c('m32')

        def chunkcol(ap, off):
            return ap[:, off:off + NT].rearrange("p (n c) -> p n c", c=32)[:, :, 0]

        def bc(ap):
            return ap.unsqueeze(2).to_broadcast([128, 32, 32])

        v3 = lambda ap: ap.rearrange("p (n c) -> p n c", c=32)

        def front(u):
            pr, d = u // 2, u % 2
            qe, qen = qe2[d]
            AT, atn = AT2[d]
            Dm, dmn = Dm2[d]
            dd, ddn = dd2[d]
            g, gn = g2[pr]
            ATv = AT.rearrange("p (h n i) -> p h n i", h=2, i=32)
            if d == 0:
                self.proj(tv[0][0], tv[0][1], pr * 128, 128, q, qn)
                yield
                self.proj(tv[1][0], tv[1][1], pr * 128, 128, g, gn)
                self.act(g, g, AF.Silu, [gn], [gn])
                yield
                for n4 in range(8):
                    b = self.bank()
                    for nn in range(4):
                        n = n4 * 4 + nn
                        for kc in range(8):
                            self.mm(self.ps[0:32, b, nn * 128:(nn + 1) * 128], self.h[:, kc, n * 32:(n + 1) * 32], tv[0][0][:, kc, 256 + pr * 128:256 + (pr + 1) * 128],
                                    kc == 0, kc == 7, [tv[0][1], ('h', kc)], [('ps', b)])
                    self.cp('act', vt[:, n4 * 512:(n4 + 1) * 512], self.ps[0:32, b, :], [('ps', b)], [(vtn, n4)])
                    yield
            col = l * 4 + d * 2 + pr
            if d == 0:
                self.proj(tv[1][0], tv[1][1], 256 + pr * 128, 128, z, zn)
            else:
                self.proj(tv[2][0], tv[2][1], pr * 128, 128, z, zn)
            yield
            F, fn = z, zn
            self.act(F, z, AF.Sigmoid, [zn], [fn])
            self.ts('dve', F, F, oml2[:, col:col + 1], lb2[:, col:col + 1], ALU.mult, ALU.add, [fn, 'lb2'], [fn])
            self.act(KK, F, AF.Identity, [fn], [kkn], bias=1.0, scale=-1.0)
            self.act(F, F, AF.Ln, [fn], [fn])
            yield
            P.op('dve', lambda e: e.tensor_tensor_scan(out=Pz[:, 1:NT + 1], data0=ones_b, data1=F, initial=0.0, op0=ALU.mult, op1=ALU.add),
                 [fn, 'cw'], [pzn])
            V = Pz[:, 1:NT + 1] if d == 0 else Pz[:, 0:NT]
            A_ = chunkcol(Pz, 16)
            P0 = chunkcol(Pz, 0)
            Pl = chunkcol(Pz, 32)
            sgn = 1.0 if d == 0 else -1.0
            self.tt('pool', v3(E), v3(V), bc(A_), ALU.subtract, [pzn], [en])
            self.ts('pool', E, E, 80.0, -80.0, ALU.min, ALU.max, [en], [en])
            yield
            self.act(X, E, AF.Exp, [en], [xn], scale=sgn, bias=self.misc[:, 171:172])
            self.tt('pool', qe, q, X, ALU.mult, [qn, xn], [qen])
            self.act(X, E, AF.Exp, [en], [xn], scale=-sgn)
            self.tt('pool', ke, KK, X, ALU.mult, [kkn, xn], [ken])
            yield
            self.tt('pool', v3(E), v3(V), bc(Pl if d == 0 else P0), ALU.subtract, [pzn], [en])
            self.act(X, E, AF.Exp, [en], [xn], scale=-sgn)
            self.tt('pool', ko, KK, X, ALU.mult, [kkn, xn], [kon])
            if d == 0:
                self.tt('pool', dd[:, 0:32], A_, P0, ALU.subtract, [pzn], [ddn])
            else:
                self.tt('pool', dd[:, 0:32], Pl, A_, ALU.subtract, [pzn], [ddn])
            self.tt('pool', dd[:, 32:64], Pl, P0, ALU.subtract, [pzn], [ddn])
            self.act(dd, dd, AF.Exp, [ddn], [ddn])
            yield
            for hl in range(2):
                ph = slice(hl * 64, (hl + 1) * 64)
                for half in range(2):
                    b = self.bank()
                    pb = self.ps[:, b, :].bitcast(BF16)
                    for nn in range(16):
                        n = half * 16 + nn
                        self.tr(pb[0:32, nn * 64:(nn + 1) * 64], ko[ph, n * 32:(n + 1) * 32], self.id16[ph, ph], [kon, 'cb16'], [('ps', b)])
                    self.cp('act', kt[:, half * 1024:(half + 1) * 1024], pb[0:32, :], [('ps', b)], [(ktn, half)])
                    yield
                for half in range(2):
                    b = self.bank()
                    for nn in range(16):
                        n = half * 16 + nn
                        self.mm(self.ps[0:32, b, nn * 32:(nn + 1) * 32], ke[ph, n * 32:(n + 1) * 32], qe[ph, n * 32:(n + 1) * 32], True, True, [ken, qen], [('ps', b)])
                    self.tt('dve', ATv[:, hl, half * 16:(half + 1) * 16, :], self.ps[0:32, b, :].rearrange("p (n i) -> p n i", i=32),
                            m32[:, d * 32:(d + 1) * 32].unsqueeze(1).to_broadcast([32, 16, 32]), ALU.mult, [('ps', b), 'cw'], [(atn, hl, half)])
                    yield
                for n8 in range(4):
                    b = self.bank()
                    for nn in range(8):
                        n = n8 * 8 + nn
                        self.mm(self.ps[ph, b, nn * 64:(nn + 1) * 64], ktv[:, n, :], vtv[:, n, hl * 64:(hl + 1) * 64], True, True, [ktn, vtn], [('ps', b)])
                    self.cp('act', Dm[ph, n8 * 512:(n8 + 1) * 512], self.ps[ph, b, :], [('ps', b)], [(dmn, hl, n8)])
                    yield

        def back(u):
            pr, d = u // 2, u % 2
            qe, qen = qe2[d]
            AT, atn = AT2[d]
            Dm, dmn = Dm2[d]
            dd, ddn = dd2[d]
            g, gn = g2[pr]
            ATv = AT.rearrange("p (h n i) -> p h n i", h=2, i=32)
            Dv = Dm.rearrange("p (n v) -> p n v", v=64)
            self.memset('dve', S, 0.0, [sn])
            order = range(32) if d == 0 else range(31, -1, -1)
            for n in order:
                seg = n // 8
                first = (n % 8 == 0) if d == 0 else (n % 8 == 7)
                last = (n % 8 == 7) if d == 0 else (n % 8 == 0)
                if first:
                    self.stt('dve', S, S, self.c('carry'), stv[:, seg, d, pr, :], ALU.mult, ALU.add, [sn, 'cw', (stn, seg, d, pr)], [sn])
                self.ts('dve', Sav[:, n, :], S, dd[:, n:n + 1], None, ALU.mult, None, [sn, ddn], [(san, n)])
                self.stt('dve', S, S, dd[:, 32 + n:33 + n], Dv[:, n, :], ALU.mult, ALU.add, [sn, ddn, dmn], [sn])
                if last:
                    self.cp('dve', stv[:, seg, d, pr, :], S, [sn], [(stn, seg, d, pr)])
                if n % 4 == 3:
                    yield
            for half in range(2):
                b = self.bank()
                for hl in range(2):
                    ph = slice(hl * 64, (hl + 1) * 64)
                    for nn in range(16):
                        n = half * 16 + nn
                        cs = slice(n * 32, (n + 1) * 32)
                        self.mm(self.ps[ph, b, nn * 32:(nn + 1) * 32], Sav[ph, n, :], qe[ph, cs], nn == 0, False, [(san, n), qen], [('ps', b)])
                    for nn in range(16):
                        n = half * 16 + nn
                        self.mm(self.ps[ph, b, nn * 32:(nn + 1) * 32], vtv[:, n, hl * 64:(hl + 1) * 64], ATv[:, hl, n, :], False, nn == 15, [vtn, (atn, hl, half)], [('ps', b)])
                osl = og[:, half * 512:(half + 1) * 512]
                if d == 0:
                    self.cp('act', osl, self.ps[:, b, :], [('ps', b)], [(ogn, half)])
                else:
                    self.tt('dve', osl, osl, self.ps[:, b, :], ALU.add, [(ogn, half), ('ps', b)], [(ogn, half)])
                yield
            if d == 1:
                self.rms_fm(og, ogn, 128, NT, sqB, sqBn, rsB, rsBn, 1.0 / 64, ones=self.blk16)
                self.stt('dve', og, og, gn2[:, pr:pr + 1], rsB, ALU.mult, ALU.mult, [ogn, gn2n, rsBn], [ogn])
                self.tt('dve', self.omix[:, pr, :], og, g, ALU.mult, [ogn, gn], ['omix'])
                yield

        def run(gen):
            for _ in gen:
                pass

        def interleave(a, b_):
            alive = [a, b_]
            while alive:
                for gg in list(alive):
                    try:
                        next(gg)
                    except StopIteration:
                        alive.remove(gg)

        run(front(0))
        interleave(back(0), front(1))
        run(back(1))
        run(front(2))
        interleave(back(2), front(3))
        run(back(3))
        for hl in range(2):
            dstv = self.hgo_d[l].rearrange("k (s d q h v) -> k s d q h v", s=4, d=2, q=2, h=2)[:, :, :, :, hl, :]
            P.dma('sp', (lambda hl, dstv: lambda e: e.dma_start(out=dstv, in_=stv[hl * 64:(hl + 1) * 64]))(hl, dstv), 'hgo', reads=[stn])
        self.prefetch(2)

    def sin_rr(self, out, arg, tmp, r, w, tn):
        self.ts('dve', tmp, arg, 1.0 / (2 * math.pi), MAGIC, ALU.mult, ALU.add, r, [tn])
        self.ts('dve', tmp, tmp, MAGIC, None, ALU.subtract, None, [tn], [tn])
        self.stt('dve', tmp, tmp, -2.0 * math.pi, arg, ALU.mult, ALU.add, [tn] + r, [tn])
        self.act(out, tmp, AF.Sin, [tn], w)

    def hyena(self, l):
        P = self.P
        self.arena_reset(self.keep16)
        zemb, zn = self.f32('zemb', 33, NT)
        winf, wfn = self.f32('winf', 128, 2048)
        winb, wbn = self.f32('winb', 128, 2048)
        self.cbig('zemb', zemb, zn)
        self.cbig('winf', winf, wfn)
        self.cbig('winb', winb, wbn)
        a1, a1n = self.f32('hy_a1', 64, NT)
        t1, t1n = self.f32('hy_t1', 64, NT)
        h1, h1n = self.f32('hy_h1', 64, NT)
        sc, scn = self.f32('hy_sc', 64, 2)
        hs, hsn = self.b16('hy_hs', 128, 2048)
        hd, hdn = self.b16('hy_hd', 128, 2048)
        assert self.a16 == self.keep16 + 4096
        hsv = hs.rearrange("p (j c) -> p j c", c=256)
        hdv = hd.rearrange("p (j c) -> p j c", c=256)
        self.tt('dve', sc[:, 0:1], self.w('hy_fr'), self.w('hy_b1'), ALU.mult, ['lw'], [scn])
        self.tt('dve', sc[:, 1:2], self.w('hy_fr'), self.w('hy_b2'), ALU.mult, ['lw'], [scn])
        src, srcn = zemb, zn
        for li, (wname, kdim) in enumerate((('hy_w1', 33), ('hy_w2', 64))):
            for blk in range(2):
                b = self.bank()
                self.mm(self.ps[0:64, b, :], self.w(wname), src[0:kdim, blk * 512:(blk + 1) * 512], True, True, ['lw', srcn], [('ps', b)])
                self.act(a1[:, blk * 512:(blk + 1) * 512], self.ps[0:64, b, :], AF.Identity, [('ps', b), 'lw', scn], [(a1n, blk)],
                         bias=sc[:, li:li + 1], scale=self.w('hy_fr'))
            dst, dstn = (h1, h1n) if li == 0 else (a1, a1n)
            self.sin_rr(dst, a1, t1, [a1n], [dstn], t1n)
            src, srcn = dst, dstn
        h2, h2n = src, srcn
        tf, tfn = self.f32('hy_tf', 128, 512)
        for pc in range(8):
            b = self.bank()
            self.mm(self.ps[:, b, :], h2[:, pc * 128:(pc + 1) * 128], self.w('hy_w3'), True, True, [h2n, 'lw'], [('ps', b)])
            self.tt('dve', tf[:, 0:256], self.ps[:, b, 0:256], winf[:, pc * 256:(pc + 1) * 256], ALU.mult, [('ps', b), wfn], [tfn])
            self.tt('dve', tf[:, 256:512], self.ps[:, b, 256:512], winb[:, pc * 256:(pc + 1) * 256], ALU.mult, [('ps', b), wbn], [tfn])
            self.tt('dve', hsv[:, pc, :], tf[:, 0:256], tf[:, 256:512], ALU.add, [tfn], [(hsn, pc)])
            self.tt('dve', hdv[:, pc, :], tf[:, 256:512], tf[:, 0:256], ALU.subtract, [tfn], [(hdn, pc)])
        self.arena_reset(self.keep16 + 4096)
        hsn, hdn = 'hyhs', 'hyhd'
        hsv = self.ar16[:, self.keep16:self.keep16 + 2048].rearrange("p (j c) -> p j c", c=256)
        hdv = self.ar16[:, self.keep16 + 2048:self.keep16 + 4096].rearrange("p (j c) -> p j c", c=256)
        U, un = self.f32('hy_u', 128, 6 * NT)
        Cc, ccn = self.f32('hy_c', 128, 3 * NT)
        t1, n1 = self.next_wtile()
        t2, n2 = self.next_wtile()
        v1 = t1.rearrange("p (k c) -> p k c", c=512)
        v2 = t2.rearrange("p (k c) -> p k c", c=512)
        for g in range(6):
            wv, wn, c0 = (v1, n1, g * 128) if g < 4 else (v2, n2, (g - 4) * 128)
            self.proj(wv, wn, c0, 128, U[:, g * NT:(g + 1) * NT], (un, g))
        nw, nwn = self.f32('hy_nw', 128, 18)
        self.ts('dve', nw, self.w('hy_cw'), self.c('ncarry'), None, ALU.mult, None, ['lw', 'cw'], [nwn])
        cwv = self.w('hy_cw').rearrange("p (g k) -> p g k", k=3)
        nwv = nw.rearrange("p (g k) -> p g k", k=3)
        zb, zbn = self.b16('hy_zb', 128, 2 * NT)
        zT, ztn = self.b16('hy_zT', 128, 2048)
        zTv = zT.rearrange("p (t c) -> p t c", c=256)

        def conv(g, dst, dn):
            x = U[:, g * NT:(g + 1) * NT]
            xn = (un, g)
            self.act(dst, x, AF.Identity, [xn, 'lw'], [dn], bias=self.w('hy_cb')[:, g:g + 1], scale=cwv[:, g, 1:2])
            self.stt('dve', dst[:, 1:NT], x[:, 0:NT - 1], cwv[:, g, 0:1], dst[:, 1:NT], ALU.mult, ALU.add, [xn, 'lw', dn], [dn])
            self.stt('dve', dst[:, 0:NT - 1], x[:, 1:NT], cwv[:, g, 2:3], dst[:, 0:NT - 1], ALU.mult, ALU.add, [xn, 'lw', dn], [dn])
            xs = x.rearrange("p (s t) -> p s t", t=256)
            ds = dst.rearrange("p (s t) -> p s t", t=256)
            self.stt('dve', ds[:, 1:4, 0], xs[:, 0:3, 255], nwv[:, g, 0:1], ds[:, 1:4, 0], ALU.mult, ALU.add, [xn, nwn, dn], [dn])
            self.stt('dve', ds[:, 0:3, 255], xs[:, 1:4, 0], nwv[:, g, 2:3], ds[:, 0:3, 255], ALU.mult, ALU.add, [xn, nwn, dn], [dn])

        for cc in range(2):
            conv(cc, Cc[:, cc * NT:(cc + 1) * NT], (ccn, cc))
        for cc in range(2):
            sc2 = Cc[:, 2 * NT:3 * NT]
            conv(2 + cc, sc2, (ccn, 2))
            conv(4 + cc, U[:, cc * NT:(cc + 1) * NT], (un, cc))
            self.tt('dve', U[:, (2 + cc) * NT:(3 + cc) * NT], sc2, U[:, cc * NT:(cc + 1) * NT], ALU.mult, [(ccn, 2), (un, cc), (un, 2 + cc)], [(un, 2 + cc)])
            self.cp('act', zb[:, cc * NT:(cc + 1) * NT], U[:, (2 + cc) * NT:(3 + cc) * NT], [(un, 2 + cc)], [(zbn, cc)])
        for tc in range(8):
            b = self.bank()
            pb = self.ps[:, b, :].bitcast(BF16)
            for cc in range(2):
                self.tr(pb[:, cc * 128:(cc + 1) * 128], zb[:, cc * NT + tc * 128:cc * NT + (tc + 1) * 128], self.id16, [(zbn, cc), 'cb16'], [('ps', b)])
            self.cp('dve' if tc % 2 else 'act', zTv[:, tc, :], pb[:, 0:256], [('ps', b)], [(ztn, tc)])
        Y, yn = self.b16('hy_Y', 128, 4096)
        Yv = Y.rearrange("p (r f c) -> p r f c", r=2, c=256)
        zz, zzn = self.f32('hy_zz', 128, 1024)
        for half in range(2):
            ct, cn = self.next_wtile(self.dft_d[half], plain=True)
            stl, sn = self.next_wtile(self.dft_d[2 + half], plain=True)
            cv = ct.rearrange("p (k c) -> p k c", c=512)
            sv = stl.rearrange("p (k c) -> p k c", c=512)
            for fcl in range(4):
                fc = half * 4 + fcl
                b = self.bank()
                for tc in range(8):
                    self.mm(self.ps[:, b, 0:256], cv[:, tc, fcl * 128:(fcl + 1) * 128], zTv[:, tc, :], tc == 0, tc == 7, [cn, ztn], [('ps', b)])
                b2 = self.bank()
                for tc in range(8):
                    self.mm(self.ps[:, b2, 0:256], sv[:, tc, fcl * 128:(fcl + 1) * 128], zTv[:, tc, :], tc == 0, tc == 7, [sn, ztn], [('ps', b2)])
                bk1 = self.bank()
                for jc in range(8):
                    self.mm(self.ps[:, bk1, 0:256], cv[:, jc, fcl * 128:(fcl + 1) * 128], hsv[:, jc, :], jc == 0, jc == 7, [cn, hsn], [('ps', bk1)])
                bk2 = self.bank()
                for jc in range(8):
                    self.mm(self.ps[:, bk2, 0:256], sv[:, jc, fcl * 128:(fcl + 1) * 128], hdv[:, jc, :], jc == 0, jc == 7, [sn, hdn], [('ps', bk2)])
                kre, kim = self.ps[:, bk1, 0:256], self.ps[:, bk2, 0:256]
                self.cp('act', zz[:, 0:256], self.ps[:, b, 0:256], [('ps', b)], [zzn])
                self.cp('act', zz[:, 256:512], self.ps[:, b2, 0:256], [('ps', b2)], [zzn])
                self.tt('dve', zz[:, 512:768], zz[:, 0:256], kre, ALU.mult, [zzn, ('ps', bk1)], [zzn])
                self.tt('dve', zz[:, 768:1024], zz[:, 256:512], kim, ALU.mult, [zzn, ('ps', bk2)], [zzn])
                self.tt('pool', Yv[:, 0, fc, :], zz[:, 512:768], zz[:, 768:1024], ALU.add, [zzn], [(yn, 0, fc)])
                self.tt('dve', zz[:, 512:768], zz[:, 256:512], kre, ALU.mult, [zzn, ('ps', bk1)], [zzn])
                self.tt('dve', zz[:, 768:1024], zz[:, 0:256], kim, ALU.mult, [zzn, ('ps', bk2)], [zzn])
                self.tt('pool', Yv[:, 1, fc, :], zz[:, 512:768], zz[:, 768:1024], ALU.subtract, [zzn], [(yn, 1, fc)])
        for blk in range(2):
            ct, cn = self.next_wtile(self.dft_d[4 + blk], plain=True)
            stl, sn = self.next_wtile(self.dft_d[6 + blk], plain=True)
            cv = ct.rearrange("p (k c) -> p k c", c=512)
            sv = stl.rearrange("p (k c) -> p k c", c=512)
            for cc in range(2):
                b = self.bank()
                for fc in range(8):
                    self.mm(self.ps[:, b, :], Yv[:, 0, fc, cc * 128:(cc + 1) * 128], cv[:, fc, :], fc == 0, False, [cn, yn], [('ps', b)])
                    self.mm(self.ps[:, b, :], Yv[:, 1, fc, cc * 128:(cc + 1) * 128], sv[:, fc, :], False, fc == 7, [sn, yn], [('ps', b)])
                zsl = U[:, (2 + cc) * NT + blk * 512:(2 + cc) * NT + (blk + 1) * 512]
                tmp = zz[:, 0:512]
                self.stt('dve', tmp, zsl, self.w('hy_skip')[:, cc:cc + 1], self.ps[:, b, :], ALU.mult, ALU.add, [(un, 2 + cc), 'lw', ('ps', b)], [zzn])
                self.tt('dve', self.omix[:, 2 + cc, blk * 512:(blk + 1) * 512], tmp, Cc[:, cc * NT + blk * 512:cc * NT + (blk + 1) * 512], ALU.mult,
                        [zzn, (ccn, cc)], ['omix'])
        self.prefetch(3)

    def gdn(self, l):
        P = self.P
        self.arena_reset(self.keep16)
        tq, tqn = self.next_wtile()
        tvt, tvn = self.next_wtile()
        tab, tabn = self.next_wtile()
        self.abmla = (tab, tabn)
        tqv = tq.rearrange("p (k c) -> p k c", c=512)
        tvv = tvt.rearrange("p (k c) -> p k c", c=512)
        tabv = tab.rearrange("p (k c) -> p k c", c=512)
        G, gn_ = self.f32('gd_G', 16, NT)
        X1, x1n = self.f32('gd_X1', 16, NT)
        BE, ben = self.f32('gd_BE', 16, NT)
        tok, tokn = self.f32('gd_tok', 64, 1280)
        tokv = tok.rearrange("p (q n r) -> p q n r", q=5, r=16)
        R16, r16n = self.f32('gd_R16', 16, NT)
        LA, lan = self.f32('gd_LA', 16, NT)
        Pz, pzn = self.f32('gd_Pz', 16, NT + 64)
        X2, x2n = self.f32('gd_X2', 16, NT)
        X3, x3n = self.f32('gd_X3', 16, NT)
        tF, tfn = self.f32('gd_tF', 16, NT)
        tB, tbn = self.f32('gd_tB', 16, NT)
        nega, ngn = self.f32('gd_nega', 16, 2)
        mf, mb = self.c('mf'), self.c('mb')
        self.proj(tabv, tabn, 0, 16, R16, r16n)
        self.act(BE, R16, AF.Sigmoid, [r16n], [ben])
        self.act(LA, R16, AF.Exp, [r16n, 'lw'], [lan], bias=self.w('gd_dtb'))
        self.act(LA, LA, AF.Ln, [lan], [lan], bias=1.0)
        self.act(nega[:, 0:1], self.w('gd_alog'), AF.Exp, ['lw'], [ngn])
        self.ts('dve', nega[:, 1:2], nega[:, 0:1], -1.0, None, ALU.mult, None, [ngn], [ngn])
        self.ts('dve', LA, LA, nega[:, 1:2], None, ALU.mult, None, [lan, ngn], [lan])
        self.memset('dve', Pz, 0.0, [pzn])
        ones_b = self.c('ones')[0:16, 0:1].to_broadcast([16, NT])
        P.op('dve', lambda e: e.tensor_tensor_scan(out=Pz[:, 1:NT + 1], data0=ones_b, data1=LA, initial=0.0, op0=ALU.mult, op1=ALU.add),
             [lan, 'cw', pzn], [pzn])
        V = Pz[:, 1:NT + 1]
        W = Pz[:, 0:NT]
        c3 = lambda ap: ap.rearrange("p (n c) -> p n c", c=64)
        bcc = lambda ap: ap.unsqueeze(2).to_broadcast([16, 16, 64])
        P0 = Pz[:, 0:NT].rearrange("p (n c) -> p n c", c=64)[:, :, 0]
        Pl = Pz[:, 64:64 + NT].rearrange("p (n c) -> p n c", c=64)[:, :, 0]

        def combine(dst, dn):
            self.ts('dve', tF, tF, mf, None, ALU.mult, None, [tfn, 'cw'], [tfn])
            self.stt('dve', dst, tB, mb, tF, ALU.mult, ALU.add, [tbn, 'cw', tfn], [dn])

        self.cp('dve', tF, V, [pzn], [tfn])
        self.ts('dve', tB, W, -1.0, None, ALU.mult, None, [pzn], [tbn])
        combine(G, gn_)
        self.tt('dve', c3(tF), c3(V), bcc(P0), ALU.subtract, [pzn], [tfn])
        self.act(tF, tF, AF.Exp, [tfn], [tfn])
        self.tt('dve', c3(tB), c3(W), bcc(Pl), ALU.subtract, [pzn], [tbn])
        self.act(tB, tB, AF.Exp, [tbn], [tbn], scale=-1.0)
        combine(X1, x1n)
        self.tt('dve', c3(tF), c3(V), bcc(Pl), ALU.subtract, [pzn], [tfn])
        self.act(tF, tF, AF.Exp, [tfn], [tfn], scale=-1.0)
        self.tt('dve', c3(tB), c3(W), bcc(P0), ALU.subtract, [pzn], [tbn])
        self.act(tB, tB, AF.Exp, [tbn], [tbn])
        combine(X2, x2n)
        self.memset('dve', X3, 0.0, [x3n])
        self.tt('dve', c3(X3), c3(X3), bcc(Pl), ALU.add, [x3n, pzn], [x3n])
        self.tt('dve', c3(X3), c3(X3), bcc(P0), ALU.subtract, [x3n, pzn], [x3n])
        self.act(X3, X3, AF.Exp, [x3n], [x3n])
        for qi, (src, srcn) in enumerate(((G, gn_), (BE, ben), (X1, x1n), (X2, x2n), (X3, x3n))):
            b = self.bank()
            for n in range(16):
                self.tr(self.ps[0:64, b, n * 16:(n + 1) * 16], src[:, n * 64:(n + 1) * 64], self.c('ident')[0:16, 0:16], [srcn, 'cw'], [('ps', b)])
            self.cp('act' if qi % 2 else 'dve', tok[:, qi * 256:(qi + 1) * 256], self.ps[0:64, b, 0:256], [('ps', b)], [(tokn, qi)])
        self.arena_reset(self.keep16, keep32=3 * NT + 1280)
        gn_, x1n, ben, tokn = 'gdG', 'gdX1', 'gdBE', 'gdtok'
        st, stn = self.f32('gd_st', 64, 2048)
        P.dma('sp', lambda e: e.dma_start(out=st, in_=self.gdst_d[l]), 'gdin', writes=[stn])
        stv = st.rearrange("p (s d h v) -> p s d h v", s=4, d=2, h=4)
        KKs, kksn = self.f32('gd_KK', 64, NT)
        QKs, qksn = self.f32('gd_QK', 64, NT)
        E3, e3n = self.f32('gd_E3', 64, NT)
        Tm, tmn = self.f32('gd_Tm', 64, NT)
        Yt, ytn = self.f32('gd_Yt', 64, NT)
        Y, yn = self.f32('gd_Y', 64, NT)
        Pm, pmn = self.f32('gd_P', 64, NT)
        og, ogn = self.f32('gd_og', 64, NT)
        S, sn = self.f32('gd_S', 64, 64)
        nw, nwn = self.f32('gd_nw', 64, 36)
        qn16, qnn = self.b16('gd_qn', 64, NT)
        kn16, knn = self.b16('gd_kn', 64, NT)
        v16, v16n = self.b16('gd_v16', 64, NT)
        sg16, sgn_ = self.b16('gd_sg', 64, NT)
        ktok, ktn = self.b16('gd_ktok', 64, NT)
        vtok, vtn = self.b16('gd_vtok', 64, NT)
        attnT, atn = self.b16('gd_attnT', 64, NT)
        T16, t16n = self.b16('gd_T16', 64, NT)
        kg, kgn = self.b16('gd_kg', 64, NT)
        kout, kon = self.b16('gd_kout', 64, NT)
        nw0T, nw0n = self.b16('gd_nw0T', 64, NT)
        qin, qinn = self.b16('gd_qin', 64, NT)
        Sall, saln = self.b16('gd_Sall', 64, NT)
        vnew, vnn = self.b16('gd_vnew', 64, NT)
        sq, sqn = self.b16('gd_sq', 64, NT)
        u3 = lambda ap: ap.rearrange("p (n c) -> p n c", c=64)
        ub = lambda ap: ap.unsqueeze(2).to_broadcast([64, 16, 64])
        mb3 = lambda ap: ap.unsqueeze(1).to_broadcast([64, 16, 64])
        G = self.ar32[0:16, 0:NT]
        X1 = self.ar32[0:16, NT:2 * NT]
        BE = self.ar32[0:16, 2 * NT:3 * NT]
        tokv = self.ar32[0:64, 3 * NT:3 * NT + 1280].rearrange("p (q n r) -> p q n r", q=5, r=16)
        m64 = self.c('m64')
        ident = self.c('ident')
        self.ts('dve', nw, self.w('gd_cw'), self.c('ncarry')[0:64, :], None, ALU.mult, None, ['lw', 'cw'], [nwn])
        cwv = self.w('gd_cw').rearrange("p (g k) -> p g k", k=3)
        nwv = nw.rearrange("p (g k) -> p g k", k=3)
        R, rn = E3, e3n
        cx, cxn = Tm, tmn

        def conv_silu(gi):
            self.act(cx, R, AF.Copy, [rn, 'lw'], [cxn], scale=cwv[:, gi, 1:2])
            self.stt('dve', cx[:, 1:NT], R[:, 0:NT - 1], cwv[:, gi, 0:1], cx[:, 1:NT], ALU.mult, ALU.add, [rn, 'lw', cxn], [cxn])
            self.stt('dve', cx[:, 0:NT - 1], R[:, 1:NT], cwv[:, gi, 2:3], cx[:, 0:NT - 1], ALU.mult, ALU.add, [rn, 'lw', cxn], [cxn])
            xs = R.rearrange("p (s t) -> p s t", t=256)
            ds = cx.rearrange("p (s t) -> p s t", t=256)
            self.stt('dve', ds[:, 1:4, 0], xs[:, 0:3, 255], nwv[:, gi, 0:1], ds[:, 1:4, 0], ALU.mult, ALU.add, [rn, nwn, cxn], [cxn])
            self.stt('dve', ds[:, 0:3, 255], xs[:, 1:4, 0], nwv[:, gi, 2:3], ds[:, 0:3, 255], ALU.mult, ALU.add, [rn, nwn, cxn], [cxn])
            self.act(cx, cx, AF.Silu, [cxn], [cxn])

        def bcast_row(src, srcn, row):
            bs = []
            for blk in range(2):
                b = self.bank()
                self.mm(self.ps[0:64, b, :], ident[0:16, row:row + 1].to_broadcast([16, 64]), src[:, blk * 512:(blk + 1) * 512], True, True,
                        [srcn, 'cw'], [('ps', b)])
                bs.append(b)
            return bs

        attnT2 = [(attnT, atn), self.b16in32('gd_attnT1', 64, NT)]
        vtok2 = [(vtok, vtn), self.b16in32('gd_vtok1', 64, NT)]
        sg2 = [(sg16, sgn_), self.b16in32('gd_sg1', 64, NT)]
        sqA, sqAn = self.b16in32('gd_sqA', 64, NT)
        rsA, rsAn = self.f32('gd_rsA', 64, NT)

        def headstart(hh):
            par = hh % 2
            vtk, vtkn = vtok2[par]
            sgb, sgbn = sg2[par]
            for which, (tvw, tn_, c0) in enumerate(((tqv, tqn, hh * 64), (tqv, tqn, 256 + hh * 64), (tvv, tvn, hh * 64))):
                self.proj(tvw, tn_, c0, 64, R, rn)
                yield
                conv_silu(which * 4 + hh)
                yield
                if which < 2:
                    self.rms_fm(cx, cxn, 64, NT, sq, sqn, Yt, ytn, 1.0)
                    dst, dn = (qn16, qnn) if which == 0 else (kn16, knn)
                    self.stt('dve', dst, cx, 0.125 if which == 0 else 1.0, Yt, ALU.mult, ALU.mult, [cxn, ytn], [dn])
                else:
                    self.cp('act', v16, cx, [cxn], [v16n])
                yield
            self.proj(tvv, tvn, 256 + hh * 64, 64, R, rn)
            self.act(sgb, R, AF.Silu, [rn], [sgbn])
            yield
            for (src, srcn, dst, dn) in ((kn16, knn, ktok, ktn), (v16, v16n, vtk, vtkn)):
                b = self.bank()
                pb = self.ps[:, b, :].bitcast(BF16)
                for n in range(16):
                    self.tr(pb[0:64, n * 64:(n + 1) * 64], src[:, n * 64:(n + 1) * 64], self.id16[0:64, 0:64], [srcn, 'cb16'], [('ps', b)])
                self.cp('act', dst, pb[0:64, :], [('ps', b)], [dn])
                yield
            for (rhs16, rhsn, dst, dn) in ((kn16, knn, KKs, kksn), (qn16, qnn, QKs, qksn)):
                for half in range(2):
                    b = self.bank()
                    for nn in range(8):
                        n = half * 8 + nn
                        cs = slice(n * 64, (n + 1) * 64)
                        self.mm(self.ps[0:64, b, nn * 64:(nn + 1) * 64], kn16[:, cs], rhs16[:, cs], True, True, [knn, rhsn], [('ps', b)])
                    self.cp('act' if half else 'dve', dst[:, half * 512:(half + 1) * 512], self.ps[0:64, b, :], [('ps', b)], [(dn, half)])
                yield

        def prep_inv(hh, d):
            r = d * 4 + hh
            aT, aTn = attnT2[d]
            m_incl_t = m64[:, 0:64] if d == 0 else m64[:, 128:192]
            m_str_t = m64[:, 64:128] if d == 0 else m64[:, 192:256]
            m_str_2 = m64[:, 192:256] if d == 0 else m64[:, 64:128]
            gb = bcast_row(G, gn_, r)
            for blk in range(2):
                hs = slice(blk * 512, (blk + 1) * 512)
                self.tt('dve', u3(E3[:, hs]), u3(self.ps[0:64, gb[blk], :]), tokv[:, 0, blk * 8:(blk + 1) * 8, r].unsqueeze(2).to_broadcast([64, 8, 64]),
                        ALU.subtract, [('ps', gb[blk]), tokn], [(e3n, blk)])
            yield
            self.ts('pool', Tm, E3, 0.0, None, ALU.min, None, [e3n], [tmn])
            self.act(Tm, Tm, AF.Exp, [tmn], [tmn])
            yield
            self.tt('pool', u3(Yt), u3(Tm), mb3(m_incl_t), ALU.mult, [tmn, 'cw'], [ytn])
            self.tt('pool', aT, Yt, QKs, ALU.mult, [ytn, qksn], [aTn])
            yield
            self.tt('pool', u3(Yt), u3(Tm), mb3(m_str_t), ALU.mult, [tmn, 'cw'], [ytn])
            self.tt('pool', Yt, Yt, KKs, ALU.mult, [ytn, kksn], [ytn])
            self.stt('dve', u3(Yt), u3(Yt), -1.0, ub(tokv[:, 1, :, 8 + r]), ALU.mult, ALU.mult, [ytn, tokn], [ytn])
            yield
            self.ts('pool', Tm, E3, 0.0, None, ALU.max, None, [e3n], [tmn])
            self.act(Tm, Tm, AF.Exp, [tmn], [tmn], scale=-1.0)
            yield
            self.tt('pool', u3(Y), u3(Tm), mb3(m_str_2), ALU.mult, [tmn, 'cw'], [yn])
            self.tt('pool', Y, Y, KKs, ALU.mult, [yn, kksn], [yn])
            bb = bcast_row(BE, ben, 8 + r)
            for blk in range(2):
                hs = slice(blk * 512, (blk + 1) * 512)
                self.stt('dve', Y[:, hs], Y[:, hs], -1.0, self.ps[0:64, bb[blk], :], ALU.mult, ALU.mult, [yn, ('ps', bb[blk])], [(yn, blk)])
            self.tt('pool', u3(Pm), u3(Yt), mb3(ident[0:64, 0:64]), ALU.add, [ytn, 'cw'], [pmn])
            yield
            for lev in range(1, 6):
                ba = [self.bank(), self.bank()]
                if lev < 5:
                    for n in range(16):
                        cs = slice(n * 64, (n + 1) * 64)
                        self.mm(self.ps[0:64, ba[n // 8], (n % 8) * 64:(n % 8 + 1) * 64], Y[:, cs], Yt[:, cs], True, True, [yn, ytn], [('ps', ba[n // 8])])
                    yield
                bbk = [self.bank(), self.bank()]
                for n in range(16):
                    cs = slice(n * 64, (n + 1) * 64)
                    self.mm(self.ps[0:64, bbk[n // 8], (n % 8) * 64:(n % 8 + 1) * 64], Yt[:, cs], Y[:, cs], True, True, [yn, ytn], [('ps', bbk[n // 8])])
                yield
                for half in range(2):
                    hs = slice(half * 512, (half + 1) * 512)
                    if lev < 5:
                        self.cp('act', Yt[:, hs], self.ps[0:64, ba[half], :], [('ps', ba[half])], [(ytn, half)])
                    self.cp('act', Y[:, hs], self.ps[0:64, bbk[half], :], [('ps', bbk[half])], [(yn, half)])
                yield
                bp = [self.bank(), self.bank()]
                for n in range(16):
                    cs = slice(n * 64, (n + 1) * 64)
                    self.mm(self.ps[0:64, bp[n // 8], (n % 8) * 64:(n % 8 + 1) * 64], Y[:, cs], Pm[:, cs], True, True, [yn, pmn], [('ps', bp[n // 8])])
                yield
                for half in range(2):
                    hs = slice(half * 512, (half + 1) * 512)
                    self.tt('dve', Pm[:, hs], Pm[:, hs], self.ps[0:64, bp[half], :], ALU.add, [(pmn, half), ('ps', bp[half])], [(pmn, half)])
                yield

        def finish(hh, d):
            r = d * 4 + hh
            self.cp('act', T16, Pm, [pmn], [t16n])
            self.tt('dve', u3(kg), u3(ktok), ub(tokv[:, 2, :, r]), ALU.mult, [ktn, tokn], [kgn])
            self.tt('dve', u3(kout), u3(ktok), ub(tokv[:, 3, :, r]), ALU.mult, [ktn, tokn], [kon])
            xb = bcast_row(X1, x1n, r)
            for blk in range(2):
                hs = slice(blk * 512, (blk + 1) * 512)
                self.tt('dve', qin[:, hs], qn16[:, hs], self.ps[0:64, xb[blk], :], ALU.mult, [qnn, ('ps', xb[blk])], [(qinn, blk)])
            for half in range(2):
                b = self.bank()
                for nn in range(8):
                    n = half * 8 + nn
                    cs = slice(n * 64, (n + 1) * 64)
                    self.mm(self.ps[0:64, b, nn * 64:(nn + 1) * 64], kg[:, cs], T16[:, cs], True, True, [kgn, t16n], [('ps', b)])
                self.ts('dve', nw0T[:, half * 512:(half + 1) * 512], self.ps[0:64, b, :], -1.0, None, ALU.mult, None, [('ps', b)], [(nw0n, half)])
            yield

        def chain(hh, d):
            r = d * 4 + hh
            vtk, vtkn = vtok2[hh % 2]
            self.memset('dve', S, 0.0, [sn])
            order = range(16) if d == 0 else range(15, -1, -1)
            for n in order:
                seg = n // 4
                first = (n % 4 == 0) if d == 0 else (n % 4 == 3)
                last = (n % 4 == 3) if d == 0 else (n % 4 == 0)
                cs = slice(n * 64, (n + 1) * 64)
                if first:
                    self.stt('dve', S, S, self.c('carry')[0:64, :], stv[:, seg, d, hh, :], ALU.mult, ALU.add, [sn, 'cw', (stn, seg, d, hh)], [sn])
                self.cp('dve', Sall[:, cs], S, [sn], [(saln, n)])
                b = self.bank()
                self.mm(self.ps[0:64, b, 0:64], T16[:, cs], vtk[:, cs], True, False, [t16n, vtkn], [('ps', b)])
                self.mm(self.ps[0:64, b, 0:64], nw0T[:, cs], Sall[:, cs], False, True, [(nw0n, n // 8), (saln, n)], [('ps', b)])
                self.ts('dve', vnew[:, cs], self.ps[0:64, b, 0:64], tokv[:, 1, n, 8 + r:9 + r], None, ALU.mult, None, [('ps', b), tokn], [(vnn, n)])
                yield
                b2 = self.bank()
                self.mm(self.ps[0:64, b2, 0:64], kout[:, cs], vnew[:, cs], True, True, [kon, (vnn, n)], [('ps', b2)])
                self.stt('dve', S, S, tokv[:, 4, n, r:r + 1], self.ps[0:64, b2, 0:64], ALU.mult, ALU.add, [sn, tokn, ('ps', b2)], [sn])
                if last:
                    self.cp('dve', stv[:, seg, d, hh, :], S, [sn], [(stn, seg, d, hh)])
                yield

        def outp(hh, d):
            aT, aTn = attnT2[d]
            for half in range(2):
                b = self.bank()
                for nn in range(8):
                    n = half * 8 + nn
                    cs = slice(n * 64, (n + 1) * 64)
                    self.mm(self.ps[0:64, b, nn * 64:(nn + 1) * 64], Sall[:, cs], qin[:, cs], True, False, [(saln, n), qinn], [('ps', b)])
                    self.mm(self.ps[0:64, b, nn * 64:(nn + 1) * 64], vnew[:, cs], aT[:, cs], False, True, [(vnn, n), aTn], [('ps', b)])
                osl = og[:, half * 512:(half + 1) * 512]
                if d == 0:
                    self.cp('act', osl, self.ps[0:64, b, :], [('ps', b)], [(ogn, half)])
                else:
                    self.tt('dve', osl, osl, self.ps[0:64, b, :], ALU.add, [(ogn, half), ('ps', b)], [(ogn, half)])
                yield

        def normg(hh):
            sgb, sgbn = sg2[hh % 2]
            self.rms_fm(og, ogn, 64, NT, sqA, sqAn, rsA, rsAn, 1.0 / 64)
            self.stt('dve', og, og, self.w('gd_gn'), rsA, ALU.mult, ALU.mult, [ogn, 'lw', rsAn], [ogn])
            po = (hh % 2) * 64
            self.tt('dve', self.omix[po:po + 64, 6 + hh // 2, :], og, sgb, ALU.mult, [ogn, sgbn], ['omix'])
            yield

        def seq(*gens):
            for g in gens:
                yield from g

        def run(g):
            for _ in g:
                pass

        def interleave(a, b, ra=1, rb=1):
            alive = [a, b]
            rates = {id(a): ra, id(b): rb}
            while alive:
                for g in list(alive):
                    for _ in range(rates[id(g)]):
                        try:
                            next(g)
                        except StopIteration:
                            alive.remove(g)
                            break

        run(headstart(0))
        run(prep_inv(0, 0))
        run(finish(0, 0))
        for hh in range(4):
            interleave(seq(chain(hh, 0), outp(hh, 0)), prep_inv(hh, 1), 1, 1)
            run(finish(hh, 1))
            if hh < 3:
                interleave(seq(chain(hh, 1), outp(hh, 1), normg(hh)), seq(headstart(hh + 1), prep_inv(hh + 1, 0)), 1, 1)
                run(finish(hh + 1, 0))
            else:
                run(seq(chain(hh, 1), outp(hh, 1), normg(hh)))
        P.dma('sp', lambda e: e.dma_start(out=self.gdo_d[l], in_=st), 'gdo', reads=[stn])

    def rms_fm(self, x, xn, parts, n, sq, sqn, rstd, rn, scale, ones=None):
        self.act(sq[0:parts, 0:n], x, AF.Square, [xn], [sqn])
        c0 = 0
        while c0 < n:
            w = min(512, n - c0)
            b = self.bank()
            self.mm(self.ps[0:parts, b, 0:w], (self.ones16 if ones is None else ones)[0:parts, 0:parts], sq[0:parts, c0:c0 + w], True, True, [sqn, 'cb16'], [('ps', b)])
            self.rsqrt(rstd[0:parts, c0:c0 + w], self.ps[0:parts, b, 0:w], scale, self.epsc[0:parts, :], [('ps', b), 'misc'], [(rn, c0)])
            c0 += w

    def mla(self, l):
        P = self.P
        self.arena_reset(self.keep16)
        self.ada_tmp = self.f32('ada_tmp', 128, 1024)
        wt, wn = self.abmla
        wv = wt.rearrange("p (k c) -> p k c", c=512)
        cq, cqn = self.f32('m_cq', 128, 2 * NT)
        ckv, ckvn = self.f32('m_ckv', 128, NT)
        kr, krn = self.f32('m_kr', 32, NT)
        cosq, cosn = self.f32('m_cos', 96, NT)
        sinq, sinn = self.f32('m_sin', 96, NT)
        rstd, rsn = self.f32('m_rstd', 128, 1280)
        qh, qhn = self.f32('m_qh', 96, NT)
        tmp, tmn = self.f32('m_tmp', 96, NT)
        kh, khn = self.f32('m_kh', 96, 1280)
        sq, sqn = self.b16('m_sq', 128, 1280)
        cqn16, cq16n = self.b16('m_cqn', 128, 2 * NT)
        kvin, kvn = self.b16('m_kvin', 128, 1280)
        kr16, kr16n = self.b16('m_kr16', 32, 1280)
        vtok, vtn = self.b16('m_vtok', 128, 2560)
        vtv = vtok.rearrange("p (k c) -> p k c", c=256)
        wq16, wqn = self.b16('m_wq', 128, 768)
        wk16, wkn = self.b16('m_wk', 128, 384)
        wv16, wvn = self.b16('m_wv', 128, 256)
        rm16, rmn = self.b16('m_rm', 96, 96)
        e16, e16n = self.b16('m_e32', 32, 96)
        xn16, xn16n = self.b16('m_xn16', 96, 1280)
        qrot, qrn = self.b16('m_qrot', 104, NT)
        krot, krotn = self.b16('m_krot', 104, 1280)
        vext, vxn = self.b16('m_vext', 128, 1280)
        vxv = vext.rearrange("p (k c) -> p k c", c=128)
        pT, ptn = self.b16('m_pT', 128, 1024)
        self.cbig('cosq', cosq, cosn)
        self.cbig('sinq', sinq, sinn)
        o, p_, c_ = CB.items['ek']
        P.dma('pool', lambda e: e.dma_start(out=krot[96:104, :], in_=self.cb_d[0:8, o:o + 1280]), 'cb2', writes=[(krotn, 'm')])
        o2 = CB.items['eq'][0]
        P.dma('pool', lambda e: e.dma_start(out=qrot[96:104, :], in_=self.cb_d[0:8, o2:o2 + 1024]), 'cb2', writes=[(qrn, 'm')])
        self.memset('pool', vext, 1.0, [vxn])
        P.dma('pool', lambda e: e.dma_start(out=kvin[:, 1024:1280], in_=self.ctxkv_d[l]), 'cb2', writes=[(kvn, 1)])
        P.dma('pool', lambda e: e.dma_start(out=kr16[:, 1024:1280], in_=self.ctxkr_d[l]), 'cb2', writes=[(kr16n, 1)])
        for (dst, dn, src) in ((wq16, wqn, 'm_wq'), (wk16, wkn, 'm_wk'), (wv16, wvn, 'm_wv')):
            self.cp('dve', dst, self.w(src), ['lw'], [dn])
        self.cp('dve', rm16, self.c('rm'), ['cw'], [rmn])
        self.cp('dve', e16, self.c('e32'), ['cw'], [e16n])
        self.proj(wv, wn, 16, 128, cq[:, 0:NT], (cqn, 0))
        self.proj(wv, wn, 144, 128, cq[:, NT:2 * NT], (cqn, 1))
        self.proj(wv, wn, 272, 128, ckv, ckvn)
        self.proj(wv, wn, 400, 32, kr, krn)
        P.dma('sp', lambda e: e.dma_start(out=self.krT_d[l], in_=kr), 'kro', reads=[krn])
        self.cp('act', kr16[:, 0:NT], kr, [krn], [(kr16n, 0)])
        yield
        banks = [self.bank(), self.bank()]
        for c in range(2):
            self.act(sq[:, 0:NT], cq[:, c * NT:(c + 1) * NT], AF.Square, [(cqn, c)], [sqn])
            for blk in range(2):
                self.mm(self.ps[:, banks[blk], :], self.ones16, sq[:, blk * 512:(blk + 1) * 512], c == 0, c == 1, [sqn, 'cb16'], [('ps', banks[blk])])
        for blk in range(2):
            self.rsqrt(rstd[:, blk * 512:(blk + 1) * 512], self.ps[:, banks[blk], :], 1.0 / 256, self.epsc, [('ps', banks[blk]), 'misc'], [(rsn, blk)])
        for c in range(2):
            self.stt('dve', cqn16[:, c * NT:(c + 1) * NT], cq[:, c * NT:(c + 1) * NT], self.w('m_qa')[:, c:c + 1], rstd[:, 0:NT], ALU.mult, ALU.mult,
                     [(cqn, c), 'lw', rsn], [(cq16n, c)])
        self.rms_fm(ckv, ckvn, 128, NT, sq, sqn, rstd, rsn, 1.0 / 128)
        self.stt('dve', ckv, ckv, self.w('m_kva'), rstd[:, 0:NT], ALU.mult, ALU.mult, [ckvn, 'lw', rsn], [ckvn])
        P.dma('sp', lambda e: e.dma_start(out=self.ckvT_d[l], in_=ckv), 'ckvo', reads=[ckvn])
        self.cp('act', kvin[:, 0:NT], ckv, [ckvn], [(kvn, 0)])
        for kc in range(10):
            b = self.bank()
            self.mm(self.ps[:, b, 0:256], kvin[:, kc * 128:(kc + 1) * 128], wv16, True, True, [kvn, wvn], [('ps', b)])
            self.cp('act' if kc % 2 else 'dve', vtv[:, kc, :], self.ps[:, b, 0:256], [('ps', b)], [(vtn, kc)])
        yield
        self.nbank = 7
        scale = 96.0 ** -0.5
        qrot1, qr1n = self.b16in32('m_qrot1', 104, NT)
        krot1, kr1n = self.b16in32('m_krot1', 104, 1280)
        vext1, vx1n = self.b16in32('m_vext1', 128, 1280)
        rec, recn = self.f32('m_rec', 64, 512)
        P.dma('pool', lambda e: e.dma_start(out=krot1[96:104, :], in_=self.cb_d[0:8, o:o + 1280]), 'cb2', writes=[(kr1n, 'm')])
        P.dma('pool', lambda e: e.dma_start(out=qrot1[96:104, :], in_=self.cb_d[0:8, o2:o2 + 1024]), 'cb2', writes=[(qr1n, 'm')])
        self.memset('pool', vext1, 1.0, [vx1n])
        qrot2 = [(qrot, qrn), (qrot1, qr1n)]
        krot2 = [(krot, krotn), (krot1, kr1n)]
        vext2 = [(vext, vxn), (vext1, vx1n)]

        def prep(hh):
            qro, qron = qrot2[hh % 2]
            kro, kron = krot2[hh % 2]
            vx, vxn_ = vext2[hh % 2]
            vxv_ = vx.rearrange("p (k c) -> p k c", c=128)
            self.cp('pool', vxv_[:, :, 0:64], vtv[:, :, hh * 64:(hh + 1) * 64], [vtn], [vxn_])
            for blk in range(2):
                b = self.bank()
                for c in range(2):
                    self.mm(self.ps[0:96, b, :], wq16[:, c * 384 + hh * 96:c * 384 + (hh + 1) * 96], cqn16[:, c * NT + blk * 512:c * NT + (blk + 1) * 512],
                            c == 0, c == 1, [wqn, cq16n], [('ps', b)])
                self.cp('act', qh[:, blk * 512:(blk + 1) * 512], self.ps[0:96, b, :], [('ps', b)], [(qhn, blk)])
            yield
            self.rms_fm(qh, qhn, 96, NT, sq, sqn, rstd, rsn, 1.0 / 96)
            self.stt('dve', qh, qh, self.w('m_gq'), rstd[0:96, 0:NT], ALU.mult, ALU.mult, [qhn, 'lw', rsn], [qhn])
            self.cp('act', xn16[:, 0:NT], qh, [qhn], [xn16n])
            yield
            for blk in range(2):
                b = self.bank()
                sl = slice(blk * 512, (blk + 1) * 512)
                self.mm(self.ps[0:96, b, :], rm16, xn16[:, sl], True, True, [rmn, xn16n], [('ps', b)])
                self.tt('dve', tmp[:, sl], self.ps[0:96, b, :], sinq[:, sl], ALU.mult, [('ps', b), sinn], [(tmn, blk)])
                self.tt('pool', qh[:, sl], qh[:, sl], cosq[:, sl], ALU.mult, [qhn, cosn], [(qhn, blk)])
                self.tt('dve', qro[0:96, sl], tmp[:, sl], qh[:, sl], ALU.add, [(tmn, blk), (qhn, blk)], [(qron, blk)])
            yield
            for (c0, w) in ((0, 512), (512, 512), (1024, 256)):
                b = self.bank()
                self.mm(self.ps[0:96, b, 0:w], wk16[:, hh * 96:(hh + 1) * 96], kvin[:, c0:c0 + w], True, False, [wkn, kvn], [('ps', b)])
                self.mm(self.ps[0:96, b, 0:w], e16, kr16[:, c0:c0 + w], False, True, [e16n, kr16n], [('ps', b)])
                self.cp('act', kh[:, c0:c0 + w], self.ps[0:96, b, 0:w], [('ps', b)], [(khn, c0)])
            yield
            self.rms_fm(kh, khn, 96, 1280, sq, sqn, rstd, rsn, 1.0 / 96)
            self.stt('dve', kh, kh, self.w('m_gk'), rstd[0:96, 0:1280], ALU.mult, ALU.mult, [khn, 'lw', rsn], [khn])
            self.cp('act', xn16, kh, [khn], [xn16n])
            yield
            self.cp('dve', kro[0:96, 1024:1280], kh[:, 1024:1280], [khn], [(kron, 2)])
            for blk in range(2):
                b = self.bank()
                sl = slice(blk * 512, (blk + 1) * 512)
                self.mm(self.ps[0:96, b, :], rm16, xn16[:, sl], True, True, [rmn, xn16n], [('ps', b)])
                self.tt('dve', tmp[:, sl], self.ps[0:96, b, :], sinq[:, sl], ALU.mult, [('ps', b), sinn], [(tmn, blk)])
                self.tt('pool', kh[:, sl], kh[:, sl], cosq[:, sl], ALU.mult, [khn, cosn], [(khn, blk)])
                self.tt('dve', kro[0:96, sl], tmp[:, sl], kh[:, sl], ALU.add, [(tmn, blk), (khn, blk)], [(kron, blk)])
            yield

        def attn(hh):
            qro, qron = qrot2[hh % 2]
            kro, kron = krot2[hh % 2]
            vx, vxn_ = vext2[hh % 2]
            vxv_ = vx.rearrange("p (k c) -> p k c", c=128)
            for qb in range(2):
                qs = slice(qb * 512, (qb + 1) * 512)

                def score(kc):
                    b = self.bank()
                    ks = slice(kc * 128, (kc + 1) * 128)
                    self.mm(self.ps[:, b, :], kro[:, ks], qro[:, qs], True, True, [kron, qron], [('ps', b)])
                    return b
                bnext = score(0)
                for kc in range(10):
                    b = bnext
                    pt = pT[:, (kc % 2) * 512:(kc % 2 + 1) * 512]
                    self.act(pt, self.ps[:, b, :], AF.Exp, [('ps', b)], [(ptn, kc % 2)], scale=scale)
                    if kc < 9:
                        bnext = score(kc + 1)
                    self.mm(self.ps[:, 7, :], vxv_[:, kc, :], pt, kc == 0, kc == 9, [vxn_, (ptn, kc % 2)], [('ps', 7)])
                    if kc % 2:
                        yield
                self.recip(rec, self.ps[64:128, 7, :], [('ps', 7)], [recn])
                po = (hh % 2) * 64
                self.tt('dve', self.omix[po:po + 64, 4 + hh // 2, qs], self.ps[0:64, 7, :], rec, ALU.mult, [('ps', 7), recn], ['omix'])
                yield

        def inter(a_, b_):
            alive = [a_, b_]
            while alive:
                for gg in list(alive):
                    try:
                        next(gg)
                        yield
                    except StopIteration:
                        alive.remove(gg)

        yield from prep(0)
        for hh in range(4):
            if hh < 3:
                yield from inter(attn(hh), prep(hh + 1))
            else:
                yield from attn(hh)
        self.nbank = 8

PROMPT_ASSIGN = [[0, 1, 2], [3, 4, 5], [6, 7, 8], [9, 10, 11], [12, 13], [14, 15]]


def core_plan():
    plan = [(True, [0] * 4), (True, [1] * 4)]
    for seqs in PROMPT_ASSIGN:
        s4 = list(seqs) + [seqs[0]] * (4 - len(seqs))
        plan.append((False, s4))
    return plan


def make_in_maps(inp, depth=DEPTH):
    inp = {k: np.asarray(v) for k, v in inp.items()}
    wstream = pack_wstream(inp)
    wada = np.ascontiguousarray(inp['w_ada'].reshape(DEPTH, 8, 128, 18, 512).transpose(0, 3, 2, 1, 4).reshape(DEPTH, 18, 128, 4096))
    lw = np.stack([pack_layer_small(inp, l) for l in range(DEPTH)])
    import ml_dtypes
    dft_s, dft_p = dft_tiles(True).astype(ml_dtypes.bfloat16), dft_tiles(False).astype(ml_dtypes.bfloat16)
    maps = []
    for ci, (is_s, segs) in enumerate(core_plan()):
        m = {}
        if is_s:
            b = segs[0]
            x = inp['x_sample'][b]
            cond = inp['c'][b]
            ckv = inp['cache_mla_ckv'][b].transpose(0, 2, 1)
            ckr = inp['cache_mla_krope'][b].transpose(0, 2, 1)
            hg = np.zeros((DEPTH, 64, 4, 2, 4, 64), np.float32)
            gd = np.zeros((DEPTH, 64, 4, 2, 4, 64), np.float32)
            hg[:, :, 0, 0] = inp['state_hgrn'][b][:, 0].transpose(0, 2, 1, 3)
            hg[:, :, 3, 1] = inp['state_hgrn'][b][:, 1].transpose(0, 2, 1, 3)
            gd[:, :, 0, 0] = inp['state_gdn'][b][:, 0].transpose(0, 2, 1, 3)
            gd[:, :, 3, 1] = inp['state_gdn'][b][:, 1].transpose(0, 2, 1, 3)
        else:
            x = np.concatenate([inp['x_prompt'][s] for s in segs], axis=0)
            cond = inp['c_ctx']
            ckv = np.zeros((DEPTH, 128, 256), np.float32)
            ckr = np.zeros((DEPTH, 32, 256), np.float32)
            hg = np.zeros((DEPTH, 64, 4, 2, 4, 64), np.float32)
            gd = np.zeros((DEPTH, 64, 4, 2, 4, 64), np.float32)
        m['xT'] = np.ascontiguousarray(x.T.reshape(8, 128, NT))
        m['cw'], m['cb'] = core_consts(is_s, cond)
        m['lw'] = lw
        m['wstream'] = wstream
        m['wada'] = wada
        m['dft'] = dft_s if is_s else dft_p
        m['ctxkv'] = np.ascontiguousarray(ckv)
        m['ctxkr'] = np.ascontiguousarray(ckr)
        m['hgst'] = np.ascontiguousarray(hg.reshape(DEPTH, 64, 2048))
        m['gdst'] = np.ascontiguousarray(gd.reshape(DEPTH, 64, 2048))
        maps.append(m)
    return maps


def assemble(results):
    y_prompt = np.zeros((16, 256, D), np.float32)
    y_sample = np.zeros((2, 1024, D), np.float32)
    n_ckv = np.zeros((16, DEPTH, 256, 128), np.float32)
    n_kr = np.zeros((16, DEPTH, 256, 32), np.float32)
    n_hg = np.zeros((16, DEPTH, 2, 4, 64, 64), np.float32)
    n_gd = np.zeros((16, DEPTH, 2, 4, 64, 64), np.float32)
    for ci, (is_s, segs) in enumerate(core_plan()):
        r = results[ci]
        y = r['yT'].reshape(D, NT).T
        if is_s:
            y_sample[segs[0]] = y
            continue
        hgo = r['hgo'].reshape(DEPTH, 64, 4, 2, 4, 64)
        gdo = r['gdo'].reshape(DEPTH, 64, 4, 2, 4, 64)
        for g, s in enumerate(segs):
            if g > 0 and s == segs[0]:
                continue
            y_prompt[s] = y[g * 256:(g + 1) * 256]
            n_ckv[s] = r['ckvT'][:, :, g * 256:(g + 1) * 256].transpose(0, 2, 1)
            n_kr[s] = r['krT'][:, :, g * 256:(g + 1) * 256].transpose(0, 2, 1)
            n_hg[s] = hgo[:, :, g].transpose(0, 2, 3, 1, 4)
            n_gd[s] = gdo[:, :, g].transpose(0, 2, 3, 1, 4)
    return (y_prompt, y_sample, n_ckv, n_kr, n_hg, n_gd)


def build_nc(depth=DEPTH, stages=None, dbg=None):
    nc = bass.Bass("TRN2", target_bir_lowering=False)
    with contextlib.ExitStack() as st:
        b = B(nc, st, depth, stages, dbg)
        b.build()
    return nc


def kernel(**inputs):
    maps = make_in_maps(inputs)
    nc = build_nc()
    res = run_bass_kernel_spmd(nc, maps, core_ids=list(range(NCORES)))
    return assemble(res.results)
```

```python
import contextlib
import math
import numpy as np
import concourse.bass as bass
import concourse.mybir as mybir
from concourse.bass_utils import run_bass_kernel_spmd

F32 = mybir.dt.float32
BF16 = mybir.dt.bfloat16
AF = mybir.ActivationFunctionType
ALU = mybir.AluOpType

D = 1024
NT = 1024
DEPTH = 4
DFF = 2816
NF = 22
EPS = 1e-6
NCORES = 8
TILE_COLS = 4096
NWSLOT = 3
TILES_PER_LAYER = 48
ADA_Q = 1152
MAGIC = 12582912.0


class Prog:
    ENG = ('pe', 'act', 'dve', 'pool', 'sp')

    def __init__(self, nc):
        self.nc = nc
        self.ops = {e: [] for e in self.ENG}
        self.count = {e: 0 for e in self.ENG}
        self.waited = {e: {} for e in self.ENG}
        self.res = {}
        self.dsem_count = {}
        self.sem_names = []

    @staticmethod
    def _norm(rs):
        return [(r,) if isinstance(r, str) else tuple(r) for r in rs]

    def _conf(self, r):
        d = self.res.setdefault(r[0], {})
        out = []
        for k, v in d.items():
            n = min(len(k), len(r))
            if k[:n] == r[:n]:
                out.append(v)
        return d, out

    def _deps(self, reads, writes):
        toks = []
        for r in reads:
            _, cs = self._conf(r)
            for v in cs:
                if v[0] is not None:
                    toks.append(v[0])
        for w in writes:
            _, cs = self._conf(w)
            for v in cs:
                if v[0] is not None:
                    toks.append(v[0])
                toks.extend(v[1])
        return toks

    def _commit(self, reads, writes, tok):
        for r in reads:
            d, _ = self._conf(r)
            d.setdefault(r, [None, []])[1].append(tok)
        for w in writes:
            d, _ = self._conf(w)
            for k in list(d.keys()):
                if len(k) >= len(w) and k[:len(w)] == w:
                    del d[k]
            d[w] = [tok, []]

    def _waits(self, eng, toks, sync_self=True):
        best = {}
        for (sk, val) in toks:
            if sk == eng and not sync_self:
                continue
            if val > best.get(sk, 0):
                best[sk] = val
        out = []
        for sk, val in best.items():
            if self.waited[eng].get(sk, 0) >= val:
                continue
            self.waited[eng][sk] = val
            out.append((sk, val))
        return out

    def op(self, eng, fn, reads=(), writes=(), sync_self=True):
        reads, writes = self._norm(reads), self._norm(writes)
        waits = self._waits(eng, self._deps(reads, writes), sync_self)
        self.count[eng] += 1
        self.ops[eng].append((waits, fn, (eng, 1)))
        self._commit(reads, writes, (eng, self.count[eng]))
        if eng not in self.sem_names:
            self.sem_names.append(eng)

    def dma(self, eng, fn, slot, reads=(), writes=()):
        reads, writes = self._norm(reads), self._norm(writes)
        toks = self._deps(reads, writes)
        sk = 'd_' + slot
        prev = self.dsem_count.get(sk, 0)
        if prev:
            toks.append((sk, prev))
        waits = self._waits(eng, toks, True)
        self.dsem_count[sk] = prev + 16
        self.ops[eng].append((waits, fn, (sk, 16)))
        self._commit(reads, writes, (sk, prev + 16))
        if sk not in self.sem_names:
            self.sem_names.append(sk)

    def barrier(self):
        toks = [(e, c) for e, c in self.count.items() if c]
        toks += [(k, v) for k, v in self.dsem_count.items() if not k.startswith('d_ws')]
        for eng in self.ENG:
            waits = self._waits(eng, toks, True)
            if waits:
                self.ops[eng].append((waits, None, None))
        self.res = {k: v for k, v in self.res.items() if k == 'ws'}

    def emit(self, stack):
        nc = self.nc
        sems = {n: stack.enter_context(nc.semaphore('s_' + n)) for n in self.sem_names}
        block = stack.enter_context(nc.Block())
        engobj = {'pe': 'tensor', 'act': 'scalar', 'dve': 'vector', 'pool': 'gpsimd', 'sp': 'sync'}

        def mk(ename):
            def body(e):
                for waits, fn, inc in self.ops[ename]:
                    for sk, val in waits:
                        e.wait_ge(sems[sk], val)
                    if fn is not None:
                        fn(e).then_inc(sems[inc[0]], inc[1])
            return body

        for ename in self.ENG:
            if self.ops[ename]:
                getattr(block, engobj[ename])(mk(ename))


class Packer:
    def __init__(self):
        self.items = {}
        self.ncols = 0

    def add(self, name, parts, cols):
        self.items[name] = (self.ncols, parts, cols)
        self.ncols += cols

    def pack(self, vals):
        out = np.zeros((128, self.ncols), np.float32)
        for name, (o, p, c) in self.items.items():
            a = np.asarray(vals[name], np.float32).reshape(p, c)
            out[:p, o:o + c] = a
        return out


def fm(v, parts=128):
    v = np.asarray(v, np.float32)
    return v.reshape(-1, parts).T


LW = Packer()
for _n, _p, _c in [
    ('bada', 128, 72), ('g_ffn0', 128, 8), ('g_mix', 128, 8), ('g_ffn1', 128, 8),
    ('hy_cw', 128, 18), ('hy_cb', 128, 6), ('hy_w1', 33, 64), ('hy_b1', 64, 1), ('hy_fr', 64, 1),
    ('hy_w2', 64, 64), ('hy_b2', 64, 1), ('hy_w3', 64, 512), ('hy_skip', 128, 2),
    ('m_qa', 128, 2), ('m_wq', 128, 768), ('m_kva', 128, 1), ('m_wk', 128, 384), ('m_wv', 128, 256),
    ('m_gq', 96, 1), ('m_gk', 96, 1),
    ('hg_lbraw', 64, 32), ('hg_gn', 64, 4),
    ('gd_cw', 64, 36), ('gd_alog', 16, 1), ('gd_dtb', 16, 1), ('gd_gn', 64, 1),
]:
    LW.add(_n, _p, _c)

CW = Packer()
for _n, _p, _c in [
    ('cond', 128, 8), ('carry', 128, 1), ('ncarry', 128, 1),
    ('ident', 128, 128), ('ones', 128, 128), ('rm', 96, 96), ('e32', 32, 96),
    ('m32', 32, 64), ('m64', 64, 256), ('mf', 16, 1), ('mb', 16, 1),
]:
    CW.add(_n, _p, _c)
CB = Packer()
for _n, _p, _c in [
    ('cosq', 96, 1024), ('sinq', 96, 1024), ('ek', 8, 1280), ('eq', 8, 1024),
    ('zemb', 33, 1024), ('winf', 128, 2048), ('winb', 128, 2048), ('sel', 32, 16 * 64),
]:
    CB.add(_n, _p, _c)


def w_in_colmap():
    HG, HY, MLA = 1280, 768, 416
    hg0, hy0, ml0, gd0 = 0, HG, HG + HY, HG + HY + MLA
    tiles = []
    def rng(a, n):
        return list(range(a, a + n))
    tiles.append(rng(hg0, 512))
    tiles.append(rng(hg0 + 512, 512))
    tiles.append(rng(hg0 + 1024, 256))
    tiles.append(rng(hy0, 512))
    tiles.append(rng(hy0 + 512, 256))
    tiles.append(rng(gd0, 512))
    tiles.append(rng(gd0 + 512, 512))
    tiles.append(rng(gd0 + 1024, 16) + rng(ml0, 416))
    return tiles


def pack_wstream(inp):
    out = np.zeros((DEPTH * TILES_PER_LAYER, 128, TILE_COLS), np.float32)
    cm = w_in_colmap()
    t = 0
    for l in range(DEPTH):
        def ffn(i):
            nonlocal t
            wgu = inp['w_ffn_gu'][l, i].reshape(8, 128, 2 * DFF)
            for g in range(11):
                v = out[t].reshape(128, 8, 512)
                v[:, :, 0:256] = wgu[:, :, g * 256:(g + 1) * 256].transpose(1, 0, 2)
                v[:, :, 256:512] = wgu[:, :, DFF + g * 256:DFF + (g + 1) * 256].transpose(1, 0, 2)
                t += 1
            wd = inp['w_ffn_down'][l, i].reshape(NF, 128, 8, 128)
            for dc in range(8):
                out[t][:, :NF * 128] = wd[:, :, dc, :].transpose(1, 0, 2).reshape(128, NF * 128)
                t += 1
        ffn(0)
        win = inp['w_in'][l].reshape(8, 128, -1)
        for cols in cm:
            v = out[t].reshape(128, 8, 512)
            v[:, :, :len(cols)] = win[:, :, cols].transpose(1, 0, 2)
            t += 1
        wo = inp['w_out'][l].reshape(8, 128, 1024)
        for h in range(2):
            out[t].reshape(128, 8, 512)[:] = wo[:, :, h * 512:(h + 1) * 512].transpose(1, 0, 2)
            t += 1
        ffn(1)
    assert t == DEPTH * TILES_PER_LAYER
    return out


def pack_layer_small(inp, l):
    v = {}
    v['bada'] = fm(inp['b_ada'][l])
    v['g_ffn0'] = fm(inp['norm_ffn'][l, 0])
    v['g_mix'] = fm(inp['norm_mix'][l])
    v['g_ffn1'] = fm(inp['norm_ffn'][l, 1])
    cw = inp['hy_conv_w'][l]
    v['hy_cw'] = np.stack([fm(cw[k]) for k in range(3)], axis=2).reshape(128, 18)
    v['hy_cb'] = fm(inp['hy_conv_b'][l])
    v['hy_w1'] = inp['hy_w1'][l]
    v['hy_b1'] = inp['hy_b1'][l].reshape(64, 1)
    v['hy_fr'] = inp['hy_freq'][l].reshape(64, 1)
    v['hy_w2'] = inp['hy_w2'][l]
    v['hy_b2'] = inp['hy_b2'][l].reshape(64, 1)
    v['hy_w3'] = inp['hy_w3'][l]
    v['hy_skip'] = fm(inp['hy_skip'][l])
    v['m_qa'] = fm(inp['mla_q_norm_a'][l])
    v['m_wq'] = inp['mla_w_q_up'][l].reshape(2, 128, 384).transpose(1, 0, 2).reshape(128, 768)
    v['m_kva'] = inp['mla_kv_norm_a'][l].reshape(128, 1)
    wkv = inp['mla_w_kv_up'][l].reshape(128, 4, 128)
    wk = np.zeros((128, 4, 96), np.float32)
    wk[:, :, :64] = wkv[:, :, :64]
    v['m_wk'] = wk.reshape(128, 384)
    v['m_wv'] = wkv[:, :, 64:].reshape(128, 256)
    v['m_gq'] = inp['mla_qk_norm'][l, 0].reshape(96, 1)
    v['m_gk'] = inp['mla_qk_norm'][l, 1].reshape(96, 1)
    lb = inp['hgrn_lb'].reshape(4, 2, 4, 64)
    v['hg_lbraw'] = lb.transpose(3, 0, 1, 2).reshape(64, 32)
    v['hg_gn'] = inp['hgrn_norm'][l].reshape(4, 64).T
    gw = inp['gdn_conv_w'][l].reshape(3, 12, 64)
    v['gd_cw'] = gw.transpose(2, 1, 0).reshape(64, 36)
    al = np.zeros((16, 1), np.float32); al[:8, 0] = inp['gdn_a_log'][l].reshape(8)
    db = np.zeros((16, 1), np.float32); db[:8, 0] = inp['gdn_dt_bias'][l].reshape(8)
    v['gd_alog'] = al
    v['gd_dtb'] = db
    v['gd_gn'] = inp['gdn_norm'][l].reshape(64, 1)
    return LW.pack(v)


def rope_np(rows, gw=64):
    T = rows * gw
    row = np.repeat(np.arange(rows, dtype=np.float32), gw)
    col = (np.arange(T) % gw).astype(np.float32)
    pairs = 8
    inv = (10000.0 ** (-np.arange(pairs, dtype=np.float32) / pairs)).astype(np.float32)
    ang = np.concatenate([row[:, None] * inv, col[:, None] * inv], axis=-1)
    return np.cos(ang), np.sin(ang)


def core_consts(is_sample, cond):
    v = {}
    v['cond'] = fm(cond)
    v['carry'] = np.full((128, 1), 1.0 if is_sample else 0.0, np.float32)
    v['ncarry'] = np.full((128, 1), 0.0 if is_sample else -1.0, np.float32)
    v['ident'] = np.eye(128, dtype=np.float32)
    v['ones'] = np.ones((128, 128), np.float32)
    cosq = np.ones((96, 1024), np.float32); sinq = np.zeros((96, 1024), np.float32)
    if is_sample:
        c, s = rope_np(16)
        cosq[64:80] = c.T; cosq[80:96] = c.T; sinq[64:80] = s.T; sinq[80:96] = s.T
    v['cosq'], v['sinq'] = cosq, sinq
    ek = np.zeros((8, 1280), np.float32); eq = np.zeros((8, 1024), np.float32)
    for g in range(4):
        ek[g, g * 256:(g + 1) * 256] = 1.0
    ek[4, 1024:] = 1.0
    if not is_sample:
        eq[:5, :] = -30000.0
        for g in range(4):
            eq[g, g * 256:(g + 1) * 256] = 0.0
    v['ek'], v['eq'] = ek, eq
    rm = np.zeros((96, 96), np.float32)
    for i in range(16):
        rm[80 + i, 64 + i] = -1.0
        rm[64 + i, 80 + i] = 1.0
    v['rm'] = rm
    e32 = np.zeros((32, 96), np.float32)
    for i in range(32):
        e32[i, 64 + i] = 1.0
    v['e32'] = e32
    L = 1024 if is_sample else 256
    pos = np.arange(L, dtype=np.float32)
    tt = pos / np.float32(L - 1)
    bands = np.linspace(1e-4, 15, 16, dtype=np.float32)
    ang = (np.float32(2.0 * math.pi / L)) * pos[:, None] * bands[None, :]
    z = np.concatenate([tt[:, None], np.cos(ang), -np.sin(ang)], axis=-1).astype(np.float32)
    z = np.tile(z, (1024 // L, 1))
    v['zemb'] = z.T
    deltas = np.linspace(math.log(1e-2) / 1.5, math.log(1e-2) / 0.3, 256, dtype=np.float32)
    win = np.exp(-tt[:, None] * np.abs(deltas)[None, :]).astype(np.float32)
    winb = win.copy(); winb[0] = 0.0
    win = np.tile(win, (1024 // L, 1)); winb = np.tile(winb, (1024 // L, 1))
    v['winf'] = win.reshape(8, 128, 256).transpose(1, 0, 2).reshape(128, 2048)
    v['winb'] = winb.reshape(8, 128, 256).transpose(1, 0, 2).reshape(128, 2048)
    j32 = np.arange(32)[:, None]; i32 = np.arange(32)[None, :]
    v['m32'] = np.concatenate([(j32 <= i32), (j32 >= i32)], axis=1).astype(np.float32)
    j64 = np.arange(64)[:, None]; i64 = np.arange(64)[None, :]
    v['m64'] = np.concatenate([(j64 <= i64), (j64 < i64), (j64 >= i64), (j64 > i64)], axis=1).astype(np.float32)
    mf = np.zeros((16, 1), np.float32); mf[0:4] = 1.0
    mb = np.zeros((16, 1), np.float32); mb[4:8] = 1.0
    v['mf'], v['mb'] = mf, mb
    sel = np.zeros((32, 16, 64), np.float32)
    for r in range(16):
        sel[r, r, :] = 1.0
    v['sel'] = sel.reshape(32, 1024)
    return CW.pack({k: v[k] for k in CW.items}), CB.pack({k: v[k] for k in CB.items})


def dft_tiles(is_sample):
    L = 1024 if is_sample else 256
    t = np.arange(L, dtype=np.float64)[:, None]
    f = np.arange(L, dtype=np.float64)[None, :]
    ang = math.pi * (2 * f + 1) * t / (2 * L)
    C = np.cos(ang); S = np.sin(ang)
    nb = 1024 // L
    def bd(M):
        out = np.zeros((1024, 1024), np.float32)
        for b in range(nb):
            out[b * L:(b + 1) * L, b * L:(b + 1) * L] = M
        return out
    mats = [bd(C), bd(S), bd(C.T / L), bd(S.T / L)]
    tiles = np.zeros((8, 128, TILE_COLS), np.float32)
    k = 0
    for M in mats:
        Mr = M.reshape(8, 128, 1024)
        for h in range(2):
            tiles[k].reshape(128, 8, 512)[:] = Mr[:, :, h * 512:(h + 1) * 512].transpose(1, 0, 2)
            k += 1
    return tiles


class B:
    def __init__(self, nc, st, depth=DEPTH, stages=None, dbg=None):
        self.nc, self.st = nc, st
        self.depth = depth
        self.stages = stages or ('ffn0', 'hg', 'hy', 'gd', 'mla', 'ffn1')
        self.dbg = dbg or {}
        self.P = Prog(nc)
        self.bankctr = 0
        self.wtile = 0
        self.uid = 0

    def sb(self, name, shape, dt=F32):
        return self.st.enter_context(self.nc.sbuf_tensor(name, shape, dt))

    nbank = 8

    def bank(self):
        b = self.bankctr % self.nbank
        self.bankctr += 1
        return b

    def mm(self, out, lhsT, rhs, start, stop, r, w, sync_self=False):
        self.P.op('pe', lambda e: e.matmul(out, lhsT=lhsT, rhs=rhs, start=start, stop=stop), r, w, sync_self=sync_self)

    def tr(self, out, in_, ident, r, w):
        self.P.op('pe', lambda e: e.transpose(out, in_, ident), r, w, sync_self=False)

    def act(self, out, in_, func, r, w, bias=0.0, scale=1.0):
        self.P.op('act', lambda e: e.activation(out=out, in_=in_, func=func, bias=bias, scale=scale), r, w)

    def tt(self, eng, out, in0, in1, op, r, w):
        self.P.op(eng, lambda e: e.tensor_tensor(out=out, in0=in0, in1=in1, op=op), r, w)

    def ts(self, eng, out, in0, s1, s2, op0, op1, r, w):
        if s2 is None:
            self.P.op(eng, lambda e: e.tensor_scalar(out=out, in0=in0, scalar1=s1, scalar2=None, op0=op0), r, w)
        else:
            self.P.op(eng, lambda e: e.tensor_scalar(out=out, in0=in0, scalar1=s1, scalar2=s2, op0=op0, op1=op1), r, w)

    def stt(self, eng, out, in0, scalar, in1, op0, op1, r, w):
        self.P.op(eng, lambda e: e.scalar_tensor_tensor(out=out, in0=in0, scalar=scalar, in1=in1, op0=op0, op1=op1), r, w)

    def cp(self, eng, out, in_, r, w):
        if eng == 'act':
            self.P.op('act', lambda e: e.copy(out=out, in_=in_), r, w)
        else:
            self.P.op(eng, lambda e: e.tensor_copy(out=out, in_=in_), r, w)

    def recip(self, out, in_, r, w):
        self.P.op('dve', lambda e: e.reciprocal(out=out, in_=in_), r, w)

    def memset(self, eng, ap, val, w):
        self.P.op(eng, lambda e: e.memset(ap, val), [], w)

    def rsqrt(self, out, in_, scale, eps_ap, r, w):
        self.act(out, in_, AF.Ln, r + ['misc'], w, bias=eps_ap, scale=scale)
        self.act(out, out, AF.Exp, w, w, scale=-0.5)

    def arena_reset(self, keep16=0, keep32=0):
        self.P.barrier()
        self.a32 = keep32
        self.a16 = keep16
        self.uid += 1

    def f32(self, name, parts, cols):
        o = self.a32
        self.a32 += cols
        assert self.a32 <= self.A32, (name, self.a32)
        return self.ar32[0:parts, o:o + cols], f'a{self.uid}.{name}'

    def b16in32(self, name, parts, cols):
        o = self.a32
        self.a32 += cols // 2
        assert self.a32 <= self.A32, (name, self.a32)
        return self.ar32[0:parts, o:o + cols // 2].bitcast(BF16), f'a{self.uid}.{name}'

    def b16(self, name, parts, cols):
        o = self.a16
        self.a16 += cols
        assert self.a16 <= self.A16, (name, self.a16)
        return self.ar16[0:parts, o:o + cols], f'a{self.uid}.{name}'

    def prefetch(self, k):
        if not self.do_prefetch:
            return
        for _ in range(k):
            self.pref.append(self.next_wtile(_force=True))

    def next_wtile(self, src=None, f32=False, plain=False, _force=False):
        if src is None and self.pref and not _force:
            return self.pref.pop(0)
        if src is None:
            t = self.wtile
            self.wtile += 1
            src_ap = self.wstream[t]
        else:
            src_ap = src
        s = self.wslot_ctr % NWSLOT
        self.wslot_ctr += 1
        dst = self.wsl[:, s, :]
        rn = ('ws', s)
        if f32:
            dst = dst.bitcast(F32)
            self.P.dma('sp', lambda e: e.dma_start(out=dst, in_=src_ap), f'ws{s}', writes=[rn])
        elif plain:
            self.P.dma('sp', lambda e: e.dma_start(out=dst, in_=src_ap), f'ws{s}', writes=[rn])
        else:
            self.P.dma('pool', lambda e: e.dma_start(out=dst, in_=src_ap), f'ws{s}', writes=[rn])
        return dst, rn

    def build(self):
        nc, P = self.nc, self.P
        dr = lambda n, s, k: nc.dram_tensor(n, s, F32, kind=k).ap()
        self.xT_d = dr('xT', [8, 128, NT], 'ExternalInput')
        self.cw_d = dr('cw', [128, CW.ncols], 'ExternalInput')
        self.cb_d = dr('cb', [128, CB.ncols], 'ExternalInput')
        self.lw_d = dr('lw', [DEPTH, 128, LW.ncols], 'ExternalInput')
        self.wstream = dr('wstream', [DEPTH * TILES_PER_LAYER, 128, TILE_COLS], 'ExternalInput')
        self.wada_d = dr('wada', [DEPTH, 18, 128, TILE_COLS], 'ExternalInput')
        self.dft_d = nc.dram_tensor('dft', [8, 128, TILE_COLS], BF16, kind='ExternalInput').ap()
        self.ctxkv_d = dr('ctxkv', [DEPTH, 128, 256], 'ExternalInput')
        self.ctxkr_d = dr('ctxkr', [DEPTH, 32, 256], 'ExternalInput')
        self.hgst_d = dr('hgst', [DEPTH, 64, 2048], 'ExternalInput')
        self.gdst_d = dr('gdst', [DEPTH, 64, 2048], 'ExternalInput')
        self.yT_d = dr('yT', [8, 128, NT], 'ExternalOutput')
        self.ckvT_d = dr('ckvT', [DEPTH, 128, NT], 'ExternalOutput')
        self.krT_d = dr('krT', [DEPTH, 32, NT], 'ExternalOutput')
        self.hgo_d = dr('hgo', [DEPTH, 64, 2048], 'ExternalOutput')
        self.gdo_d = dr('gdo', [DEPTH, 64, 2048], 'ExternalOutput')
        self.dbg_d = {}
        for n, shp in self.dbg.items():
            self.dbg_d[n] = dr('dbg_' + n, list(shp), 'ExternalOutput')

        self.xT = self.sb('xT_sb', [128, 8, NT])
        self.h = self.sb('h_sb', [128, 8, NT], BF16)
        self.wsl = self.sb('wsl', [128, NWSLOT, TILE_COLS], BF16)
        self.cw = self.sb('cw_sb', [128, CW.ncols])
        self.lw = self.sb('lw_sb', [128, LW.ncols])
        self.A32, self.A16 = 18304, 24576
        self.ar32 = self.sb('ar32', [128, self.A32])
        self.ar16 = self.sb('ar16', [128, self.A16], BF16)
        self.misc = self.sb('misc', [128, 512])
        self.cb16 = self.sb('cb16', [128, 1024], BF16)
        self.ps = self.st.enter_context(nc.psum_tensor('ps', [128, 8, 512], F32))
        self.wslot_ctr = 0
        self.a32 = self.a16 = 0
        self.pref = []
        self.do_prefetch = (self.depth == DEPTH and all(x in self.stages for x in ('ffn0', 'hg', 'hy', 'gd', 'mla', 'ffn1')))

        self.ada = self.misc[:, 0:72]
        self.s2 = self.misc[:, 72:88].rearrange("p (k t) -> p k t", t=2)
        self.modA = self.misc[:, 96:120].rearrange("p (s k) -> p s k", k=8)
        self.modB = self.misc[:, 120:144].rearrange("p (s k) -> p s k", k=8)
        self.modG = self.misc[:, 144:168].rearrange("p (s k) -> p s k", k=8)
        self.epsc = self.misc[:, 168:169]
        self.eps6 = self.misc[:, 169:170]
        self.negpi = self.misc[:, 170:171]
        self.lball = self.misc[0:64, 176:208]
        self.adaraw = self.misc[:, 288:360]

        for kc in range(8):
            P.dma('sp', (lambda kc: lambda e: e.dma_start(out=self.xT[:, kc, :], in_=self.xT_d[kc]))(kc), 'init%d' % (kc % 2), writes=[('xT', kc)])
        P.dma('sp', lambda e: e.dma_start(out=self.cw[:], in_=self.cw_d), 'init', writes=['cw'])
        self.memset('dve', self.misc[:], 0.0, ['misc'])
        self.memset('dve', self.epsc, EPS, ['misc'])
        self.memset('dve', self.eps6, 1e-6, ['misc'])
        self.memset('dve', self.negpi, -math.pi, ['misc'])
        self.memset('dve', self.misc[:, 171:172], math.log(0.125), ['misc'])
        P.barrier()
        self.ones16 = self.cb16[:, 0:128]
        self.id16 = self.cb16[:, 128:256]
        self.cp('dve', self.ones16, self.c('ones'), ['cw'], ['cb16'])
        self.cp('dve', self.id16, self.c('ident'), ['cw'], ['cb16'])
        self.act(self.s2[:, :, 0], self.c('cond'), AF.Silu, ['cw'], ['misc'])
        self.cp('dve', self.s2[:, :, 1], self.s2[:, :, 0], ['misc'], ['misc'])
        self.s16 = self.cb16[:, 256:264]
        self.blk16 = self.cb16[:, 384:512]
        self.memset('dve', self.blk16, 0.0, ['cb16'])
        self.memset('dve', self.blk16[0:64, 0:64], 1.0, ['cb16'])
        self.memset('dve', self.blk16[64:128, 64:128], 1.0, ['cb16'])
        self.cp('dve', self.s16, self.s2[:, :, 0], ['misc'], ['cb16'])
        P.barrier()

        for l in range(self.depth):
            self.layer(l)

        P.barrier()
        for kc in range(8):
            P.dma('sp', (lambda kc: lambda e: e.dma_start(out=self.yT_d[kc], in_=self.xT[:, kc, :]))(kc), 'out%d' % (kc % 2), reads=[('xT', kc)])
        P.barrier()
        P.emit(self.st)

    def c(self, name):
        o, p, c = CW.items[name]
        return self.cw[0:p, o:o + c]

    def cbig(self, name, dst, dn):
        o, p, c = CB.items[name]
        self.P.dma('sp', lambda e: e.dma_start(out=dst, in_=self.cb_d[0:p, o:o + c]), 'cb', writes=[dn])

    def w(self, name):
        o, p, c = LW.items[name]
        return self.lw[0:p, o:o + c]

    def dump(self, name, ap, r):
        if name in self.dbg_d:
            self.P.dma('pool', lambda e: e.dma_start(out=self.dbg_d[name], in_=ap), 'dbg', reads=r)

    def layer(self, l):
        P = self.P
        self.arena_reset()
        P.dma('sp', lambda e: e.dma_start(out=self.lw[:], in_=self.lw_d[l]), 'lw', writes=['lw'])
        self.adaln(l)
        if l == 0:
            self.dump('misc', self.misc[:], ['misc', 'ada', 'mod'])
        if 'ffn0' in self.stages:
            self.ffn(l, 0)
        else:
            self.wtile += 19
        self.mixer(l)
        if 'ffn1' in self.stages:
            self.ffn(l, 2)
        else:
            self.wtile += 19
        if l == 0:
            self.dump('x_l0', self.xT[:], ['xT'])

    def ada_mm(self, l, tmp, tn):
        P = self.P
        identb = self.c('ident').unsqueeze(1).to_broadcast([128, 4, 128])
        for t in range(18):
            wt, wn = self.next_wtile(self.wada_d[l, t])
            wv = wt.rearrange("p (k c) -> p k c", c=512)
            b = self.bank()
            for kc in range(8):
                self.mm(self.ps[:, b, :], self.s16[:, kc:kc + 1].to_broadcast([128, 128]), wv[:, kc, :], kc == 0, kc == 7,
                        [wn, 'cb16'], [('ps', b)])
            tv = tmp[:, (t % 2) * 512:(t % 2 + 1) * 512].rearrange("p (j c) -> p j c", c=128)
            self.tt('dve', tv, self.ps[:, b, :].rearrange("p (j c) -> p j c", c=128), identb, ALU.mult, [('ps', b), 'cw'], [(tn, t % 2)])
            P.op('dve', (lambda tv, t: lambda e: e.tensor_reduce(out=self.adaraw[:, 4 * t:4 * t + 4], in_=tv, axis=mybir.AxisListType.X, op=ALU.add))(tv, t),
                 [(tn, t % 2)], ['adaraw'])
            yield

    def adaln(self, l):
        if l == 0 or 'mla' not in self.stages:
            tmp, tn = self.f32('ada_tmp', 128, 1024)
            for _ in self.ada_mm(l, tmp, tn):
                pass
        self.tt('dve', self.ada, self.adaraw, self.w('bada'), ALU.add, ['adaraw', 'lw'], ['ada'])
        adav = self.ada.rearrange("p (i k) -> p i k", k=8)
        gains = ['g_ffn0', 'g_mix', 'g_ffn1']
        for s in range(3):
            self.stt('dve', self.modA[:, s, :], adav[:, 3 * s + 1, :], 1.0, self.w(gains[s]), ALU.add, ALU.mult, ['ada', 'lw'], [('mod', s)])
            self.cp('dve', self.modB[:, s, :], adav[:, 3 * s, :], ['ada'], [('mod', s)])
            self.ts('dve', self.modG[:, s, :], adav[:, 3 * s + 2, :], 1.0 if s == 1 else 0.5, None, ALU.mult, None, ['ada'], [('mod', s)])

    def norm_mod(self, s):
        sq2 = [self.b16('sq0', 128, NT), self.b16('sq1', 128, NT)]
        rstd, rn = self.f32('rstd', 128, NT)
        tmp, tn = self.f32('nm_tmp', 128, 2 * NT)
        banks = [self.bank(), self.bank()]
        for kc in range(8):
            sq, sqn = sq2[kc % 2]
            self.act(sq, self.xT[:, kc, :], AF.Square, [('xT', kc)], [sqn])
            for blk in range(2):
                self.mm(self.ps[:, banks[blk], :], self.ones16, sq[:, blk * 512:(blk + 1) * 512], kc == 0, kc == 7,
                        [sqn, 'cb16'], [('ps', banks[blk])])
        for blk in range(2):
            self.rsqrt(rstd[:, blk * 512:(blk + 1) * 512], self.ps[:, banks[blk], :], 1.0 / D, self.epsc,
                       [('ps', banks[blk]), 'misc'], [(rn, blk)])
        for kc in range(8):
            t = tmp[:, (kc % 2) * NT:(kc % 2 + 1) * NT]
            self.stt('dve', t, self.xT[:, kc, :], self.modA[:, s, kc:kc + 1], rstd, ALU.mult, ALU.mult,
                     [('xT', kc), ('mod', s), rn], [(tn, kc % 2)])
            self.act(self.h[:, kc, :], t, AF.Identity, [(tn, kc % 2), ('mod', s)], [('h', kc)], bias=self.modB[:, s, kc:kc + 1])

    def ffn(self, l, s):
        self.arena_reset()
        self.norm_mod(s)
        if l == 0 and s == 0:
            self.dump('h0', self.h[:], ['h'])
        actb, an = self.b16('ffn_act', 128, NF * NT)
        actv = actb.rearrange("p (f t) -> p f t", t=NT)
        sg, sgn = self.f32('ffn_sg', 128, 4 * 512)
        k = 0
        for g in range(11):
            wt, wn = self.next_wtile()
            wv = wt.rearrange("p (k c) -> p k c", c=512)
            for ff in range(2):
                f = 2 * g + ff
                for blk in range(2):
                    bg, bu = self.bank(), self.bank()
                    for kc in range(8):
                        self.mm(self.ps[:, bg, :], wv[:, kc, ff * 128:(ff + 1) * 128], self.h[:, kc, blk * 512:(blk + 1) * 512],
                                kc == 0, kc == 7, [wn, ('h', kc)], [('ps', bg)])
                    for kc in range(8):
                        self.mm(self.ps[:, bu, :], wv[:, kc, 256 + ff * 128:256 + (ff + 1) * 128], self.h[:, kc, blk * 512:(blk + 1) * 512],
                                kc == 0, kc == 7, [wn, ('h', kc)], [('ps', bu)])
                    sgt = sg[:, (k % 4) * 512:(k % 4 + 1) * 512]
                    self.act(sgt, self.ps[:, bg, :], AF.Silu, [('ps', bg)], [(sgn, k % 4)])
                    self.tt('dve', actv[:, f, blk * 512:(blk + 1) * 512], sgt, self.ps[:, bu, :], ALU.mult,
                            [(sgn, k % 4), ('ps', bu)], [(an, f, blk)])
                    k += 1
        if l == 0 and s == 0:
            self.dump('act0', actv, [an])
        for dc in range(8):
            wt, wn = self.next_wtile()
            wv = wt[:, 0:NF * 128].rearrange("p (f c) -> p f c", c=128)
            for blk in range(2):
                b = self.bank()
                for f in range(NF):
                    self.mm(self.ps[:, b, :], wv[:, f, :], actv[:, f, blk * 512:(blk + 1) * 512], f == 0, f == NF - 1,
                            [wn, (an, f, blk)], [('ps', b)])
                xs = self.xT[:, dc, blk * 512:(blk + 1) * 512]
                self.stt('dve', xs, self.ps[:, b, :], self.modG[:, s, dc:dc + 1], xs, ALU.mult, ALU.add,
                         [('ps', b), ('mod', s), ('xT', dc)], [('xT', dc)])
        if s == 0:
            self.prefetch(3)
        elif l + 1 < self.depth:
            self.prefetch(2)

    def mixer(self, l):
        self.keep16 = 8 * NT
        self.arena_reset(self.keep16)
        self.norm_mod(1)
        self.omix = self.ar16[:, 0:8 * NT].rearrange("p (k t) -> p k t", t=NT)
        self.memset('pool', self.ar16[:, 0:8 * NT], 0.0, ['omix'])
        if 'hg' in self.stages:
            self.hgrn(l)
        else:
            self.wtile += 3
        if 'hy' in self.stages:
            self.hyena(l)
        else:
            self.wtile += 2
        if 'gd' in self.stages:
            self.gdn(l)
        else:
            self.wtile += 2
            self.arena_reset(self.keep16)
            self.abmla = self.next_wtile()
        if 'mla' in self.stages:
            g_mla = self.mla(l)
            next(g_mla)
            alive = [g_mla]
            if l + 1 < self.depth:
                alive.append(self.ada_mm(l + 1, *self.ada_tmp))
            while alive:
                for gg in list(alive):
                    try:
                        next(gg)
                    except StopIteration:
                        alive.remove(gg)
            self.prefetch(2)
        self.arena_reset(self.keep16)
        for hh in range(2):
            wt, wn = self.next_wtile()
            wv = wt.rearrange("p (k c) -> p k c", c=512)
            for dcl in range(4):
                dc = hh * 4 + dcl
                for blk in range(2):
                    b = self.bank()
                    for kc in range(8):
                        self.mm(self.ps[:, b, :], wv[:, kc, dcl * 128:(dcl + 1) * 128], self.omix[:, kc, blk * 512:(blk + 1) * 512],
                                kc == 0, kc == 7, [wn, 'omix'], [('ps', b)])
                    xs = self.xT[:, dc, blk * 512:(blk + 1) * 512]
                    self.stt('dve', xs, self.ps[:, b, :], self.modG[:, 1, dc:dc + 1], xs, ALU.mult, ALU.add,
                             [('ps', b), ('mod', 1), ('xT', dc)], [('xT', dc)])
        self.prefetch(2)

    def proj(self, wv, wn, c0, m, dst, dn, scale=None):
        if m == 64:
            c1 = min(c0, 384)
            off, mm_ = c0 - c1, 128
        else:
            c1, off, mm_ = c0, 0, m
        dn = (dn,) if isinstance(dn, str) else tuple(dn)
        for blk in range(2):
            b = self.bank()
            for kc in range(8):
                self.mm(self.ps[0:mm_, b, :], wv[:, kc, c1:c1 + mm_], self.h[:, kc, blk * 512:(blk + 1) * 512], kc == 0, kc == 7,
                        [wn, ('h', kc)], [('ps', b)])
            self.cp('act' if blk == 0 else 'dve', dst[0:m, blk * 512:(blk + 1) * 512], self.ps[off:off + m, b, :], [('ps', b)], [dn + (blk,)])

    def hgrn_lb(self):
        raw = self.w('hg_lbraw')
        e = self.misc[0:64, 240:272]
        ssum = self.misc[0:64, 272:280]
        self.act(e, raw, AF.Exp, ['lw'], ['lbe'])
        self.tt('dve', ssum, e[:, 0:8], e[:, 8:16], ALU.add, ['lbe'], ['lbs'])
        self.tt('dve', ssum, ssum, e[:, 16:24], ALU.add, ['lbe', 'lbs'], ['lbs'])
        self.tt('dve', ssum, ssum, e[:, 24:32], ALU.add, ['lbe', 'lbs'], ['lbs'])
        self.recip(ssum, ssum, ['lbs'], ['lbs'])
        lb = self.lball
        self.memset('dve', lb[:, 0:8], 0.0, ['lb'])
        for li in range(1, 4):
            self.tt('dve', e[:, li * 8:(li + 1) * 8], e[:, li * 8:(li + 1) * 8], ssum, ALU.mult, ['lbe', 'lbs'], ['lbe'])
            self.tt('dve', lb[:, li * 8:(li + 1) * 8], lb[:, (li - 1) * 8:li * 8], e[:, li * 8:(li + 1) * 8], ALU.add, ['lbe', 'lb'], ['lb'])
        self.ts('dve', self.misc[0:64, 208:240], lb, -1.0, 1.0, ALU.mult, ALU.add, ['lb'], ['lb'])

    def hgrn(self, l):
        P = self.P
        self.arena_reset(self.keep16)
        lb2 = self.misc[:, 360:376]
        oml2 = self.misc[:, 376:392]
        if l == 0:
            self.hgrn_lb()
            for hl in range(2):
                for (dst, src) in ((lb2, self.lball), (oml2, self.misc[0:64, 208:240])):
                    self.cp('dve', dst[hl * 64:(hl + 1) * 64, :].rearrange("k (a p) -> k a p", p=2),
                            src.rearrange("k (a p h) -> k a p h", p=2, h=2)[:, :, :, hl], ['lb'], ['lb2'])
        tiles = [self.next_wtile() for _ in range(3)]
        tv = [(t.rearrange("p (k c) -> p k c", c=512), n) for t, n in tiles]
        st, stn = self.f32('hg_st', 128, 1024)
        stv = st.rearrange("p (s d q v) -> p s d q v", s=4, d=2, q=2)
        for hl in range(2):
            srcv = self.hgst_d[l].rearrange("k (s d q h v) -> k s d q h v", s=4, d=2, q=2, h=2)[:, :, :, :, hl, :]
            P.dma('sp', (lambda hl, srcv: lambda e: e.dma_start(out=stv[hl * 64:(hl + 1) * 64], in_=srcv))(hl, srcv), 'hgin', writes=[stn])
        gn2, gn2n = self.f32('hg_gn2', 128, 2)
        for hl in range(2):
            self.cp('dve', gn2[hl * 64:(hl + 1) * 64, :], self.w('hg_gn').rearrange("k (q h) -> k q h", h=2)[:, :, hl], ['lw'], [gn2n])
        q, qn = self.f32('hg_q', 128, NT)
        g2 = [self.f32('hg_g0', 128, NT), self.f32('hg_g1', 128, NT)]
        z, zn = self.f32('hg_z', 128, NT)
        KK, kkn = self.f32('hg_kk', 128, NT)
        Pz, pzn = self.f32('hg_pz', 128, NT + 32)
        E, en = self.f32('hg_e', 128, NT)
        X, xn = self.f32('hg_x', 128, NT)
        og, ogn = self.f32('hg_og', 128, NT)
        rsB, rsBn = self.f32('hg_rsB', 128, NT)
        Dm2 = [self.f32('hg_D0', 128, 2048), self.f32('hg_D1', 128, 2048)]
        dd2 = [self.f32('hg_dd0', 128, 64), self.f32('hg_dd1', 128, 64)]
        S, sn = self.f32('hg_S', 128, 64)
        sqB, sqBn = self.b16in32('hg_sqB', 128, NT)
        qe2 = [self.b16('hg_qe0', 128, NT), self.b16('hg_qe1', 128, NT)]
        ke, ken = self.b16('hg_ke', 128, NT)
        ko, kon = self.b16('hg_ko', 128, NT)
        kt, ktn = self.b16('hg_kt', 32, 2048)
        vt, vtn = self.b16('hg_vt', 32, 4096)
        AT2 = [self.b16('hg_AT0', 32, 2 * NT), self.b16('hg_AT1', 32, 2 * NT)]
        Sa, san = self.b16('hg_Sa', 128, 2048)
        ktv = kt.rearrange("p (n k) -> p n k", k=64)
        vtv = vt.rearrange("p (n c) -> p n c", c=128)
        Sav = Sa.rearrange("p (n v) -> p n v", v=64)
        ones_b = self.c('ones')[:, 0:1].to_broadcast([128, NT])
        self.memset('dve', Pz[:, 0:1], 0.0, [pzn])
        self.memset('dve', Pz[:, NT + 1:NT + 32], 0.0, [pzn])
        m32 = self.c('m32')

        def chunkcol(ap, off):
            return ap[:, off:off + NT].rearrange("p (n c) -> p n c", c=32)[:, :, 0]

        def bc(ap):
            return ap.unsqueeze(2).to_broadcast([128, 32, 32])

        v3 = lambda ap: ap.rearrange("p (n c) -> p n c", c=32)

        def front(u):
            pr, d = u // 2, u % 2
            qe, qen = qe2[d]
            AT, atn = AT2[d]
            Dm, dmn = Dm2[d]
            dd, ddn = dd2[d]
            g, gn = g2[pr]
            ATv = AT.rearrange("p (h n i) -> p h n i", h=2, i=32)
            if d == 0:
                self.proj(tv[0][0], tv[0][1], pr * 128, 128, q, qn)
                yield
                self.proj(tv[1][0], tv[1][1], pr * 128, 128, g, gn)
                self.act(g, g, AF.Silu, [gn], [gn])
                yield
                for n4 in range(8):
                    b = self.bank()
                    for nn in range(4):
                        n = n4 * 4 + nn
                        for kc in range(8):
                            self.mm(self.ps[0:32, b, nn * 128:(nn + 1) * 128], self.h[:, kc, n * 32:(n + 1) * 32], tv[0][0][:, kc, 256 + pr * 128:256 + (pr + 1) * 128],
                                    kc == 0, kc == 7, [tv[0][1], ('h', kc)], [('ps', b)])
                    self.cp('act', vt[:, n4 * 512:(n4 + 1) * 512], self.ps[0:32, b, :], [('ps', b)], [(vtn, n4)])
                    yield
            col = l * 4 + d * 2 + pr
            if d == 0:
                self.proj(tv[1][0], tv[1][1], 256 + pr * 128, 128, z, zn)
            else:
                self.proj(tv[2][0], tv[2][1], pr * 128, 128, z, zn)
            yield
            F, fn = z, zn
            self.act(F, z, AF.Sigmoid, [zn], [fn])
            self.ts('dve', F, F, oml2[:, col:col + 1], lb2[:, col:col + 1], ALU.mult, ALU.add, [fn, 'lb2'], [fn])
            self.act(KK, F, AF.Identity, [fn], [kkn], bias=1.0, scale=-1.0)
            self.act(F, F, AF.Ln, [fn], [fn])
            yield
            P.op('dve', lambda e: e.tensor_tensor_scan(out=Pz[:, 1:NT + 1], data0=ones_b, data1=F, initial=0.0, op0=ALU.mult, op1=ALU.add),
                 [fn, 'cw'], [pzn])
            V = Pz[:, 1:NT + 1] if d == 0 else Pz[:, 0:NT]
            A_ = chunkcol(Pz, 16)
            P0 = chunkcol(Pz, 0)
            Pl = chunkcol(Pz, 32)
            sgn = 1.0 if d == 0 else -1.0
            self.tt('dve', v3(E), v3(V), bc(A_), ALU.subtract, [pzn], [en])
            self.ts('pool', E, E, 80.0, -80.0, ALU.min, ALU.max, [en], [en])
            yield
            self.act(X, E, AF.Exp, [en], [xn], scale=sgn, bias=self.misc[:, 171:172])
            self.tt('pool', qe, q, X, ALU.mult, [qn, xn], [qen])
            self.act(X, E, AF.Exp, [en], [xn], scale=-sgn)
            self.tt('pool', ke, KK, X, ALU.mult, [kkn, xn], [ken])
            yield
            self.tt('dve', v3(E), v3(V), bc(Pl if d == 0 else P0), ALU.subtract, [pzn], [en])
            self.act(X, E, AF.Exp, [en], [xn], scale=-sgn)
            self.tt('pool', ko, KK, X, ALU.mult, [kkn, xn], [kon])
            if d == 0:
                self.tt('pool', dd[:, 0:32], A_, P0, ALU.subtract, [pzn], [ddn])
            else:
                self.tt('pool', dd[:, 0:32], Pl, A_, ALU.subtract, [pzn], [ddn])
            self.tt('pool', dd[:, 32:64], Pl, P0, ALU.subtract, [pzn], [ddn])
            self.act(dd, dd, AF.Exp, [ddn], [ddn])
            yield
            for hl in range(2):
                ph = slice(hl * 64, (hl + 1) * 64)
                for half in range(2):
                    b = self.bank()
                    pb = self.ps[:, b, :].bitcast(BF16)
                    for nn in range(16):
                        n = half * 16 + nn
                        self.tr(pb[0:32, nn * 64:(nn + 1) * 64], ko[ph, n * 32:(n + 1) * 32], self.id16[ph, ph], [kon, 'cb16'], [('ps', b)])
                    self.cp('act', kt[:, half * 1024:(half + 1) * 1024], pb[0:32, :], [('ps', b)], [(ktn, half)])
                    yield
                for half in range(2):
                    b = self.bank()
                    for nn in range(16):
                        n = half * 16 + nn
                        self.mm(self.ps[0:32, b, nn * 32:(nn + 1) * 32], ke[ph, n * 32:(n + 1) * 32], qe[ph, n * 32:(n + 1) * 32], True, True, [ken, qen], [('ps', b)])
                    self.tt('dve', ATv[:, hl, half * 16:(half + 1) * 16, :], self.ps[0:32, b, :].rearrange("p (n i) -> p n i", i=32),
                            m32[:, d * 32:(d + 1) * 32].unsqueeze(1).to_broadcast([32, 16, 32]), ALU.mult, [('ps', b), 'cw'], [(atn, hl, half)])
                    yield
                for n8 in range(4):
                    b = self.bank()
                    for nn in range(8):
                        n = n8 * 8 + nn
                        self.mm(self.ps[ph, b, nn * 64:(nn + 1) * 64], ktv[:, n, :], vtv[:, n, hl * 64:(hl + 1) * 64], True, True, [ktn, vtn], [('ps', b)])
                    self.cp('act', Dm[ph, n8 * 512:(n8 + 1) * 512], self.ps[ph, b, :], [('ps', b)], [(dmn, hl, n8)])
                    yield

        def back(u):
            pr, d = u // 2, u % 2
            qe, qen = qe2[d]
            AT, atn = AT2[d]
            Dm, dmn = Dm2[d]
            dd, ddn = dd2[d]
            g, gn = g2[pr]
            ATv = AT.rearrange("p (h n i) -> p h n i", h=2, i=32)
            Dv = Dm.rearrange("p (n v) -> p n v", v=64)
            self.memset('dve', S, 0.0, [sn])
            order = range(32) if d == 0 else range(31, -1, -1)
            for n in order:
                seg = n // 8
                first = (n % 8 == 0) if d == 0 else (n % 8 == 7)
                last = (n % 8 == 7) if d == 0 else (n % 8 == 0)
                if first:
                    self.stt('dve', S, S, self.c('carry'), stv[:, seg, d, pr, :], ALU.mult, ALU.add, [sn, 'cw', (stn, seg, d, pr)], [sn])
                self.ts('dve', Sav[:, n, :], S, dd[:, n:n + 1], None, ALU.mult, None, [sn, ddn], [(san, n)])
                self.stt('dve', S, S, dd[:, 32 + n:33 + n], Dv[:, n, :], ALU.mult, ALU.add, [sn, ddn, dmn], [sn])
                if last:
                    self.cp('dve', stv[:, seg, d, pr, :], S, [sn], [(stn, seg, d, pr)])
                if n % 4 == 3:
                    yield
            for half in range(2):
                b = self.bank()
                for hl in range(2):
                    ph = slice(hl * 64, (hl + 1) * 64)
                    for nn in range(16):
                        n = half * 16 + nn
                        cs = slice(n * 32, (n + 1) * 32)
                        self.mm(self.ps[ph, b, nn * 32:(nn + 1) * 32], Sav[ph, n, :], qe[ph, cs], nn == 0, False, [(san, n), qen], [('ps', b)])
                    for nn in range(16):
                        n = half * 16 + nn
                        self.mm(self.ps[ph, b, nn * 32:(nn + 1) * 32], vtv[:, n, hl * 64:(hl + 1) * 64], ATv[:, hl, n, :], False, nn == 15, [vtn, (atn, hl, half)], [('ps', b)])
                osl = og[:, half * 512:(half + 1) * 512]
                if d == 0:
                    self.cp('act', osl, self.ps[:, b, :], [('ps', b)], [(ogn, half)])
                else:
                    self.tt('dve', osl, osl, self.ps[:, b, :], ALU.add, [(ogn, half), ('ps', b)], [(ogn, half)])
                yield
            if d == 1:
                self.rms_fm(og, ogn, 128, NT, sqB, sqBn, rsB, rsBn, 1.0 / 64, ones=self.blk16)
                self.stt('dve', og, og, gn2[:, pr:pr + 1], rsB, ALU.mult, ALU.mult, [ogn, gn2n, rsBn], [ogn])
                self.tt('dve', self.omix[:, pr, :], og, g, ALU.mult, [ogn, gn], ['omix'])
                yield

        def run(gen):
            for _ in gen:
                pass

        def interleave(a, b_):
            alive = [a, b_]
            while alive:
                for gg in list(alive):
                    try:
                        next(gg)
                    except StopIteration:
                        alive.remove(gg)

        run(front(0))
        interleave(back(0), front(1))
        run(back(1))
        run(front(2))
        interleave(back(2), front(3))
        run(back(3))
        for hl in range(2):
            dstv = self.hgo_d[l].rearrange("k (s d q h v) -> k s d q h v", s=4, d=2, q=2, h=2)[:, :, :, :, hl, :]
            P.dma('sp', (lambda hl, dstv: lambda e: e.dma_start(out=dstv, in_=stv[hl * 64:(hl + 1) * 64]))(hl, dstv), 'hgo', reads=[stn])
        self.prefetch(2)

    def sin_rr(self, out, arg, tmp, r, w, tn):
        self.ts('dve', tmp, arg, 1.0 / (2 * math.pi), MAGIC, ALU.mult, ALU.add, r, [tn])
        self.ts('dve', tmp, tmp, MAGIC, None, ALU.subtract, None, [tn], [tn])
        self.stt('dve', tmp, tmp, -2.0 * math.pi, arg, ALU.mult, ALU.add, [tn] + r, [tn])
        self.act(out, tmp, AF.Sin, [tn], w)

    def hyena(self, l):
        P = self.P
        self.arena_reset(self.keep16)
        zemb, zn = self.f32('zemb', 33, NT)
        winf, wfn = self.f32('winf', 128, 2048)
        winb, wbn = self.f32('winb', 128, 2048)
        self.cbig('zemb', zemb, zn)
        self.cbig('winf', winf, wfn)
        self.cbig('winb', winb, wbn)
        a1, a1n = self.f32('hy_a1', 64, NT)
        t1, t1n = self.f32('hy_t1', 64, NT)
        h1, h1n = self.f32('hy_h1', 64, NT)
        sc, scn = self.f32('hy_sc', 64, 2)
        hs, hsn = self.b16('hy_hs', 128, 2048)
        hd, hdn = self.b16('hy_hd', 128, 2048)
        assert self.a16 == self.keep16 + 4096
        hsv = hs.rearrange("p (j c) -> p j c", c=256)
        hdv = hd.rearrange("p (j c) -> p j c", c=256)
        self.tt('dve', sc[:, 0:1], self.w('hy_fr'), self.w('hy_b1'), ALU.mult, ['lw'], [scn])
        self.tt('dve', sc[:, 1:2], self.w('hy_fr'), self.w('hy_b2'), ALU.mult, ['lw'], [scn])
        src, srcn = zemb, zn
        for li, (wname, kdim) in enumerate((('hy_w1', 33), ('hy_w2', 64))):
            for blk in range(2):
                b = self.bank()
                self.mm(self.ps[0:64, b, :], self.w(wname), src[0:kdim, blk * 512:(blk + 1) * 512], True, True, ['lw', srcn], [('ps', b)])
                self.act(a1[:, blk * 512:(blk + 1) * 512], self.ps[0:64, b, :], AF.Identity, [('ps', b), 'lw', scn], [(a1n, blk)],
                         bias=sc[:, li:li + 1], scale=self.w('hy_fr'))
            dst, dstn = (h1, h1n) if li == 0 else (a1, a1n)
            self.sin_rr(dst, a1, t1, [a1n], [dstn], t1n)
            src, srcn = dst, dstn
        h2, h2n = src, srcn
        tf, tfn = self.f32('hy_tf', 128, 512)
        for pc in range(8):
            b = self.bank()
            self.mm(self.ps[:, b, :], h2[:, pc * 128:(pc + 1) * 128], self.w('hy_w3'), True, True, [h2n, 'lw'], [('ps', b)])
            self.tt('dve', tf[:, 0:256], self.ps[:, b, 0:256], winf[:, pc * 256:(pc + 1) * 256], ALU.mult, [('ps', b), wfn], [tfn])
            self.tt('dve', tf[:, 256:512], self.ps[:, b, 256:512], winb[:, pc * 256:(pc + 1) * 256], ALU.mult, [('ps', b), wbn], [tfn])
            self.tt('dve', hsv[:, pc, :], tf[:, 0:256], tf[:, 256:512], ALU.add, [tfn], [(hsn, pc)])
            self.tt('dve', hdv[:, pc, :], tf[:, 256:512], tf[:, 0:256], ALU.subtract, [tfn], [(hdn, pc)])
        self.arena_reset(self.keep16 + 4096)
        hsn, hdn = 'hyhs', 'hyhd'
        hsv = self.ar16[:, self.keep16:self.keep16 + 2048].rearrange("p (j c) -> p j c", c=256)
        hdv = self.ar16[:, self.keep16 + 2048:self.keep16 + 4096].rearrange("p (j c) -> p j c", c=256)
        U, un = self.f32('hy_u', 128, 6 * NT)
        Cc, ccn = self.f32('hy_c', 128, 3 * NT)
        t1, n1 = self.next_wtile()
        t2, n2 = self.next_wtile()
        v1 = t1.rearrange("p (k c) -> p k c", c=512)
        v2 = t2.rearrange("p (k c) -> p k c", c=512)
        for g in range(6):
            wv, wn, c0 = (v1, n1, g * 128) if g < 4 else (v2, n2, (g - 4) * 128)
            self.proj(wv, wn, c0, 128, U[:, g * NT:(g + 1) * NT], (un, g))
        nw, nwn = self.f32('hy_nw', 128, 18)
        self.ts('dve', nw, self.w('hy_cw'), self.c('ncarry'), None, ALU.mult, None, ['lw', 'cw'], [nwn])
        cwv = self.w('hy_cw').rearrange("p (g k) -> p g k", k=3)
        nwv = nw.rearrange("p (g k) -> p g k", k=3)
        zb, zbn = self.b16('hy_zb', 128, 2 * NT)
        zT, ztn = self.b16('hy_zT', 128, 2048)
        zTv = zT.rearrange("p (t c) -> p t c", c=256)

        def conv(g, dst, dn):
            x = U[:, g * NT:(g + 1) * NT]
            xn = (un, g)
            self.act(dst, x, AF.Identity, [xn, 'lw'], [dn], bias=self.w('hy_cb')[:, g:g + 1], scale=cwv[:, g, 1:2])
            self.stt('dve', dst[:, 1:NT], x[:, 0:NT - 1], cwv[:, g, 0:1], dst[:, 1:NT], ALU.mult, ALU.add, [xn, 'lw', dn], [dn])
            self.stt('dve', dst[:, 0:NT - 1], x[:, 1:NT], cwv[:, g, 2:3], dst[:, 0:NT - 1], ALU.mult, ALU.add, [xn, 'lw', dn], [dn])
            xs = x.rearrange("p (s t) -> p s t", t=256)
            ds = dst.rearrange("p (s t) -> p s t", t=256)
            self.stt('dve', ds[:, 1:4, 0], xs[:, 0:3, 255], nwv[:, g, 0:1], ds[:, 1:4, 0], ALU.mult, ALU.add, [xn, nwn, dn], [dn])
            self.stt('dve', ds[:, 0:3, 255], xs[:, 1:4, 0], nwv[:, g, 2:3], ds[:, 0:3, 255], ALU.mult, ALU.add, [xn, nwn, dn], [dn])

        for cc in range(2):
            conv(cc, Cc[:, cc * NT:(cc + 1) * NT], (ccn, cc))
        for cc in range(2):
            sc2 = Cc[:, 2 * NT:3 * NT]
            conv(2 + cc, sc2, (ccn, 2))
            conv(4 + cc, U[:, cc * NT:(cc + 1) * NT], (un, cc))
            self.tt('dve', U[:, (2 + cc) * NT:(3 + cc) * NT], sc2, U[:, cc * NT:(cc + 1) * NT], ALU.mult, [(ccn, 2), (un, cc), (un, 2 + cc)], [(un, 2 + cc)])
            self.cp('act', zb[:, cc * NT:(cc + 1) * NT], U[:, (2 + cc) * NT:(3 + cc) * NT], [(un, 2 + cc)], [(zbn, cc)])
        for tc in range(8):
            b = self.bank()
            pb = self.ps[:, b, :].bitcast(BF16)
            for cc in range(2):
                self.tr(pb[:, cc * 128:(cc + 1) * 128], zb[:, cc * NT + tc * 128:cc * NT + (tc + 1) * 128], self.id16, [(zbn, cc), 'cb16'], [('ps', b)])
            self.cp('dve' if tc % 2 else 'act', zTv[:, tc, :], pb[:, 0:256], [('ps', b)], [(ztn, tc)])
        Y, yn = self.b16('hy_Y', 128, 4096)
        Yv = Y.rearrange("p (r f c) -> p r f c", r=2, c=256)
        zz, zzn = self.f32('hy_zz', 128, 1024)
        for half in range(2):
            ct, cn = self.next_wtile(self.dft_d[half], plain=True)
            stl, sn = self.next_wtile(self.dft_d[2 + half], plain=True)
            cv = ct.rearrange("p (k c) -> p k c", c=512)
            sv = stl.rearrange("p (k c) -> p k c", c=512)
            for fcl in range(4):
                fc = half * 4 + fcl
                b = self.bank()
                for tc in range(8):
                    self.mm(self.ps[:, b, 0:256], cv[:, tc, fcl * 128:(fcl + 1) * 128], zTv[:, tc, :], tc == 0, tc == 7, [cn, ztn], [('ps', b)])
                b2 = self.bank()
                for tc in range(8):
                    self.mm(self.ps[:, b2, 0:256], sv[:, tc, fcl * 128:(fcl + 1) * 128], zTv[:, tc, :], tc == 0, tc == 7, [sn, ztn], [('ps', b2)])
                bk1 = self.bank()
                for jc in range(8):
                    self.mm(self.ps[:, bk1, 0:256], cv[:, jc, fcl * 128:(fcl + 1) * 128], hsv[:, jc, :], jc == 0, jc == 7, [cn, hsn], [('ps', bk1)])
                bk2 = self.bank()
                for jc in range(8):
                    self.mm(self.ps[:, bk2, 0:256], sv[:, jc, fcl * 128:(fcl + 1) * 128], hdv[:, jc, :], jc == 0, jc == 7, [sn, hdn], [('ps', bk2)])
                kre, kim = self.ps[:, bk1, 0:256], self.ps[:, bk2, 0:256]
                self.cp('act', zz[:, 0:256], self.ps[:, b, 0:256], [('ps', b)], [zzn])
                self.cp('act', zz[:, 256:512], self.ps[:, b2, 0:256], [('ps', b2)], [zzn])
                self.tt('dve', zz[:, 512:768], zz[:, 0:256], kre, ALU.mult, [zzn, ('ps', bk1)], [zzn])
                self.tt('dve', zz[:, 768:1024], zz[:, 256:512], kim, ALU.mult, [zzn, ('ps', bk2)], [zzn])
                self.tt('pool', Yv[:, 0, fc, :], zz[:, 512:768], zz[:, 768:1024], ALU.add, [zzn], [(yn, 0, fc)])
                self.tt('dve', zz[:, 512:768], zz[:, 256:512], kre, ALU.mult, [zzn, ('ps', bk1)], [zzn])
                self.tt('dve', zz[:, 768:1024], zz[:, 0:256], kim, ALU.mult, [zzn, ('ps', bk2)], [zzn])
                self.tt('pool', Yv[:, 1, fc, :], zz[:, 512:768], zz[:, 768:1024], ALU.subtract, [zzn], [(yn, 1, fc)])
        for blk in range(2):
            ct, cn = self.next_wtile(self.dft_d[4 + blk], plain=True)
            stl, sn = self.next_wtile(self.dft_d[6 + blk], plain=True)
            cv = ct.rearrange("p (k c) -> p k c", c=512)
            sv = stl.rearrange("p (k c) -> p k c", c=512)
            for cc in range(2):
                b = self.bank()
                for fc in range(8):
                    self.mm(self.ps[:, b, :], Yv[:, 0, fc, cc * 128:(cc + 1) * 128], cv[:, fc, :], fc == 0, False, [cn, yn], [('ps', b)])
                    self.mm(self.ps[:, b, :], Yv[:, 1, fc, cc * 128:(cc + 1) * 128], sv[:, fc, :], False, fc == 7, [sn, yn], [('ps', b)])
                zsl = U[:, (2 + cc) * NT + blk * 512:(2 + cc) * NT + (blk + 1) * 512]
                tmp = zz[:, 0:512]
                self.stt('dve', tmp, zsl, self.w('hy_skip')[:, cc:cc + 1], self.ps[:, b, :], ALU.mult, ALU.add, [(un, 2 + cc), 'lw', ('ps', b)], [zzn])
                self.tt('dve', self.omix[:, 2 + cc, blk * 512:(blk + 1) * 512], tmp, Cc[:, cc * NT + blk * 512:cc * NT + (blk + 1) * 512], ALU.mult,
                        [zzn, (ccn, cc)], ['omix'])
        self.prefetch(3)

    def gdn(self, l):
        P = self.P
        self.arena_reset(self.keep16)
        tq, tqn = self.next_wtile()
        tvt, tvn = self.next_wtile()
        tab, tabn = self.next_wtile()
        self.abmla = (tab, tabn)
        tqv = tq.rearrange("p (k c) -> p k c", c=512)
        tvv = tvt.rearrange("p (k c) -> p k c", c=512)
        tabv = tab.rearrange("p (k c) -> p k c", c=512)
        G, gn_ = self.f32('gd_G', 16, NT)
        X1, x1n = self.f32('gd_X1', 16, NT)
        BE, ben = self.f32('gd_BE', 16, NT)
        tok, tokn = self.f32('gd_tok', 64, 1280)
        tokv = tok.rearrange("p (q n r) -> p q n r", q=5, r=16)
        R16, r16n = self.f32('gd_R16', 16, NT)
        LA, lan = self.f32('gd_LA', 16, NT)
        Pz, pzn = self.f32('gd_Pz', 16, NT + 64)
        X2, x2n = self.f32('gd_X2', 16, NT)
        X3, x3n = self.f32('gd_X3', 16, NT)
        tF, tfn = self.f32('gd_tF', 16, NT)
        tB, tbn = self.f32('gd_tB', 16, NT)
        nega, ngn = self.f32('gd_nega', 16, 2)
        mf, mb = self.c('mf'), self.c('mb')
        self.proj(tabv, tabn, 0, 16, R16, r16n)
        self.act(BE, R16, AF.Sigmoid, [r16n], [ben])
        self.act(LA, R16, AF.Exp, [r16n, 'lw'], [lan], bias=self.w('gd_dtb'))
        self.act(LA, LA, AF.Ln, [lan], [lan], bias=1.0)
        self.act(nega[:, 0:1], self.w('gd_alog'), AF.Exp, ['lw'], [ngn])
        self.ts('dve', nega[:, 1:2], nega[:, 0:1], -1.0, None, ALU.mult, None, [ngn], [ngn])
        self.ts('dve', LA, LA, nega[:, 1:2], None, ALU.mult, None, [lan, ngn], [lan])
        self.memset('dve', Pz, 0.0, [pzn])
        ones_b = self.c('ones')[0:16, 0:1].to_broadcast([16, NT])
        P.op('dve', lambda e: e.tensor_tensor_scan(out=Pz[:, 1:NT + 1], data0=ones_b, data1=LA, initial=0.0, op0=ALU.mult, op1=ALU.add),
             [lan, 'cw', pzn], [pzn])
        V = Pz[:, 1:NT + 1]
        W = Pz[:, 0:NT]
        c3 = lambda ap: ap.rearrange("p (n c) -> p n c", c=64)
        bcc = lambda ap: ap.unsqueeze(2).to_broadcast([16, 16, 64])
        P0 = Pz[:, 0:NT].rearrange("p (n c) -> p n c", c=64)[:, :, 0]
        Pl = Pz[:, 64:64 + NT].rearrange("p (n c) -> p n c", c=64)[:, :, 0]

        def combine(dst, dn):
            self.ts('dve', tF, tF, mf, None, ALU.mult, None, [tfn, 'cw'], [tfn])
            self.stt('dve', dst, tB, mb, tF, ALU.mult, ALU.add, [tbn, 'cw', tfn], [dn])

        self.cp('dve', tF, V, [pzn], [tfn])
        self.ts('dve', tB, W, -1.0, None, ALU.mult, None, [pzn], [tbn])
        combine(G, gn_)
        self.tt('dve', c3(tF), c3(V), bcc(P0), ALU.subtract, [pzn], [tfn])
        self.act(tF, tF, AF.Exp, [tfn], [tfn])
        self.tt('dve', c3(tB), c3(W), bcc(Pl), ALU.subtract, [pzn], [tbn])
        self.act(tB, tB, AF.Exp, [tbn], [tbn], scale=-1.0)
        combine(X1, x1n)
        self.tt('dve', c3(tF), c3(V), bcc(Pl), ALU.subtract, [pzn], [tfn])
        self.act(tF, tF, AF.Exp, [tfn], [tfn], scale=-1.0)
        self.tt('dve', c3(tB), c3(W), bcc(P0), ALU.subtract, [pzn], [tbn])
        self.act(tB, tB, AF.Exp, [tbn], [tbn])
        combine(X2, x2n)
        self.memset('dve', X3, 0.0, [x3n])
        self.tt('dve', c3(X3), c3(X3), bcc(Pl), ALU.add, [x3n, pzn], [x3n])
        self.tt('dve', c3(X3), c3(X3), bcc(P0), ALU.subtract, [x3n, pzn], [x3n])
        self.act(X3, X3, AF.Exp, [x3n], [x3n])
        for qi, (src, srcn) in enumerate(((G, gn_), (BE, ben), (X1, x1n), (X2, x2n), (X3, x3n))):
            b = self.bank()
            for n in range(16):
                self.tr(self.ps[0:64, b, n * 16:(n + 1) * 16], src[:, n * 64:(n + 1) * 64], self.c('ident')[0:16, 0:16], [srcn, 'cw'], [('ps', b)])
            self.cp('act' if qi % 2 else 'dve', tok[:, qi * 256:(qi + 1) * 256], self.ps[0:64, b, 0:256], [('ps', b)], [(tokn, qi)])
        self.arena_reset(self.keep16, keep32=3 * NT + 1280)
        gn_, x1n, ben, tokn = 'gdG', 'gdX1', 'gdBE', 'gdtok'
        st, stn = self.f32('gd_st', 64, 2048)
        P.dma('sp', lambda e: e.dma_start(out=st, in_=self.gdst_d[l]), 'gdin', writes=[stn])
        stv = st.rearrange("p (s d h v) -> p s d h v", s=4, d=2, h=4)
        KKs, kksn = self.f32('gd_KK', 64, NT)
        QKs, qksn = self.f32('gd_QK', 64, NT)
        E3, e3n = self.f32('gd_E3', 64, NT)
        Tm, tmn = self.f32('gd_Tm', 64, NT)
        Yt, ytn = self.f32('gd_Yt', 64, NT)
        Y, yn = self.f32('gd_Y', 64, NT)
        Pm, pmn = self.f32('gd_P', 64, NT)
        og, ogn = self.f32('gd_og', 64, NT)
        S, sn = self.f32('gd_S', 64, 64)
        nw, nwn = self.f32('gd_nw', 64, 36)
        qn16, qnn = self.b16('gd_qn', 64, NT)
        kn16, knn = self.b16('gd_kn', 64, NT)
        v16, v16n = self.b16('gd_v16', 64, NT)
        sg16, sgn_ = self.b16('gd_sg', 64, NT)
        ktok, ktn = self.b16('gd_ktok', 64, NT)
        vtok, vtn = self.b16('gd_vtok', 64, NT)
        attnT, atn = self.b16('gd_attnT', 64, NT)
        T16, t16n = self.b16('gd_T16', 64, NT)
        kg, kgn = self.b16('gd_kg', 64, NT)
        kout, kon = self.b16('gd_kout', 64, NT)
        nw0T, nw0n = self.b16('gd_nw0T', 64, NT)
        qin, qinn = self.b16('gd_qin', 64, NT)
        Sall, saln = self.b16('gd_Sall', 64, NT)
        vnew, vnn = self.b16('gd_vnew', 64, NT)
        sq, sqn = self.b16('gd_sq', 64, NT)
        u3 = lambda ap: ap.rearrange("p (n c) -> p n c", c=64)
        ub = lambda ap: ap.unsqueeze(2).to_broadcast([64, 16, 64])
        mb3 = lambda ap: ap.unsqueeze(1).to_broadcast([64, 16, 64])
        G = self.ar32[0:16, 0:NT]
        X1 = self.ar32[0:16, NT:2 * NT]
        BE = self.ar32[0:16, 2 * NT:3 * NT]
        tokv = self.ar32[0:64, 3 * NT:3 * NT + 1280].rearrange("p (q n r) -> p q n r", q=5, r=16)
        m64 = self.c('m64')
        ident = self.c('ident')
        self.ts('dve', nw, self.w('gd_cw'), self.c('ncarry')[0:64, :], None, ALU.mult, None, ['lw', 'cw'], [nwn])
        cwv = self.w('gd_cw').rearrange("p (g k) -> p g k", k=3)
        nwv = nw.rearrange("p (g k) -> p g k", k=3)
        R, rn = E3, e3n
        cx, cxn = Tm, tmn

        def conv_silu(gi):
            self.act(cx, R, AF.Copy, [rn, 'lw'], [cxn], scale=cwv[:, gi, 1:2])
            self.stt('dve', cx[:, 1:NT], R[:, 0:NT - 1], cwv[:, gi, 0:1], cx[:, 1:NT], ALU.mult, ALU.add, [rn, 'lw', cxn], [cxn])
            self.stt('dve', cx[:, 0:NT - 1], R[:, 1:NT], cwv[:, gi, 2:3], cx[:, 0:NT - 1], ALU.mult, ALU.add, [rn, 'lw', cxn], [cxn])
            xs = R.rearrange("p (s t) -> p s t", t=256)
            ds = cx.rearrange("p (s t) -> p s t", t=256)
            self.stt('dve', ds[:, 1:4, 0], xs[:, 0:3, 255], nwv[:, gi, 0:1], ds[:, 1:4, 0], ALU.mult, ALU.add, [rn, nwn, cxn], [cxn])
            self.stt('dve', ds[:, 0:3, 255], xs[:, 1:4, 0], nwv[:, gi, 2:3], ds[:, 0:3, 255], ALU.mult, ALU.add, [rn, nwn, cxn], [cxn])
            self.act(cx, cx, AF.Silu, [cxn], [cxn])

        def bcast_row(src, srcn, row):
            bs = []
            for blk in range(2):
                b = self.bank()
                self.mm(self.ps[0:64, b, :], ident[0:16, row:row + 1].to_broadcast([16, 64]), src[:, blk * 512:(blk + 1) * 512], True, True,
                        [srcn, 'cw'], [('ps', b)])
                bs.append(b)
            return bs

        attnT2 = [(attnT, atn), self.b16in32('gd_attnT1', 64, NT)]
        vtok2 = [(vtok, vtn), self.b16in32('gd_vtok1', 64, NT)]
        sg2 = [(sg16, sgn_), self.b16in32('gd_sg1', 64, NT)]
        sqA, sqAn = self.b16in32('gd_sqA', 64, NT)
        rsA, rsAn = self.f32('gd_rsA', 64, NT)

        def headstart(hh):
            par = hh % 2
            vtk, vtkn = vtok2[par]
            sgb, sgbn = sg2[par]
            for which, (tvw, tn_, c0) in enumerate(((tqv, tqn, hh * 64), (tqv, tqn, 256 + hh * 64), (tvv, tvn, hh * 64))):
                self.proj(tvw, tn_, c0, 64, R, rn)
                yield
                conv_silu(which * 4 + hh)
                yield
                if which < 2:
                    self.rms_fm(cx, cxn, 64, NT, sq, sqn, Yt, ytn, 1.0)
                    dst, dn = (qn16, qnn) if which == 0 else (kn16, knn)
                    self.stt('dve', dst, cx, 0.125 if which == 0 else 1.0, Yt, ALU.mult, ALU.mult, [cxn, ytn], [dn])
                else:
                    self.cp('act', v16, cx, [cxn], [v16n])
                yield
            self.proj(tvv, tvn, 256 + hh * 64, 64, R, rn)
            self.act(sgb, R, AF.Silu, [rn], [sgbn])
            yield
            for (src, srcn, dst, dn) in ((kn16, knn, ktok, ktn), (v16, v16n, vtk, vtkn)):
                b = self.bank()
                pb = self.ps[:, b, :].bitcast(BF16)
                for n in range(16):
                    self.tr(pb[0:64, n * 64:(n + 1) * 64], src[:, n * 64:(n + 1) * 64], self.id16[0:64, 0:64], [srcn, 'cb16'], [('ps', b)])
                self.cp('act', dst, pb[0:64, :], [('ps', b)], [dn])
                yield
            for (rhs16, rhsn, dst, dn) in ((kn16, knn, KKs, kksn), (qn16, qnn, QKs, qksn)):
                for half in range(2):
                    b = self.bank()
                    for nn in range(8):
                        n = half * 8 + nn
                        cs = slice(n * 64, (n + 1) * 64)
                        self.mm(self.ps[0:64, b, nn * 64:(nn + 1) * 64], kn16[:, cs], rhs16[:, cs], True, True, [knn, rhsn], [('ps', b)])
                    self.cp('act' if half else 'dve', dst[:, half * 512:(half + 1) * 512], self.ps[0:64, b, :], [('ps', b)], [(dn, half)])
                yield

        def prep_inv(hh, d):
            r = d * 4 + hh
            aT, aTn = attnT2[d]
            m_incl_t = m64[:, 0:64] if d == 0 else m64[:, 128:192]
            m_str_t = m64[:, 64:128] if d == 0 else m64[:, 192:256]
            m_str_2 = m64[:, 192:256] if d == 0 else m64[:, 64:128]
            gb = bcast_row(G, gn_, r)
            for blk in range(2):
                hs = slice(blk * 512, (blk + 1) * 512)
                self.tt('dve', u3(E3[:, hs]), u3(self.ps[0:64, gb[blk], :]), tokv[:, 0, blk * 8:(blk + 1) * 8, r].unsqueeze(2).to_broadcast([64, 8, 64]),
                        ALU.subtract, [('ps', gb[blk]), tokn], [(e3n, blk)])
            yield
            self.ts('dve', Tm, E3, 0.0, None, ALU.min, None, [e3n], [tmn])
            self.act(Tm, Tm, AF.Exp, [tmn], [tmn])
            yield
            self.tt('dve', u3(Yt), u3(Tm), mb3(m_incl_t), ALU.mult, [tmn, 'cw'], [ytn])
            self.tt('dve', aT, Yt, QKs, ALU.mult, [ytn, qksn], [aTn])
            yield
            self.tt('dve', u3(Yt), u3(Tm), mb3(m_str_t), ALU.mult, [tmn, 'cw'], [ytn])
            self.tt('dve', Yt, Yt, KKs, ALU.mult, [ytn, kksn], [ytn])
            self.stt('dve', u3(Yt), u3(Yt), -1.0, ub(tokv[:, 1, :, 8 + r]), ALU.mult, ALU.mult, [ytn, tokn], [ytn])
            yield
            self.ts('dve', Tm, E3, 0.0, None, ALU.max, None, [e3n], [tmn])
            self.act(Tm, Tm, AF.Exp, [tmn], [tmn], scale=-1.0)
            yield
            self.tt('dve', u3(Y), u3(Tm), mb3(m_str_2), ALU.mult, [tmn, 'cw'], [yn])
            self.tt('dve', Y, Y, KKs, ALU.mult, [yn, kksn], [yn])
            bb = bcast_row(BE, ben, 8 + r)
            for blk in range(2):
                hs = slice(blk * 512, (blk + 1) * 512)
                self.stt('dve', Y[:, hs], Y[:, hs], -1.0, self.ps[0:64, bb[blk], :], ALU.mult, ALU.mult, [yn, ('ps', bb[blk])], [(yn, blk)])
            self.tt('dve', u3(Pm), u3(Yt), mb3(ident[0:64, 0:64]), ALU.add, [ytn, 'cw'], [pmn])
            yield
            for lev in range(1, 6):
                ba = [self.bank(), self.bank()]
                if lev < 5:
                    for n in range(16):
                        cs = slice(n * 64, (n + 1) * 64)
                        self.mm(self.ps[0:64, ba[n // 8], (n % 8) * 64:(n % 8 + 1) * 64], Y[:, cs], Yt[:, cs], True, True, [yn, ytn], [('ps', ba[n // 8])])
                    yield
                bbk = [self.bank(), self.bank()]
                for n in range(16):
                    cs = slice(n * 64, (n + 1) * 64)
                    self.mm(self.ps[0:64, bbk[n // 8], (n % 8) * 64:(n % 8 + 1) * 64], Yt[:, cs], Y[:, cs], True, True, [yn, ytn], [('ps', bbk[n // 8])])
                yield
                for half in range(2):
                    hs = slice(half * 512, (half + 1) * 512)
                    if lev < 5:
                        self.cp('act', Yt[:, hs], self.ps[0:64, ba[half], :], [('ps', ba[half])], [(ytn, half)])
                    self.cp('act', Y[:, hs], self.ps[0:64, bbk[half], :], [('ps', bbk[half])], [(yn, half)])
                yield
                bp = [self.bank(), self.bank()]
                for n in range(16):
                    cs = slice(n * 64, (n + 1) * 64)
                    self.mm(self.ps[0:64, bp[n // 8], (n % 8) * 64:(n % 8 + 1) * 64], Y[:, cs], Pm[:, cs], True, True, [yn, pmn], [('ps', bp[n // 8])])
                yield
                for half in range(2):
                    hs = slice(half * 512, (half + 1) * 512)
                    self.tt('dve', Pm[:, hs], Pm[:, hs], self.ps[0:64, bp[half], :], ALU.add, [(pmn, half), ('ps', bp[half])], [(pmn, half)])
                yield

        def finish(hh, d):
            r = d * 4 + hh
            self.cp('act', T16, Pm, [pmn], [t16n])
            self.tt('dve', u3(kg), u3(ktok), ub(tokv[:, 2, :, r]), ALU.mult, [ktn, tokn], [kgn])
            self.tt('dve', u3(kout), u3(ktok), ub(tokv[:, 3, :, r]), ALU.mult, [ktn, tokn], [kon])
            xb = bcast_row(X1, x1n, r)
            for blk in range(2):
                hs = slice(blk * 512, (blk + 1) * 512)
                self.tt('dve', qin[:, hs], qn16[:, hs], self.ps[0:64, xb[blk], :], ALU.mult, [qnn, ('ps', xb[blk])], [(qinn, blk)])
            for half in range(2):
                b = self.bank()
                for nn in range(8):
                    n = half * 8 + nn
                    cs = slice(n * 64, (n + 1) * 64)
                    self.mm(self.ps[0:64, b, nn * 64:(nn + 1) * 64], kg[:, cs], T16[:, cs], True, True, [kgn, t16n], [('ps', b)])
                self.ts('dve', nw0T[:, half * 512:(half + 1) * 512], self.ps[0:64, b, :], -1.0, None, ALU.mult, None, [('ps', b)], [(nw0n, half)])
            yield

        def chain(hh, d):
            r = d * 4 + hh
            vtk, vtkn = vtok2[hh % 2]
            self.memset('dve', S, 0.0, [sn])
            order = range(16) if d == 0 else range(15, -1, -1)
            for n in order:
                seg = n // 4
                first = (n % 4 == 0) if d == 0 else (n % 4 == 3)
                last = (n % 4 == 3) if d == 0 else (n % 4 == 0)
                cs = slice(n * 64, (n + 1) * 64)
                if first:
                    self.stt('dve', S, S, self.c('carry')[0:64, :], stv[:, seg, d, hh, :], ALU.mult, ALU.add, [sn, 'cw', (stn, seg, d, hh)], [sn])
                self.cp('dve', Sall[:, cs], S, [sn], [(saln, n)])
                b = self.bank()
                self.mm(self.ps[0:64, b, 0:64], T16[:, cs], vtk[:, cs], True, False, [t16n, vtkn], [('ps', b)])
                self.mm(self.ps[0:64, b, 0:64], nw0T[:, cs], Sall[:, cs], False, True, [(nw0n, n // 8), (saln, n)], [('ps', b)])
                self.ts('dve', vnew[:, cs], self.ps[0:64, b, 0:64], tokv[:, 1, n, 8 + r:9 + r], None, ALU.mult, None, [('ps', b), tokn], [(vnn, n)])
                yield
                b2 = self.bank()
                self.mm(self.ps[0:64, b2, 0:64], kout[:, cs], vnew[:, cs], True, True, [kon, (vnn, n)], [('ps', b2)])
                self.stt('dve', S, S, tokv[:, 4, n, r:r + 1], self.ps[0:64, b2, 0:64], ALU.mult, ALU.add, [sn, tokn, ('ps', b2)], [sn])
                if last:
                    self.cp('dve', stv[:, seg, d, hh, :], S, [sn], [(stn, seg, d, hh)])
                yield

        def outp(hh, d):
            aT, aTn = attnT2[d]
            for half in range(2):
                b = self.bank()
                for nn in range(8):
                    n = half * 8 + nn
                    cs = slice(n * 64, (n + 1) * 64)
                    self.mm(self.ps[0:64, b, nn * 64:(nn + 1) * 64], Sall[:, cs], qin[:, cs], True, False, [(saln, n), qinn], [('ps', b)])
                    self.mm(self.ps[0:64, b, nn * 64:(nn + 1) * 64], vnew[:, cs], aT[:, cs], False, True, [(vnn, n), aTn], [('ps', b)])
                osl = og[:, half * 512:(half + 1) * 512]
                if d == 0:
                    self.cp('act', osl, self.ps[0:64, b, :], [('ps', b)], [(ogn, half)])
                else:
                    self.tt('dve', osl, osl, self.ps[0:64, b, :], ALU.add, [(ogn, half), ('ps', b)], [(ogn, half)])
                yield

        def normg(hh):
            sgb, sgbn = sg2[hh % 2]
            self.rms_fm(og, ogn, 64, NT, sqA, sqAn, rsA, rsAn, 1.0 / 64)
            self.stt('dve', og, og, self.w('gd_gn'), rsA, ALU.mult, ALU.mult, [ogn, 'lw', rsAn], [ogn])
            po = (hh % 2) * 64
            self.tt('dve', self.omix[po:po + 64, 6 + hh // 2, :], og, sgb, ALU.mult, [ogn, sgbn], ['omix'])
            yield

        def seq(*gens):
            for g in gens:
                yield from g

        def run(g):
            for _ in g:
                pass

        def interleave(a, b, ra=1, rb=1):
            alive = [a, b]
            rates = {id(a): ra, id(b): rb}
            while alive:
                for g in list(alive):
                    for _ in range(rates[id(g)]):
                        try:
                            next(g)
                        except StopIteration:
                            alive.remove(g)
                            break

        run(headstart(0))
        run(prep_inv(0, 0))
        run(finish(0, 0))
        for hh in range(4):
            interleave(seq(chain(hh, 0), outp(hh, 0)), prep_inv(hh, 1), 1, 1)
            run(finish(hh, 1))
            if hh < 3:
                interleave(seq(chain(hh, 1), outp(hh, 1), normg(hh)), seq(headstart(hh + 1), prep_inv(hh + 1, 0)), 1, 1)
                run(finish(hh + 1, 0))
            else:
                run(seq(chain(hh, 1), outp(hh, 1), normg(hh)))
        P.dma('sp', lambda e: e.dma_start(out=self.gdo_d[l], in_=st), 'gdo', reads=[stn])

    def rms_fm(self, x, xn, parts, n, sq, sqn, rstd, rn, scale, ones=None):
        self.act(sq[0:parts, 0:n], x, AF.Square, [xn], [sqn])
        c0 = 0
        while c0 < n:
            w = min(512, n - c0)
            b = self.bank()
            self.mm(self.ps[0:parts, b, 0:w], (self.ones16 if ones is None else ones)[0:parts, 0:parts], sq[0:parts, c0:c0 + w], True, True, [sqn, 'cb16'], [('ps', b)])
            self.rsqrt(rstd[0:parts, c0:c0 + w], self.ps[0:parts, b, 0:w], scale, self.epsc[0:parts, :], [('ps', b), 'misc'], [(rn, c0)])
            c0 += w

    def mla(self, l):
        P = self.P
        self.arena_reset(self.keep16)
        self.ada_tmp = self.f32('ada_tmp', 128, 1024)
        wt, wn = self.abmla
        wv = wt.rearrange("p (k c) -> p k c", c=512)
        cq, cqn = self.f32('m_cq', 128, 2 * NT)
        ckv, ckvn = self.f32('m_ckv', 128, NT)
        kr, krn = self.f32('m_kr', 32, NT)
        cosq, cosn = self.f32('m_cos', 96, NT)
        sinq, sinn = self.f32('m_sin', 96, NT)
        rstd, rsn = self.f32('m_rstd', 128, 1280)
        qh, qhn = self.f32('m_qh', 96, NT)
        tmp, tmn = self.f32('m_tmp', 96, NT)
        kh, khn = self.f32('m_kh', 96, 1280)
        sq, sqn = self.b16('m_sq', 128, 1280)
        cqn16, cq16n = self.b16('m_cqn', 128, 2 * NT)
        kvin, kvn = self.b16('m_kvin', 128, 1280)
        kr16, kr16n = self.b16('m_kr16', 32, 1280)
        vtok, vtn = self.b16('m_vtok', 128, 2560)
        vtv = vtok.rearrange("p (k c) -> p k c", c=256)
        wq16, wqn = self.b16('m_wq', 128, 768)
        wk16, wkn = self.b16('m_wk', 128, 384)
        wv16, wvn = self.b16('m_wv', 128, 256)
        rm16, rmn = self.b16('m_rm', 96, 96)
        e16, e16n = self.b16('m_e32', 32, 96)
        xn16, xn16n = self.b16('m_xn16', 96, 1280)
        qrot, qrn = self.b16('m_qrot', 104, NT)
        krot, krotn = self.b16('m_krot', 104, 1280)
        vext, vxn = self.b16('m_vext', 128, 1280)
        vxv = vext.rearrange("p (k c) -> p k c", c=128)
        pT, ptn = self.b16('m_pT', 128, 1024)
        self.cbig('cosq', cosq, cosn)
        self.cbig('sinq', sinq, sinn)
        o, p_, c_ = CB.items['ek']
        P.dma('pool', lambda e: e.dma_start(out=krot[96:104, :], in_=self.cb_d[0:8, o:o + 1280]), 'cb2', writes=[(krotn, 'm')])
        o2 = CB.items['eq'][0]
        P.dma('pool', lambda e: e.dma_start(out=qrot[96:104, :], in_=self.cb_d[0:8, o2:o2 + 1024]), 'cb2', writes=[(qrn, 'm')])
        self.memset('pool', vext, 1.0, [vxn])
        P.dma('pool', lambda e: e.dma_start(out=kvin[:, 1024:1280], in_=self.ctxkv_d[l]), 'cb2', writes=[(kvn, 1)])
        P.dma('pool', lambda e: e.dma_start(out=kr16[:, 1024:1280], in_=self.ctxkr_d[l]), 'cb2', writes=[(kr16n, 1)])
        for (dst, dn, src) in ((wq16, wqn, 'm_wq'), (wk16, wkn, 'm_wk'), (wv16, wvn, 'm_wv')):
            self.cp('dve', dst, self.w(src), ['lw'], [dn])
        self.cp('dve', rm16, self.c('rm'), ['cw'], [rmn])
        self.cp('dve', e16, self.c('e32'), ['cw'], [e16n])
        self.proj(wv, wn, 16, 128, cq[:, 0:NT], (cqn, 0))
        self.proj(wv, wn, 144, 128, cq[:, NT:2 * NT], (cqn, 1))
        self.proj(wv, wn, 272, 128, ckv, ckvn)
        self.proj(wv, wn, 400, 32, kr, krn)
        P.dma('sp', lambda e: e.dma_start(out=self.krT_d[l], in_=kr), 'kro', reads=[krn])
        self.cp('act', kr16[:, 0:NT], kr, [krn], [(kr16n, 0)])
        yield
        banks = [self.bank(), self.bank()]
        for c in range(2):
            self.act(sq[:, 0:NT], cq[:, c * NT:(c + 1) * NT], AF.Square, [(cqn, c)], [sqn])
            for blk in range(2):
                self.mm(self.ps[:, banks[blk], :], self.ones16, sq[:, blk * 512:(blk + 1) * 512], c == 0, c == 1, [sqn, 'cb16'], [('ps', banks[blk])])
        for blk in range(2):
            self.rsqrt(rstd[:, blk * 512:(blk + 1) * 512], self.ps[:, banks[blk], :], 1.0 / 256, self.epsc, [('ps', banks[blk]), 'misc'], [(rsn, blk)])
        for c in range(2):
            self.stt('dve', cqn16[:, c * NT:(c + 1) * NT], cq[:, c * NT:(c + 1) * NT], self.w('m_qa')[:, c:c + 1], rstd[:, 0:NT], ALU.mult, ALU.mult,
                     [(cqn, c), 'lw', rsn], [(cq16n, c)])
        self.rms_fm(ckv, ckvn, 128, NT, sq, sqn, rstd, rsn, 1.0 / 128)
        self.stt('dve', ckv, ckv, self.w('m_kva'), rstd[:, 0:NT], ALU.mult, ALU.mult, [ckvn, 'lw', rsn], [ckvn])
        P.dma('sp', lambda e: e.dma_start(out=self.ckvT_d[l], in_=ckv), 'ckvo', reads=[ckvn])
        self.cp('act', kvin[:, 0:NT], ckv, [ckvn], [(kvn, 0)])
        for kc in range(10):
            b = self.bank()
            self.mm(self.ps[:, b, 0:256], kvin[:, kc * 128:(kc + 1) * 128], wv16, True, True, [kvn, wvn], [('ps', b)])
            self.cp('act' if kc % 2 else 'dve', vtv[:, kc, :], self.ps[:, b, 0:256], [('ps', b)], [(vtn, kc)])
        yield
        self.nbank = 7
        scale = 96.0 ** -0.5
        qrot1, qr1n = self.b16in32('m_qrot1', 104, NT)
        krot1, kr1n = self.b16in32('m_krot1', 104, 1280)
        vext1, vx1n = self.b16in32('m_vext1', 128, 1280)
        rec, recn = self.f32('m_rec', 64, 512)
        P.dma('pool', lambda e: e.dma_start(out=krot1[96:104, :], in_=self.cb_d[0:8, o:o + 1280]), 'cb2', writes=[(kr1n, 'm')])
        P.dma('pool', lambda e: e.dma_start(out=qrot1[96:104, :], in_=self.cb_d[0:8, o2:o2 + 1024]), 'cb2', writes=[(qr1n, 'm')])
        self.memset('pool', vext1, 1.0, [vx1n])
        qrot2 = [(qrot, qrn), (qrot1, qr1n)]
        krot2 = [(krot, krotn), (krot1, kr1n)]
        vext2 = [(vext, vxn), (vext1, vx1n)]

        def prep(hh):
            qro, qron = qrot2[hh % 2]
            kro, kron = krot2[hh % 2]
            vx, vxn_ = vext2[hh % 2]
            vxv_ = vx.rearrange("p (k c) -> p k c", c=128)
            self.cp('pool', vxv_[:, :, 0:64], vtv[:, :, hh * 64:(hh + 1) * 64], [vtn], [vxn_])
            for blk in range(2):
                b = self.bank()
                for c in range(2):
                    self.mm(self.ps[0:96, b, :], wq16[:, c * 384 + hh * 96:c * 384 + (hh + 1) * 96], cqn16[:, c * NT + blk * 512:c * NT + (blk + 1) * 512],
                            c == 0, c == 1, [wqn, cq16n], [('ps', b)])
                self.cp('act', qh[:, blk * 512:(blk + 1) * 512], self.ps[0:96, b, :], [('ps', b)], [(qhn, blk)])
            yield
            self.rms_fm(qh, qhn, 96, NT, sq, sqn, rstd, rsn, 1.0 / 96)
            self.stt('dve', qh, qh, self.w('m_gq'), rstd[0:96, 0:NT], ALU.mult, ALU.mult, [qhn, 'lw', rsn], [qhn])
            self.cp('act', xn16[:, 0:NT], qh, [qhn], [xn16n])
            yield
            for blk in range(2):
                b = self.bank()
                sl = slice(blk * 512, (blk + 1) * 512)
                self.mm(self.ps[0:96, b, :], rm16, xn16[:, sl], True, True, [rmn, xn16n], [('ps', b)])
                self.tt('dve', tmp[:, sl], self.ps[0:96, b, :], sinq[:, sl], ALU.mult, [('ps', b), sinn], [(tmn, blk)])
                self.tt('pool', qh[:, sl], qh[:, sl], cosq[:, sl], ALU.mult, [qhn, cosn], [(qhn, blk)])
                self.tt('dve', qro[0:96, sl], tmp[:, sl], qh[:, sl], ALU.add, [(tmn, blk), (qhn, blk)], [(qron, blk)])
            yield
            for (c0, w) in ((0, 512), (512, 512), (1024, 256)):
                b = self.bank()
                self.mm(self.ps[0:96, b, 0:w], wk16[:, hh * 96:(hh + 1) * 96], kvin[:, c0:c0 + w], True, False, [wkn, kvn], [('ps', b)])
                self.mm(self.ps[0:96, b, 0:w], e16, kr16[:, c0:c0 + w], False, True, [e16n, kr16n], [('ps', b)])
                self.cp('act', kh[:, c0:c0 + w], self.ps[0:96, b, 0:w], [('ps', b)], [(khn, c0)])
            yield
            self.rms_fm(kh, khn, 96, 1280, sq, sqn, rstd, rsn, 1.0 / 96)
            self.stt('dve', kh, kh, self.w('m_gk'), rstd[0:96, 0:1280], ALU.mult, ALU.mult, [khn, 'lw', rsn], [khn])
            self.cp('act', xn16, kh, [khn], [xn16n])
            yield
            self.cp('dve', kro[0:96, 1024:1280], kh[:, 1024:1280], [khn], [(kron, 2)])
            for blk in range(2):
                b = self.bank()
                sl = slice(blk * 512, (blk + 1) * 512)
                self.mm(self.ps[0:96, b, :], rm16, xn16[:, sl], True, True, [rmn, xn16n], [('ps', b)])
                self.tt('dve', tmp[:, sl], self.ps[0:96, b, :], sinq[:, sl], ALU.mult, [('ps', b), sinn], [(tmn, blk)])
                self.tt('pool', kh[:, sl], kh[:, sl], cosq[:, sl], ALU.mult, [khn, cosn], [(khn, blk)])
                self.tt('dve', kro[0:96, sl], tmp[:, sl], kh[:, sl], ALU.add, [(tmn, blk), (khn, blk)], [(kron, blk)])
            yield

        def attn(hh):
            qro, qron = qrot2[hh % 2]
            kro, kron = krot2[hh % 2]
            vx, vxn_ = vext2[hh % 2]
            vxv_ = vx.rearrange("p (k c) -> p k c", c=128)
            for qb in range(2):
                qs = slice(qb * 512, (qb + 1) * 512)

                def score(kc):
                    b = self.bank()
                    ks = slice(kc * 128, (kc + 1) * 128)
                    self.mm(self.ps[:, b, :], kro[:, ks], qro[:, qs], True, True, [kron, qron], [('ps', b)])
                    return b
                bnext = score(0)
                for kc in range(10):
                    b = bnext
                    pt = pT[:, (kc % 2) * 512:(kc % 2 + 1) * 512]
                    self.act(pt, self.ps[:, b, :], AF.Exp, [('ps', b)], [(ptn, kc % 2)], scale=scale)
                    if kc < 9:
                        bnext = score(kc + 1)
                    self.mm(self.ps[:, 7, :], vxv_[:, kc, :], pt, kc == 0, kc == 9, [vxn_, (ptn, kc % 2)], [('ps', 7)])
                    if kc % 2:
                        yield
                self.recip(rec, self.ps[64:128, 7, :], [('ps', 7)], [recn])
                po = (hh % 2) * 64
                self.tt('dve', self.omix[po:po + 64, 4 + hh // 2, qs], self.ps[0:64, 7, :], rec, ALU.mult, [('ps', 7), recn], ['omix'])
                yield

        def inter(a_, b_):
            alive = [a_, b_]
            while alive:
                for gg in list(alive):
                    try:
                        next(gg)
                        yield
                    except StopIteration:
                        alive.remove(gg)

        yield from prep(0)
        for hh in range(4):
            if hh < 3:
                yield from inter(attn(hh), prep(hh + 1))
            else:
                yield from attn(hh)
        self.nbank = 8

PROMPT_ASSIGN = [[0, 1, 2], [3, 4, 5], [6, 7, 8], [9, 10, 11], [12, 13], [14, 15]]


def core_plan():
    plan = [(True, [0] * 4), (True, [1] * 4)]
    for seqs in PROMPT_ASSIGN:
        s4 = list(seqs) + [seqs[0]] * (4 - len(seqs))
        plan.append((False, s4))
    return plan


def make_in_maps(inp, depth=DEPTH):
    inp = {k: np.asarray(v) for k, v in inp.items()}
    wstream = pack_wstream(inp)
    wada = np.ascontiguousarray(inp['w_ada'].reshape(DEPTH, 8, 128, 18, 512).transpose(0, 3, 2, 1, 4).reshape(DEPTH, 18, 128, 4096))
    lw = np.stack([pack_layer_small(inp, l) for l in range(DEPTH)])
    import ml_dtypes
    dft_s, dft_p = dft_tiles(True).astype(ml_dtypes.bfloat16), dft_tiles(False).astype(ml_dtypes.bfloat16)
    maps = []
    for ci, (is_s, segs) in enumerate(core_plan()):
        m = {}
        if is_s:
            b = segs[0]
            x = inp['x_sample'][b]
            cond = inp['c'][b]
            ckv = inp['cache_mla_ckv'][b].transpose(0, 2, 1)
            ckr = inp['cache_mla_krope'][b].transpose(0, 2, 1)
            hg = np.zeros((DEPTH, 64, 4, 2, 4, 64), np.float32)
            gd = np.zeros((DEPTH, 64, 4, 2, 4, 64), np.float32)
            hg[:, :, 0, 0] = inp['state_hgrn'][b][:, 0].transpose(0, 2, 1, 3)
            hg[:, :, 3, 1] = inp['state_hgrn'][b][:, 1].transpose(0, 2, 1, 3)
            gd[:, :, 0, 0] = inp['state_gdn'][b][:, 0].transpose(0, 2, 1, 3)
            gd[:, :, 3, 1] = inp['state_gdn'][b][:, 1].transpose(0, 2, 1, 3)
        else:
            x = np.concatenate([inp['x_prompt'][s] for s in segs], axis=0)
            cond = inp['c_ctx']
            ckv = np.zeros((DEPTH, 128, 256), np.float32)
            ckr = np.zeros((DEPTH, 32, 256), np.float32)
            hg = np.zeros((DEPTH, 64, 4, 2, 4, 64), np.float32)
            gd = np.zeros((DEPTH, 64, 4, 2, 4, 64), np.float32)
        m['xT'] = np.ascontiguousarray(x.T.reshape(8, 128, NT))
        m['cw'], m['cb'] = core_consts(is_s, cond)
        m['lw'] = lw
        m['wstream'] = wstream
        m['wada'] = wada
        m['dft'] = dft_s if is_s else dft_p
        m['ctxkv'] = np.ascontiguousarray(ckv)
        m['ctxkr'] = np.ascontiguousarray(ckr)
        m['hgst'] = np.ascontiguousarray(hg.reshape(DEPTH, 64, 2048))
        m['gdst'] = np.ascontiguousarray(gd.reshape(DEPTH, 64, 2048))
        maps.append(m)
    return maps


def assemble(results):
    y_prompt = np.zeros((16, 256, D), np.float32)
    y_sample = np.zeros((2, 1024, D), np.float32)
    n_ckv = np.zeros((16, DEPTH, 256, 128), np.float32)
    n_kr = np.zeros((16, DEPTH, 256, 32), np.float32)
    n_hg = np.zeros((16, DEPTH, 2, 4, 64, 64), np.float32)
    n_gd = np.zeros((16, DEPTH, 2, 4, 64, 64), np.float32)
    for ci, (is_s, segs) in enumerate(core_plan()):
        r = results[ci]
        y = r['yT'].reshape(D, NT).T
        if is_s:
            y_sample[segs[0]] = y
            continue
        hgo = r['hgo'].reshape(DEPTH, 64, 4, 2, 4, 64)
        gdo = r['gdo'].reshape(DEPTH, 64, 4, 2, 4, 64)
        for g, s in enumerate(segs):
            if g > 0 and s == segs[0]:
                continue
            y_prompt[s] = y[g * 256:(g + 1) * 256]
            n_ckv[s] = r['ckvT'][:, :, g * 256:(g + 1) * 256].transpose(0, 2, 1)
            n_kr[s] = r['krT'][:, :, g * 256:(g + 1) * 256].transpose(0, 2, 1)
            n_hg[s] = hgo[:, :, g].transpose(0, 2, 3, 1, 4)
            n_gd[s] = gdo[:, :, g].transpose(0, 2, 3, 1, 4)
    return (y_prompt, y_sample, n_ckv, n_kr, n_hg, n_gd)


def build_nc(depth=DEPTH, stages=None, dbg=None):
    nc = bass.Bass("TRN2", target_bir_lowering=False)
    with contextlib.ExitStack() as st:
        b = B(nc, st, depth, stages, dbg)
        b.build()
    return nc


def kernel(**inputs):
    maps = make_in_maps(inputs)
    nc = build_nc()
    res = run_bass_kernel_spmd(nc, maps, core_ids=list(range(NCORES)))
    return assemble(res.results)
```

```python
import contextlib
import math
import numpy as np
import concourse.bass as bass
import concourse.mybir as mybir
from concourse.bass_utils import run_bass_kernel_spmd

F32 = mybir.dt.float32
BF16 = mybir.dt.bfloat16
AF = mybir.ActivationFunctionType
ALU = mybir.AluOpType

D = 1024
NT = 1024
DEPTH = 4
DFF = 2816
NF = 22
EPS = 1e-6
NCORES = 8
TILE_COLS = 4096
NWSLOT = 3
TILES_PER_LAYER = 48
ADA_Q = 1152
MAGIC = 12582912.0


class Prog:
    ENG = ('pe', 'act', 'dve', 'pool', 'sp')

    def __init__(self, nc):
        self.nc = nc
        self.ops = {e: [] for e in self.ENG}
        self.count = {e: 0 for e in self.ENG}
        self.waited = {e: {} for e in self.ENG}
        self.res = {}
        self.dsem_count = {}
        self.sem_names = []

    @staticmethod
    def _norm(rs):
        return [(r,) if isinstance(r, str) else tuple(r) for r in rs]

    def _conf(self, r):
        d = self.res.setdefault(r[0], {})
        out = []
        for k, v in d.items():
            n = min(len(k), len(r))
            if k[:n] == r[:n]:
                out.append(v)
        return d, out

    def _deps(self, reads, writes):
        toks = []
        for r in reads:
            _, cs = self._conf(r)
            for v in cs:
                if v[0] is not None:
                    toks.append(v[0])
        for w in writes:
            _, cs = self._conf(w)
            for v in cs:
                if v[0] is not None:
                    toks.append(v[0])
                toks.extend(v[1])
        return toks

    def _commit(self, reads, writes, tok):
        for r in reads:
            d, _ = self._conf(r)
            d.setdefault(r, [None, []])[1].append(tok)
        for w in writes:
            d, _ = self._conf(w)
            for k in list(d.keys()):
                if len(k) >= len(w) and k[:len(w)] == w:
                    del d[k]
            d[w] = [tok, []]

    def _waits(self, eng, toks, sync_self=True):
        best = {}
        for (sk, val) in toks:
            if sk == eng and not sync_self:
                continue
            if val > best.get(sk, 0):
                best[sk] = val
        out = []
        for sk, val in best.items():
            if self.waited[eng].get(sk, 0) >= val:
                continue
            self.waited[eng][sk] = val
            out.append((sk, val))
        return out

    def op(self, eng, fn, reads=(), writes=(), sync_self=True):
        reads, writes = self._norm(reads), self._norm(writes)
        waits = self._waits(eng, self._deps(reads, writes), sync_self)
        self.count[eng] += 1
        self.ops[eng].append((waits, fn, (eng, 1)))
        self._commit(reads, writes, (eng, self.count[eng]))
        if eng not in self.sem_names:
            self.sem_names.append(eng)

    def dma(self, eng, fn, slot, reads=(), writes=()):
        reads, writes = self._norm(reads), self._norm(writes)
        toks = self._deps(reads, writes)
        sk = 'd_' + slot
        prev = self.dsem_count.get(sk, 0)
        if prev:
            toks.append((sk, prev))
        waits = self._waits(eng, toks, True)
        self.dsem_count[sk] = prev + 16
        self.ops[eng].append((waits, fn, (sk, 16)))
        self._commit(reads, writes, (sk, prev + 16))
        if sk not in self.sem_names:
            self.sem_names.append(sk)

    def barrier(self):
        toks = [(e, c) for e, c in self.count.items() if c]
        toks += [(k, v) for k, v in self.dsem_count.items() if not k.startswith('d_ws')]
        for eng in self.ENG:
            waits = self._waits(eng, toks, True)
            if waits:
                self.ops[eng].append((waits, None, None))
        self.res = {k: v for k, v in self.res.items() if k == 'ws'}

    def emit(self, stack):
        nc = self.nc
        sems = {n: stack.enter_context(nc.semaphore('s_' + n)) for n in self.sem_names}
        block = stack.enter_context(nc.Block())
        engobj = {'pe': 'tensor', 'act': 'scalar', 'dve': 'vector', 'pool': 'gpsimd', 'sp': 'sync'}

        def mk(ename):
            def body(e):
                for waits, fn, inc in self.ops[ename]:
                    for sk, val in waits:
                        e.wait_ge(sems[sk], val)
                    if fn is not None:
                        fn(e).then_inc(sems[inc[0]], inc[1])
            return body

        for ename in self.ENG:
            if self.ops[ename]:
                getattr(block, engobj[ename])(mk(ename))


class Packer:
    def __init__(self):
        self.items = {}
        self.ncols = 0

    def add(self, name, parts, cols):
        self.items[name] = (self.ncols, parts, cols)
        self.ncols += cols

    def pack(self, vals):
        out = np.zeros((128, self.ncols), np.float32)
        for name, (o, p, c) in self.items.items():
            a = np.asarray(vals[name], np.float32).reshape(p, c)
            out[:p, o:o + c] = a
        return out


def fm(v, parts=128):
    v = np.asarray(v, np.float32)
    return v.reshape(-1, parts).T


LW = Packer()
for _n, _p, _c in [
    ('bada', 128, 72), ('g_ffn0', 128, 8), ('g_mix', 128, 8), ('g_ffn1', 128, 8),
    ('hy_cw', 128, 18), ('hy_cb', 128, 6), ('hy_w1', 33, 64), ('hy_b1', 64, 1), ('hy_fr', 64, 1),
    ('hy_w2', 64, 64), ('hy_b2', 64, 1), ('hy_w3', 64, 512), ('hy_skip', 128, 2),
    ('m_qa', 128, 2), ('m_wq', 128, 768), ('m_kva', 128, 1), ('m_wk', 128, 384), ('m_wv', 128, 256),
    ('m_gq', 96, 1), ('m_gk', 96, 1),
    ('hg_lbraw', 64, 32), ('hg_gn', 64, 4),
    ('gd_cw', 64, 36), ('gd_alog', 16, 1), ('gd_dtb', 16, 1), ('gd_gn', 64, 1),
]:
    LW.add(_n, _p, _c)

CW = Packer()
for _n, _p, _c in [
    ('cond', 128, 8), ('carry', 128, 1), ('ncarry', 128, 1),
    ('ident', 128, 128), ('ones', 128, 128), ('rm', 96, 96), ('e32', 32, 96),
    ('m32', 32, 64), ('m64', 64, 256), ('mf', 16, 1), ('mb', 16, 1),
]:
    CW.add(_n, _p, _c)
CB = Packer()
for _n, _p, _c in [
    ('cosq', 96, 1024), ('sinq', 96, 1024), ('ek', 8, 1280), ('eq', 8, 1024),
    ('zemb', 33, 1024), ('winf', 128, 2048), ('winb', 128, 2048), ('sel', 32, 16 * 64),
]:
    CB.add(_n, _p, _c)


def w_in_colmap():
    HG, HY, MLA = 1280, 768, 416
    hg0, hy0, ml0, gd0 = 0, HG, HG + HY, HG + HY + MLA
    tiles = []
    def rng(a, n):
        return list(range(a, a + n))
    tiles.append(rng(hg0, 512))
    tiles.append(rng(hg0 + 512, 512))
    tiles.append(rng(hg0 + 1024, 256))
    tiles.append(rng(hy0, 512))
    tiles.append(rng(hy0 + 512, 256))
    tiles.append(rng(gd0, 512))
    tiles.append(rng(gd0 + 512, 512))
    tiles.append(rng(gd0 + 1024, 16) + rng(ml0, 416))
    return tiles


def pack_wstream(inp):
    out = np.zeros((DEPTH * TILES_PER_LAYER, 128, TILE_COLS), np.float32)
    cm = w_in_colmap()
    t = 0
    for l in range(DEPTH):
        def ffn(i):
            nonlocal t
            wgu = inp['w_ffn_gu'][l, i].reshape(8, 128, 2 * DFF)
            for g in range(11):
                v = out[t].reshape(128, 8, 512)
                v[:, :, 0:256] = wgu[:, :, g * 256:(g + 1) * 256].transpose(1, 0, 2)
                v[:, :, 256:512] = wgu[:, :, DFF + g * 256:DFF + (g + 1) * 256].transpose(1, 0, 2)
                t += 1
            wd = inp['w_ffn_down'][l, i].reshape(NF, 128, 8, 128)
            for dc in range(8):
                out[t][:, :NF * 128] = wd[:, :, dc, :].transpose(1, 0, 2).reshape(128, NF * 128)
                t += 1
        ffn(0)
        win = inp['w_in'][l].reshape(8, 128, -1)
        for cols in cm:
            v = out[t].reshape(128, 8, 512)
            v[:, :, :len(cols)] = win[:, :, cols].transpose(1, 0, 2)
            t += 1
        wo = inp['w_out'][l].reshape(8, 128, 1024)
        for h in range(2):
            out[t].reshape(128, 8, 512)[:] = wo[:, :, h * 512:(h + 1) * 512].transpose(1, 0, 2)
            t += 1
        ffn(1)
    assert t == DEPTH * TILES_PER_LAYER
    return out


def pack_layer_small(inp, l):
    v = {}
    v['bada'] = fm(inp['b_ada'][l])
    v['g_ffn0'] = fm(inp['norm_ffn'][l, 0])
    v['g_mix'] = fm(inp['norm_mix'][l])
    v['g_ffn1'] = fm(inp['norm_ffn'][l, 1])
    cw = inp['hy_conv_w'][l]
    v['hy_cw'] = np.stack([fm(cw[k]) for k in range(3)], axis=2).reshape(128, 18)
    v['hy_cb'] = fm(inp['hy_conv_b'][l])
    v['hy_w1'] = inp['hy_w1'][l]
    v['hy_b1'] = inp['hy_b1'][l].reshape(64, 1)
    v['hy_fr'] = inp['hy_freq'][l].reshape(64, 1)
    v['hy_w2'] = inp['hy_w2'][l]
    v['hy_b2'] = inp['hy_b2'][l].reshape(64, 1)
    v['hy_w3'] = inp['hy_w3'][l]
    v['hy_skip'] = fm(inp['hy_skip'][l])
    v['m_qa'] = fm(inp['mla_q_norm_a'][l])
    v['m_wq'] = inp['mla_w_q_up'][l].reshape(2, 128, 384).transpose(1, 0, 2).reshape(128, 768)
    v['m_kva'] = inp['mla_kv_norm_a'][l].reshape(128, 1)
    wkv = inp['mla_w_kv_up'][l].reshape(128, 4, 128)
    wk = np.zeros((128, 4, 96), np.float32)
    wk[:, :, :64] = wkv[:, :, :64]
    v['m_wk'] = wk.reshape(128, 384)
    v['m_wv'] = wkv[:, :, 64:].reshape(128, 256)
    v['m_gq'] = inp['mla_qk_norm'][l, 0].reshape(96, 1)
    v['m_gk'] = inp['mla_qk_norm'][l, 1].reshape(96, 1)
    lb = inp['hgrn_lb'].reshape(4, 2, 4, 64)
    v['hg_lbraw'] = lb.transpose(3, 0, 1, 2).reshape(64, 32)
    v['hg_gn'] = inp['hgrn_norm'][l].reshape(4, 64).T
    gw = inp['gdn_conv_w'][l].reshape(3, 12, 64)
    v['gd_cw'] = gw.transpose(2, 1, 0).reshape(64, 36)
    al = np.zeros((16, 1), np.float32); al[:8, 0] = inp['gdn_a_log'][l].reshape(8)
    db = np.zeros((16, 1), np.float32); db[:8, 0] = inp['gdn_dt_bias'][l].reshape(8)
    v['gd_alog'] = al
    v['gd_dtb'] = db
    v['gd_gn'] = inp['gdn_norm'][l].reshape(64, 1)
    return LW.pack(v)


def rope_np(rows, gw=64):
    T = rows * gw
    row = np.repeat(np.arange(rows, dtype=np.float32), gw)
    col = (np.arange(T) % gw).astype(np.float32)
    pairs = 8
    inv = (10000.0 ** (-np.arange(pairs, dtype=np.float32) / pairs)).astype(np.float32)
    ang = np.concatenate([row[:, None] * inv, col[:, None] * inv], axis=-1)
    return np.cos(ang), np.sin(ang)


def core_consts(is_sample, cond):
    v = {}
    v['cond'] = fm(cond)
    v['carry'] = np.full((128, 1), 1.0 if is_sample else 0.0, np.float32)
    v['ncarry'] = np.full((128, 1), 0.0 if is_sample else -1.0, np.float32)
    v['ident'] = np.eye(128, dtype=np.float32)
    v['ones'] = np.ones((128, 128), np.float32)
    cosq = np.ones((96, 1024), np.float32); sinq = np.zeros((96, 1024), np.float32)
    if is_sample:
        c, s = rope_np(16)
        cosq[64:80] = c.T; cosq[80:96] = c.T; sinq[64:80] = s.T; sinq[80:96] = s.T
    v['cosq'], v['sinq'] = cosq, sinq
    ek = np.zeros((8, 1280), np.float32); eq = np.zeros((8, 1024), np.float32)
    for g in range(4):
        ek[g, g * 256:(g + 1) * 256] = 1.0
    ek[4, 1024:] = 1.0
    if not is_sample:
        eq[:5, :] = -30000.0
        for g in range(4):
            eq[g, g * 256:(g + 1) * 256] = 0.0
    v['ek'], v['eq'] = ek, eq
    rm = np.zeros((96, 96), np.float32)
    for i in range(16):
        rm[80 + i, 64 + i] = -1.0
        rm[64 + i, 80 + i] = 1.0
    v['rm'] = rm
    e32 = np.zeros((32, 96), np.float32)
    for i in range(32):
        e32[i, 64 + i] = 1.0
    v['e32'] = e32
    L = 1024 if is_sample else 256
    pos = np.arange(L, dtype=np.float32)
    tt = pos / np.float32(L - 1)
    bands = np.linspace(1e-4, 15, 16, dtype=np.float32)
    ang = (np.float32(2.0 * math.pi / L)) * pos[:, None] * bands[None, :]
    z = np.concatenate([tt[:, None], np.cos(ang), -np.sin(ang)], axis=-1).astype(np.float32)
    z = np.tile(z, (1024 // L, 1))
    v['zemb'] = z.T
    deltas = np.linspace(math.log(1e-2) / 1.5, math.log(1e-2) / 0.3, 256, dtype=np.float32)
    win = np.exp(-tt[:, None] * np.abs(deltas)[None, :]).astype(np.float32)
    winb = win.copy(); winb[0] = 0.0
    win = np.tile(win, (1024 // L, 1)); winb = np.tile(winb, (1024 // L, 1))
    v['winf'] = win.reshape(8, 128, 256).transpose(1, 0, 2).reshape(128, 2048)
    v['winb'] = winb.reshape(8, 128, 256).transpose(1, 0, 2).reshape(128, 2048)
    j32 = np.arange(32)[:, None]; i32 = np.arange(32)[None, :]
    v['m32'] = np.concatenate([(j32 <= i32), (j32 >= i32)], axis=1).astype(np.float32)
    j64 = np.arange(64)[:, None]; i64 = np.arange(64)[None, :]
    v['m64'] = np.concatenate([(j64 <= i64), (j64 < i64), (j64 >= i64), (j64 > i64)], axis=1).astype(np.float32)
    mf = np.zeros((16, 1), np.float32); mf[0:4] = 1.0
    mb = np.zeros((16, 1), np.float32); mb[4:8] = 1.0
    v['mf'], v['mb'] = mf, mb
    sel = np.zeros((32, 16, 64), np.float32)
    for r in range(16):
        sel[r, r, :] = 1.0
    v['sel'] = sel.reshape(32, 1024)
    return CW.pack({k: v[k] for k in CW.items}), CB.pack({k: v[k] for k in CB.items})


def dft_tiles(is_sample):
    L = 1024 if is_sample else 256
    t = np.arange(L, dtype=np.float64)[:, None]
    f = np.arange(L, dtype=np.float64)[None, :]
    ang = math.pi * (2 * f + 1) * t / (2 * L)
    C = np.cos(ang); S = np.sin(ang)
    nb = 1024 // L
    def bd(M):
        out = np.zeros((1024, 1024), np.float32)
        for b in range(nb):
            out[b * L:(b + 1) * L, b * L:(b + 1) * L] = M
        return out
    mats = [bd(C), bd(S), bd(C.T / L), bd(S.T / L)]
    tiles = np.zeros((8, 128, TILE_COLS), np.float32)
    k = 0
    for M in mats:
        Mr = M.reshape(8, 128, 1024)
        for h in range(2):
            tiles[k].reshape(128, 8, 512)[:] = Mr[:, :, h * 512:(h + 1) * 512].transpose(1, 0, 2)
            k += 1
    return tiles


class B:
    def __init__(self, nc, st, depth=DEPTH, stages=None, dbg=None):
        self.nc, self.st = nc, st
        self.depth = depth
        self.stages = stages or ('ffn0', 'hg', 'hy', 'gd', 'mla', 'ffn1')
        self.dbg = dbg or {}
        self.P = Prog(nc)
        self.bankctr = 0
        self.wtile = 0
        self.uid = 0

    def sb(self, name, shape, dt=F32):
        return self.st.enter_context(self.nc.sbuf_tensor(name, shape, dt))

    nbank = 8

    def bank(self):
        b = self.bankctr % self.nbank
        self.bankctr += 1
        return b

    def mm(self, out, lhsT, rhs, start, stop, r, w, sync_self=False):
        self.P.op('pe', lambda e: e.matmul(out, lhsT=lhsT, rhs=rhs, start=start, stop=stop), r, w, sync_self=sync_self)

    def tr(self, out, in_, ident, r, w):
        self.P.op('pe', lambda e: e.transpose(out, in_, ident), r, w, sync_self=False)

    def act(self, out, in_, func, r, w, bias=0.0, scale=1.0):
        self.P.op('act', lambda e: e.activation(out=out, in_=in_, func=func, bias=bias, scale=scale), r, w)

    def tt(self, eng, out, in0, in1, op, r, w):
        self.P.op(eng, lambda e: e.tensor_tensor(out=out, in0=in0, in1=in1, op=op), r, w)

    def ts(self, eng, out, in0, s1, s2, op0, op1, r, w):
        if s2 is None:
            self.P.op(eng, lambda e: e.tensor_scalar(out=out, in0=in0, scalar1=s1, scalar2=None, op0=op0), r, w)
        else:
            self.P.op(eng, lambda e: e.tensor_scalar(out=out, in0=in0, scalar1=s1, scalar2=s2, op0=op0, op1=op1), r, w)

    def stt(self, eng, out, in0, scalar, in1, op0, op1, r, w):
        self.P.op(eng, lambda e: e.scalar_tensor_tensor(out=out, in0=in0, scalar=scalar, in1=in1, op0=op0, op1=op1), r, w)

    def cp(self, eng, out, in_, r, w):
        if eng == 'act':
            self.P.op('act', lambda e: e.copy(out=out, in_=in_), r, w)
        else:
            self.P.op(eng, lambda e: e.tensor_copy(out=out, in_=in_), r, w)

    def recip(self, out, in_, r, w):
        self.P.op('dve', lambda e: e.reciprocal(out=out, in_=in_), r, w)

    def memset(self, eng, ap, val, w):
        self.P.op(eng, lambda e: e.memset(ap, val), [], w)

    def rsqrt(self, out, in_, scale, eps_ap, r, w):
        self.act(out, in_, AF.Ln, r + ['misc'], w, bias=eps_ap, scale=scale)
        self.act(out, out, AF.Exp, w, w, scale=-0.5)

    def arena_reset(self, keep16=0, keep32=0):
        self.P.barrier()
        self.a32 = keep32
        self.a16 = keep16
        self.uid += 1

    def f32(self, name, parts, cols):
        o = self.a32
        self.a32 += cols
        assert self.a32 <= self.A32, (name, self.a32)
        return self.ar32[0:parts, o:o + cols], f'a{self.uid}.{name}'

    def b16in32(self, name, parts, cols):
        o = self.a32
        self.a32 += cols // 2
        assert self.a32 <= self.A32, (name, self.a32)
        return self.ar32[0:parts, o:o + cols // 2].bitcast(BF16), f'a{self.uid}.{name}'

    def b16(self, name, parts, cols):
        o = self.a16
        self.a16 += cols
        assert self.a16 <= self.A16, (name, self.a16)
        return self.ar16[0:parts, o:o + cols], f'a{self.uid}.{name}'

    def prefetch(self, k):
        if not self.do_prefetch:
            return
        for _ in range(k):
            self.pref.append(self.next_wtile(_force=True))

    def next_wtile(self, src=None, f32=False, plain=False, _force=False):
        if src is None and self.pref and not _force:
            return self.pref.pop(0)
        if src is None:
            t = self.wtile
            self.wtile += 1
            src_ap = self.wstream[t]
        else:
            src_ap = src
        s = self.wslot_ctr % NWSLOT
        self.wslot_ctr += 1
        dst = self.wsl[:, s, :]
        rn = ('ws', s)
        if f32:
            dst = dst.bitcast(F32)
            self.P.dma('sp', lambda e: e.dma_start(out=dst, in_=src_ap), f'ws{s}', writes=[rn])
        elif plain:
            self.P.dma('sp', lambda e: e.dma_start(out=dst, in_=src_ap), f'ws{s}', writes=[rn])
        else:
            self.P.dma('pool', lambda e: e.dma_start(out=dst, in_=src_ap), f'ws{s}', writes=[rn])
        return dst, rn

    def build(self):
        nc, P = self.nc, self.P
        dr = lambda n, s, k: nc.dram_tensor(n, s, F32, kind=k).ap()
        self.xT_d = dr('xT', [8, 128, NT], 'ExternalInput')
        self.cw_d = dr('cw', [128, CW.ncols], 'ExternalInput')
        self.cb_d = dr('cb', [128, CB.ncols], 'ExternalInput')
        self.lw_d = dr('lw', [DEPTH, 128, LW.ncols], 'ExternalInput')
        self.wstream = dr('wstream', [DEPTH * TILES_PER_LAYER, 128, TILE_COLS], 'ExternalInput')
        self.wada_d = dr('wada', [DEPTH, 18, 128, TILE_COLS], 'ExternalInput')
        self.dft_d = nc.dram_tensor('dft', [8, 128, TILE_COLS], BF16, kind='ExternalInput').ap()
        self.ctxkv_d = dr('ctxkv', [DEPTH, 128, 256], 'ExternalInput')
        self.ctxkr_d = dr('ctxkr', [DEPTH, 32, 256], 'ExternalInput')
        self.hgst_d = dr('hgst', [DEPTH, 64, 2048], 'ExternalInput')
        self.gdst_d = dr('gdst', [DEPTH, 64, 2048], 'ExternalInput')
        self.yT_d = dr('yT', [8, 128, NT], 'ExternalOutput')
        self.ckvT_d = dr('ckvT', [DEPTH, 128, NT], 'ExternalOutput')
        self.krT_d = dr('krT', [DEPTH, 32, NT], 'ExternalOutput')
        self.hgo_d = dr('hgo', [DEPTH, 64, 2048], 'ExternalOutput')
        self.gdo_d = dr('gdo', [DEPTH, 64, 2048], 'ExternalOutput')
        self.dbg_d = {}
        for n, shp in self.dbg.items():
            self.dbg_d[n] = dr('dbg_' + n, list(shp), 'ExternalOutput')

        self.xT = self.sb('xT_sb', [128, 8, NT])
        self.h = self.sb('h_sb', [128, 8, NT], BF16)
        self.wsl = self.sb('wsl', [128, NWSLOT, TILE_COLS], BF16)
        self.cw = self.sb('cw_sb', [128, CW.ncols])
        self.lw = self.sb('lw_sb', [128, LW.ncols])
        self.A32, self.A16 = 18304, 24576
        self.ar32 = self.sb('ar32', [128, self.A32])
        self.ar16 = self.sb('ar16', [128, self.A16], BF16)
        self.misc = self.sb('misc', [128, 512])
        self.cb16 = self.sb('cb16', [128, 1024], BF16)
        self.ps = self.st.enter_context(nc.psum_tensor('ps', [128, 8, 512], F32))
        self.wslot_ctr = 0
        self.a32 = self.a16 = 0
        self.pref = []
        self.do_prefetch = (self.depth == DEPTH and all(x in self.stages for x in ('ffn0', 'hg', 'hy', 'gd', 'mla', 'ffn1')))

        self.ada = self.misc[:, 0:72]
        self.s2 = self.misc[:, 72:88].rearrange("p (k t) -> p k t", t=2)
        self.modA = self.misc[:, 96:120].rearrange("p (s k) -> p s k", k=8)
        self.modB = self.misc[:, 120:144].rearrange("p (s k) -> p s k", k=8)
        self.modG = self.misc[:, 144:168].rearrange("p (s k) -> p s k", k=8)
        self.epsc = self.misc[:, 168:169]
        self.eps6 = self.misc[:, 169:170]
        self.negpi = self.misc[:, 170:171]
        self.lball = self.misc[0:64, 176:208]
        self.adaraw = self.misc[:, 288:360]

        for kc in range(8):
            P.dma('sp', (lambda kc: lambda e: e.dma_start(out=self.xT[:, kc, :], in_=self.xT_d[kc]))(kc), 'init%d' % (kc % 2), writes=[('xT', kc)])
        P.dma('sp', lambda e: e.dma_start(out=self.cw[:], in_=self.cw_d), 'init', writes=['cw'])
        self.memset('dve', self.misc[:], 0.0, ['misc'])
        self.memset('dve', self.epsc, EPS, ['misc'])
        self.memset('dve', self.eps6, 1e-6, ['misc'])
        self.memset('dve', self.negpi, -math.pi, ['misc'])
        self.memset('dve', self.misc[:, 171:172], math.log(0.125), ['misc'])
        P.barrier()
        self.ones16 = self.cb16[:, 0:128]
        self.id16 = self.cb16[:, 128:256]
        self.cp('dve', self.ones16, self.c('ones'), ['cw'], ['cb16'])
        self.cp('dve', self.id16, self.c('ident'), ['cw'], ['cb16'])
        self.act(self.s2[:, :, 0], self.c('cond'), AF.Silu, ['cw'], ['misc'])
        self.cp('dve', self.s2[:, :, 1], self.s2[:, :, 0], ['misc'], ['misc'])
        self.s16 = self.cb16[:, 256:264]
        self.blk16 = self.cb16[:, 384:512]
        self.memset('dve', self.blk16, 0.0, ['cb16'])
        self.memset('dve', self.blk16[0:64, 0:64], 1.0, ['cb16'])
        self.memset('dve', self.blk16[64:128, 64:128], 1.0, ['cb16'])
        self.cp('dve', self.s16, self.s2[:, :, 0], ['misc'], ['cb16'])
        P.barrier()

        for l in range(self.depth):
            self.layer(l)

        P.barrier()
        for kc in range(8):
            P.dma('sp', (lambda kc: lambda e: e.dma_start(out=self.yT_d[kc], in_=self.xT[:, kc, :]))(kc), 'out%d' % (kc % 2), reads=[('xT', kc)])
        P.barrier()
        P.emit(self.st)

    def c(self, name):
        o, p, c = CW.items[name]
        return self.cw[0:p, o:o + c]

    def cbig(self, name, dst, dn):
        o, p, c = CB.items[name]
        self.P.dma('sp', lambda e: e.dma_start(out=dst, in_=self.cb_d[0:p, o:o + c]), 'cb', writes=[dn])

    def w(self, name):
        o, p, c = LW.items[name]
        return self.lw[0:p, o:o + c]

    def dump(self, name, ap, r):
        if name in self.dbg_d:
            self.P.dma('pool', lambda e: e.dma_start(out=self.dbg_d[name], in_=ap), 'dbg', reads=r)

    def layer(self, l):
        P = self.P
        self.arena_reset()
        P.dma('sp', lambda e: e.dma_start(out=self.lw[:], in_=self.lw_d[l]), 'lw', writes=['lw'])
        self.adaln(l)
        if l == 0:
            self.dump('misc', self.misc[:], ['misc', 'ada', 'mod'])
        if 'ffn0' in self.stages:
            self.ffn(l, 0)
        else:
            self.wtile += 19
        self.mixer(l)
        if 'ffn1' in self.stages:
            self.ffn(l, 2)
        else:
            self.wtile += 19
        if l == 0:
            self.dump('x_l0', self.xT[:], ['xT'])

    def ada_mm(self, l, tmp, tn):
        P = self.P
        identb = self.c('ident').unsqueeze(1).to_broadcast([128, 4, 128])
        for t in range(18):
            wt, wn = self.next_wtile(self.wada_d[l, t])
            wv = wt.rearrange("p (k c) -> p k c", c=512)
            b = self.bank()
            for kc in range(8):
                self.mm(self.ps[:, b, :], self.s16[:, kc:kc + 1].to_broadcast([128, 128]), wv[:, kc, :], kc == 0, kc == 7,
                        [wn, 'cb16'], [('ps', b)])
            tv = tmp[:, (t % 2) * 512:(t % 2 + 1) * 512].rearrange("p (j c) -> p j c", c=128)
            self.tt('dve', tv, self.ps[:, b, :].rearrange("p (j c) -> p j c", c=128), identb, ALU.mult, [('ps', b), 'cw'], [(tn, t % 2)])
            P.op('dve', (lambda tv, t: lambda e: e.tensor_reduce(out=self.adaraw[:, 4 * t:4 * t + 4], in_=tv, axis=mybir.AxisListType.X, op=ALU.add))(tv, t),
                 [(tn, t % 2)], ['adaraw'])
            yield

    def adaln(self, l):
        if l == 0 or 'mla' not in self.stages:
            tmp, tn = self.f32('ada_tmp', 128, 1024)
            for _ in self.ada_mm(l, tmp, tn):
                pass
        self.tt('dve', self.ada, self.adaraw, self.w('bada'), ALU.add, ['adaraw', 'lw'], ['ada'])
        adav = self.ada.rearrange("p (i k) -> p i k", k=8)
        gains = ['g_ffn0', 'g_mix', 'g_ffn1']
        for s in range(3):
            self.stt('dve', self.modA[:, s, :], adav[:, 3 * s + 1, :], 1.0, self.w(gains[s]), ALU.add, ALU.mult, ['ada', 'lw'], [('mod', s)])
            self.cp('dve', self.modB[:, s, :], adav[:, 3 * s, :], ['ada'], [('mod', s)])
            self.ts('dve', self.modG[:, s, :], adav[:, 3 * s + 2, :], 1.0 if s == 1 else 0.5, None, ALU.mult, None, ['ada'], [('mod', s)])

    def norm_mod(self, s):
        sq2 = [self.b16('sq0', 128, NT), self.b16('sq1', 128, NT)]
        rstd, rn = self.f32('rstd', 128, NT)
        tmp, tn = self.f32('nm_tmp', 128, 2 * NT)
        banks = [self.bank(), self.bank()]
        for kc in range(8):
            sq, sqn = sq2[kc % 2]
            self.act(sq, self.xT[:, kc, :], AF.Square, [('xT', kc)], [sqn])
            for blk in range(2):
                self.mm(self.ps[:, banks[blk], :], self.ones16, sq[:, blk * 512:(blk + 1) * 512], kc == 0, kc == 7,
                        [sqn, 'cb16'], [('ps', banks[blk])])
        for blk in range(2):
            self.rsqrt(rstd[:, blk * 512:(blk + 1) * 512], self.ps[:, banks[blk], :], 1.0 / D, self.epsc,
                       [('ps', banks[blk]), 'misc'], [(rn, blk)])
        for kc in range(8):
            t = tmp[:, (kc % 2) * NT:(kc % 2 + 1) * NT]
            self.stt('dve', t, self.xT[:, kc, :], self.modA[:, s, kc:kc + 1], rstd, ALU.mult, ALU.mult,
                     [('xT', kc), ('mod', s), rn], [(tn, kc % 2)])
            self.act(self.h[:, kc, :], t, AF.Identity, [(tn, kc % 2), ('mod', s)], [('h', kc)], bias=self.modB[:, s, kc:kc + 1])

    def ffn(self, l, s):
        self.arena_reset()
        self.norm_mod(s)
        if l == 0 and s == 0:
            self.dump('h0', self.h[:], ['h'])
        actb, an = self.b16('ffn_act', 128, NF * NT)
        actv = actb.rearrange("p (f t) -> p f t", t=NT)
        sg, sgn = self.f32('ffn_sg', 128, 4 * 512)
        k = 0
        for g in range(11):
            wt, wn = self.next_wtile()
            wv = wt.rearrange("p (k c) -> p k c", c=512)
            for ff in range(2):
                f = 2 * g + ff
                for blk in range(2):
                    bg, bu = self.bank(), self.bank()
                    for kc in range(8):
                        self.mm(self.ps[:, bg, :], wv[:, kc, ff * 128:(ff + 1) * 128], self.h[:, kc, blk * 512:(blk + 1) * 512],
                                kc == 0, kc == 7, [wn, ('h', kc)], [('ps', bg)])
                    for kc in range(8):
                        self.mm(self.ps[:, bu, :], wv[:, kc, 256 + ff * 128:256 + (ff + 1) * 128], self.h[:, kc, blk * 512:(blk + 1) * 512],
                                kc == 0, kc == 7, [wn, ('h', kc)], [('ps', bu)])
                    sgt = sg[:, (k % 4) * 512:(k % 4 + 1) * 512]
                    self.act(sgt, self.ps[:, bg, :], AF.Silu, [('ps', bg)], [(sgn, k % 4)])
                    self.tt('dve', actv[:, f, blk * 512:(blk + 1) * 512], sgt, self.ps[:, bu, :], ALU.mult,
                            [(sgn, k % 4), ('ps', bu)], [(an, f, blk)])
                    k += 1
        if l == 0 and s == 0:
            self.dump('act0', actv, [an])
        for dc in range(8):
            wt, wn = self.next_wtile()
            wv = wt[:, 0:NF * 128].rearrange("p (f c) -> p f c", c=128)
            for blk in range(2):
                b = self.bank()
                for f in range(NF):
                    self.mm(self.ps[:, b, :], wv[:, f, :], actv[:, f, blk * 512:(blk + 1) * 512], f == 0, f == NF - 1,
                            [wn, (an, f, blk)], [('ps', b)])
                xs = self.xT[:, dc, blk * 512:(blk + 1) * 512]
                self.stt('dve', xs, self.ps[:, b, :], self.modG[:, s, dc:dc + 1], xs, ALU.mult, ALU.add,
                         [('ps', b), ('mod', s), ('xT', dc)], [('xT', dc)])
        if s == 0:
            self.prefetch(3)
        elif l + 1 < self.depth:
            self.prefetch(2)

    def mixer(self, l):
        self.keep16 = 8 * NT
        self.arena_reset(self.keep16)
        self.norm_mod(1)
        self.omix = self.ar16[:, 0:8 * NT].rearrange("p (k t) -> p k t", t=NT)
        self.memset('pool', self.ar16[:, 0:8 * NT], 0.0, ['omix'])
        if 'hg' in self.stages:
            self.hgrn(l)
        else:
            self.wtile += 3
        if 'hy' in self.stages:
            self.hyena(l)
        else:
            self.wtile += 2
        if 'gd' in self.stages:
            self.gdn(l)
        else:
            self.wtile += 2
            self.arena_reset(self.keep16)
            self.abmla = self.next_wtile()
        if 'mla' in self.stages:
            g_mla = self.mla(l)
            next(g_mla)
            alive = [g_mla]
            if l + 1 < self.depth:
                alive.append(self.ada_mm(l + 1, *self.ada_tmp))
            while alive:
                for gg in list(alive):
                    try:
                        next(gg)
                    except StopIteration:
                        alive.remove(gg)
            self.prefetch(2)
        self.arena_reset(self.keep16)
        for hh in range(2):
            wt, wn = self.next_wtile()
            wv = wt.rearrange("p (k c) -> p k c", c=512)
            for dcl in range(4):
                dc = hh * 4 + dcl
                for blk in range(2):
                    b = self.bank()
                    for kc in range(8):
                        self.mm(self.ps[:, b, :], wv[:, kc, dcl * 128:(dcl + 1) * 128], self.omix[:, kc, blk * 512:(blk + 1) * 512],
                                kc == 0, kc == 7, [wn, 'omix'], [('ps', b)])
                    xs = self.xT[:, dc, blk * 512:(blk + 1) * 512]
                    self.stt('dve', xs, self.ps[:, b, :], self.modG[:, 1, dc:dc + 1], xs, ALU.mult, ALU.add,
                             [('ps', b), ('mod', 1), ('xT', dc)], [('xT', dc)])
        self.prefetch(2)

    def proj(self, wv, wn, c0, m, dst, dn, scale=None):
        if m == 64:
            c1 = min(c0, 384)
            off, mm_ = c0 - c1, 128
        else:
            c1, off, mm_ = c0, 0, m
        dn = (dn,) if isinstance(dn, str) else tuple(dn)
        for blk in range(2):
            b = self.bank()
            for kc in range(8):
                self.mm(self.ps[0:mm_, b, :], wv[:, kc, c1:c1 + mm_], self.h[:, kc, blk * 512:(blk + 1) * 512], kc == 0, kc == 7,
                        [wn, ('h', kc)], [('ps', b)])
            self.cp('act' if blk == 0 else 'dve', dst[0:m, blk * 512:(blk + 1) * 512], self.ps[off:off + m, b, :], [('ps', b)], [dn + (blk,)])

    def hgrn_lb(self):
        raw = self.w('hg_lbraw')
        e = self.misc[0:64, 240:272]
        ssum = self.misc[0:64, 272:280]
        self.act(e, raw, AF.Exp, ['lw'], ['lbe'])
        self.tt('dve', ssum, e[:, 0:8], e[:, 8:16], ALU.add, ['lbe'], ['lbs'])
        self.tt('dve', ssum, ssum, e[:, 16:24], ALU.add, ['lbe', 'lbs'], ['lbs'])
        self.tt('dve', ssum, ssum, e[:, 24:32], ALU.add, ['lbe', 'lbs'], ['lbs'])
        self.recip(ssum, ssum, ['lbs'], ['lbs'])
        lb = self.lball
        self.memset('dve', lb[:, 0:8], 0.0, ['lb'])
        for li in range(1, 4):
            self.tt('dve', e[:, li * 8:(li + 1) * 8], e[:, li * 8:(li + 1) * 8], ssum, ALU.mult, ['lbe', 'lbs'], ['lbe'])
            self.tt('dve', lb[:, li * 8:(li + 1) * 8], lb[:, (li - 1) * 8:li * 8], e[:, li * 8:(li + 1) * 8], ALU.add, ['lbe', 'lb'], ['lb'])
        self.ts('dve', self.misc[0:64, 208:240], lb, -1.0, 1.0, ALU.mult, ALU.add, ['lb'], ['lb'])

    def hgrn(self, l):
        P = self.P
        self.arena_reset(self.keep16)
        lb2 = self.misc[:, 360:376]
        oml2 = self.misc[:, 376:392]
        if l == 0:
            self.hgrn_lb()
            for hl in range(2):
                for (dst, src) in ((lb2, self.lball), (oml2, self.misc[0:64, 208:240])):
                    self.cp('dve', dst[hl * 64:(hl + 1) * 64, :].rearrange("k (a p) -> k a p", p=2),
                            src.rearrange("k (a p h) -> k a p h", p=2, h=2)[:, :, :, hl], ['lb'], ['lb2'])
        tiles = [self.next_wtile() for _ in range(3)]
        tv = [(t.rearrange("p (k c) -> p k c", c=512), n) for t, n in tiles]
        st, stn = self.f32('hg_st', 128, 1024)
        stv = st.rearrange("p (s d q v) -> p s d q v", s=4, d=2, q=2)
        for hl in range(2):
            srcv = self.hgst_d[l].rearrange("k (s d q h v) -> k s d q h v", s=4, d=2, q=2, h=2)[:, :, :, :, hl, :]
            P.dma('sp', (lambda hl, srcv: lambda e: e.dma_start(out=stv[hl * 64:(hl + 1) * 64], in_=srcv))(hl, srcv), 'hgin', writes=[stn])
        gn2, gn2n = self.f32('hg_gn2', 128, 2)
        for hl in range(2):
            self.cp('dve', gn2[hl * 64:(hl + 1) * 64, :], self.w('hg_gn').rearrange("k (q h) -> k q h", h=2)[:, :, hl], ['lw'], [gn2n])
        q, qn = self.f32('hg_q', 128, NT)
        g2 = [self.f32('hg_g0', 128, NT), self.f32('hg_g1', 128, NT)]
        z, zn = self.f32('hg_z', 128, NT)
        KK, kkn = self.f32('hg_kk', 128, NT)
        Pz, pzn = self.f32('hg_pz', 128, NT + 32)
        E, en = self.f32('hg_e', 128, NT)
        X, xn = self.f32('hg_x', 128, NT)
        og, ogn = self.f32('hg_og', 128, NT)
        rsB, rsBn = self.f32('hg_rsB', 128, NT)
        Dm2 = [self.f32('hg_D0', 128, 2048), self.f32('hg_D1', 128, 2048)]
        dd2 = [self.f32('hg_dd0', 128, 64), self.f32('hg_dd1', 128, 64)]
        S, sn = self.f32('hg_S', 128, 64)
        sqB, sqBn = self.b16in32('hg_sqB', 128, NT)
        qe2 = [self.b16('hg_qe0', 128, NT), self.b16('hg_qe1', 128, NT)]
        ke, ken = self.b16('hg_ke', 128, NT)
        ko, kon = self.b16('hg_ko', 128, NT)
        kt, ktn = self.b16('hg_kt', 32, 2048)
        vt, vtn = self.b16('hg_vt', 32, 4096)
        AT2 = [self.b16('hg_AT0', 32, 2 * NT), self.b16('hg_AT1', 32, 2 * NT)]
        Sa, san = self.b16('hg_Sa', 128, 2048)
        ktv = kt.rearrange("p (n k) -> p n k", k=64)
        vtv = vt.rearrange("p (n c) -> p n c", c=128)
        Sav = Sa.rearrange("p (n v) -> p n v", v=64)
        ones_b = self.c('ones')[:, 0:1].to_broadcast([128, NT])
        self.memset('dve', Pz[:, 0:1], 0.0, [pzn])
        self.memset('dve', Pz[:, NT + 1:NT + 32], 0.0, [pzn])
        m32 = self.c('m32')

        def chunkcol(ap, off):
            return ap[:, off:off + NT].rearrange("p (n c) -> p n c", c=32)[:, :, 0]

        def bc(ap):
            return ap.unsqueeze(2).to_broadcast([128, 32, 32])

        v3 = lambda ap: ap.rearrange("p (n c) -> p n c", c=32)

        def front(u):
            pr, d = u // 2, u % 2
            qe, qen = qe2[d]
            AT, atn = AT2[d]
            Dm, dmn = Dm2[d]
            dd, ddn = dd2[d]
            g, gn = g2[pr]
            ATv = AT.rearrange("p (h n i) -> p h n i", h=2, i=32)
            if d == 0:
                self.proj(tv[0][0], tv[0][1], pr * 128, 128, q, qn)
                yield
                self.proj(tv[1][0], tv[1][1], pr * 128, 128, g, gn)
                self.act(g, g, AF.Silu, [gn], [gn])
                yield
                for n4 in range(8):
                    b = self.bank()
                    for nn in range(4):
                        n = n4 * 4 + nn
                        for kc in range(8):
                            self.mm(self.ps[0:32, b, nn * 128:(nn + 1) * 128], self.h[:, kc, n * 32:(n + 1) * 32], tv[0][0][:, kc, 256 + pr * 128:256 + (pr + 1) * 128],
                                    kc == 0, kc == 7, [tv[0][1], ('h', kc)], [('ps', b)])
                    self.cp('act', vt[:, n4 * 512:(n4 + 1) * 512], self.ps[0:32, b, :], [('ps', b)], [(vtn, n4)])
                    yield
            col = l * 4 + d * 2 + pr
            if d == 0:
                self.proj(tv[1][0], tv[1][1], 256 + pr * 128, 128, z, zn)
            else:
                self.proj(tv[2][0], tv[2][1], pr * 128, 128, z, zn)
            yield
            F, fn = z, zn
            self.act(F, z, AF.Sigmoid, [zn], [fn])
            self.ts('dve', F, F, oml2[:, col:col + 1], lb2[:, col:col + 1], ALU.mult, ALU.add, [fn, 'lb2'], [fn])
            self.act(KK, F, AF.Identity, [fn], [kkn], bias=1.0, scale=-1.0)
            self.act(F, F, AF.Ln, [fn], [fn])
            yield
            P.op('dve', lambda e: e.tensor_tensor_scan(out=Pz[:, 1:NT + 1], data0=ones_b, data1=F, initial=0.0, op0=ALU.mult, op1=ALU.add),
                 [fn, 'cw'], [pzn])
            V = Pz[:, 1:NT + 1] if d == 0 else Pz[:, 0:NT]
            A_ = chunkcol(Pz, 16)
            P0 = chunkcol(Pz, 0)
            Pl = chunkcol(Pz, 32)
            sgn = 1.0 if d == 0 else -1.0
            self.tt('pool', v3(E), v3(V), bc(A_), ALU.subtract, [pzn], [en])
            self.ts('pool', E, E, 80.0, -80.0, ALU.min, ALU.max, [en], [en])
            yield
            self.act(X, E, AF.Exp, [en], [xn], scale=sgn, bias=self.misc[:, 171:172])
            self.tt('pool', qe, q, X, ALU.mult, [qn, xn], [qen])
            self.act(X, E, AF.Exp, [en], [xn], scale=-sgn)
            self.tt('pool', ke, KK, X, ALU.mult, [kkn, xn], [ken])
            yield
            self.tt('pool', v3(E), v3(V), bc(Pl if d == 0 else P0), ALU.subtract, [pzn], [en])
            self.act(X, E, AF.Exp, [en], [xn], scale=-sgn)
            self.tt('pool', ko, KK, X, ALU.mult, [kkn, xn], [kon])
            if d == 0:
                self.tt('pool', dd[:, 0:32], A_, P0, ALU.subtract, [pzn], [ddn])
            else:
                self.tt('pool', dd[:, 0:32], Pl, A_, ALU.subtract, [pzn], [ddn])
            self.tt('pool', dd[:, 32:64], Pl, P0, ALU.subtract, [pzn], [ddn])
            self.act(dd, dd, AF.Exp, [ddn], [ddn])
            yield
            for hl in range(2):
                ph = slice(hl * 64, (hl + 1) * 64)
                for half in range(2):
                    b = self.bank()
                    pb = self.ps[:, b, :].bitcast(BF16)
                    for nn in range(16):
                        n = half * 16 + nn
                        self.tr(pb[0:32, nn * 64:(nn + 1) * 64], ko[ph, n * 32:(n + 1) * 32], self.id16[ph, ph], [kon, 'cb16'], [('ps', b)])
                    self.cp('act', kt[:, half * 1024:(half + 1) * 1024], pb[0:32, :], [('ps', b)], [(ktn, half)])
                    yield
                for half in range(2):
                    b = self.bank()
                    for nn in range(16):
                        n = half * 16 + nn
                        self.mm(self.ps[0:32, b, nn * 32:(nn + 1) * 32], ke[ph, n * 32:(n + 1) * 32], qe[ph, n * 32:(n + 1) * 32], True, True, [ken, qen], [('ps', b)])
                    self.tt('dve', ATv[:, hl, half * 16:(half + 1) * 16, :], self.ps[0:32, b, :].rearrange("p (n i) -> p n i", i=32),
                            m32[:, d * 32:(d + 1) * 32].unsqueeze(1).to_broadcast([32, 16, 32]), ALU.mult, [('ps', b), 'cw'], [(atn, hl, half)])
                    yield
                for n8 in range(4):
                    b = self.bank()
                    for nn in range(8):
                        n = n8 * 8 + nn
                        self.mm(self.ps[ph, b, nn * 64:(nn + 1) * 64], ktv[:, n, :], vtv[:, n, hl * 64:(hl + 1) * 64], True, True, [ktn, vtn], [('ps', b)])
                    self.cp('act', Dm[ph, n8 * 512:(n8 + 1) * 512], self.ps[ph, b, :], [('ps', b)], [(dmn, hl, n8)])
                    yield

        def back(u):
            pr, d = u // 2, u % 2
            qe, qen = qe2[d]
            AT, atn = AT2[d]
            Dm, dmn = Dm2[d]
            dd, ddn = dd2[d]
            g, gn = g2[pr]
            ATv = AT.rearrange("p (h n i) -> p h n i", h=2, i=32)
            Dv = Dm.rearrange("p (n v) -> p n v", v=64)
            self.memset('dve', S, 0.0, [sn])
            order = range(32) if d == 0 else range(31, -1, -1)
            for n in order:
                seg = n // 8
                first = (n % 8 == 0) if d == 0 else (n % 8 == 7)
                last = (n % 8 == 7) if d == 0 else (n % 8 == 0)
                if first:
                    self.stt('dve', S, S, self.c('carry'), stv[:, seg, d, pr, :], ALU.mult, ALU.add, [sn, 'cw', (stn, seg, d, pr)], [sn])
                self.ts('dve', Sav[:, n, :], S, dd[:, n:n + 1], None, ALU.mult, None, [sn, ddn], [(san, n)])
                self.stt('dve', S, S, dd[:, 32 + n:33 + n], Dv[:, n, :], ALU.mult, ALU.add, [sn, ddn, dmn], [sn])
                if last:
                    self.cp('dve', stv[:, seg, d, pr, :], S, [sn], [(stn, seg, d, pr)])
                if n % 4 == 3:
                    yield
            for half in range(2):
                b = self.bank()
                for hl in range(2):
                    ph = slice(hl * 64, (hl + 1) * 64)
                    for nn in range(16):
                        n = half * 16 + nn
                        cs = slice(n * 32, (n + 1) * 32)
                        self.mm(self.ps[ph, b, nn * 32:(nn + 1) * 32], Sav[ph, n, :], qe[ph, cs], nn == 0, False, [(san, n), qen], [('ps', b)])
                    for nn in range(16):
                        n = half * 16 + nn
                        self.mm(self.ps[ph, b, nn * 32:(nn + 1) * 32], vtv[:, n, hl * 64:(hl + 1) * 64], ATv[:, hl, n, :], False, nn == 15, [vtn, (atn, hl, half)], [('ps', b)])
                osl = og[:, half * 512:(half + 1) * 512]
                if d == 0:
                    self.cp('act', osl, self.ps[:, b, :], [('ps', b)], [(ogn, half)])
                else:
                    self.tt('dve', osl, osl, self.ps[:, b, :], ALU.add, [(ogn, half), ('ps', b)], [(ogn, half)])
                yield
            if d == 1:
                self.rms_fm(og, ogn, 128, NT, sqB, sqBn, rsB, rsBn, 1.0 / 64, ones=self.blk16)
                self.stt('dve', og, og, gn2[:, pr:pr + 1], rsB, ALU.mult, ALU.mult, [ogn, gn2n, rsBn], [ogn])
                self.tt('dve', self.omix[:, pr, :], og, g, ALU.mult, [ogn, gn], ['omix'])
                yield

        def run(gen):
            for _ in gen:
                pass

        def interleave(a, b_):
            alive = [a, b_]
            while alive:
                for gg in list(alive):
                    try:
                        next(gg)
                    except StopIteration:
                        alive.remove(gg)

        run(front(0))
        interleave(back(0), front(1))
        run(back(1))
        run(front(2))
        interleave(back(2), front(3))
        run(back(3))
        for hl in range(2):
            dstv = self.hgo_d[l].rearrange("k (s d q h v) -> k s d q h v", s=4, d=2, q=2, h=2)[:, :, :, :, hl, :]
            P.dma('sp', (lambda hl, dstv: lambda e: e.dma_start(out=dstv, in_=stv[hl * 64:(hl + 1) * 64]))(hl, dstv), 'hgo', reads=[stn])
        self.prefetch(2)

    def sin_rr(self, out, arg, tmp, r, w, tn):
        self.ts('dve', tmp, arg, 1.0 / (2 * math.pi), MAGIC, ALU.mult, ALU.add, r, [tn])
        self.ts('dve', tmp, tmp, MAGIC, None, ALU.subtract, None, [tn], [tn])
        self.stt('dve', tmp, tmp, -2.0 * math.pi, arg, ALU.mult, ALU.add, [tn] + r, [tn])
        self.act(out, tmp, AF.Sin, [tn], w)

    def hyena(self, l):
        P = self.P
        self.arena_reset(self.keep16)
        zemb, zn = self.f32('zemb', 33, NT)
        winf, wfn = self.f32('winf', 128, 2048)
        winb, wbn = self.f32('winb', 128, 2048)
        self.cbig('zemb', zemb, zn)
        self.cbig('winf', winf, wfn)
        self.cbig('winb', winb, wbn)
        a1, a1n = self.f32('hy_a1', 64, NT)
        t1, t1n = self.f32('hy_t1', 64, NT)
        h1, h1n = self.f32('hy_h1', 64, NT)
        sc, scn = self.f32('hy_sc', 64, 2)
        hs, hsn = self.b16('hy_hs', 128, 2048)
        hd, hdn = self.b16('hy_hd', 128, 2048)
        assert self.a16 == self.keep16 + 4096
        hsv = hs.rearrange("p (j c) -> p j c", c=256)
        hdv = hd.rearrange("p (j c) -> p j c", c=256)
        self.tt('dve', sc[:, 0:1], self.w('hy_fr'), self.w('hy_b1'), ALU.mult, ['lw'], [scn])
        self.tt('dve', sc[:, 1:2], self.w('hy_fr'), self.w('hy_b2'), ALU.mult, ['lw'], [scn])
        src, srcn = zemb, zn
        for li, (wname, kdim) in enumerate((('hy_w1', 33), ('hy_w2', 64))):
            for blk in range(2):
                b = self.bank()
                self.mm(self.ps[0:64, b, :], self.w(wname), src[0:kdim, blk * 512:(blk + 1) * 512], True, True, ['lw', srcn], [('ps', b)])
                self.act(a1[:, blk * 512:(blk + 1) * 512], self.ps[0:64, b, :], AF.Identity, [('ps', b), 'lw', scn], [(a1n, blk)],
                         bias=sc[:, li:li + 1], scale=self.w('hy_fr'))
            dst, dstn = (h1, h1n) if li == 0 else (a1, a1n)
            self.sin_rr(dst, a1, t1, [a1n], [dstn], t1n)
            src, srcn = dst, dstn
        h2, h2n = src, srcn
        tf, tfn = self.f32('hy_tf', 128, 512)
        for pc in range(8):
            b = self.bank()
            self.mm(self.ps[:, b, :], h2[:, pc * 128:(pc + 1) * 128], self.w('hy_w3'), True, True, [h2n, 'lw'], [('ps', b)])
            self.tt('dve', tf[:, 0:256], self.ps[:, b, 0:256], winf[:, pc * 256:(pc + 1) * 256], ALU.mult, [('ps', b), wfn], [tfn])
            self.tt('dve', tf[:, 256:512], self.ps[:, b, 256:512], winb[:, pc * 256:(pc + 1) * 256], ALU.mult, [('ps', b), wbn], [tfn])
            self.tt('dve', hsv[:, pc, :], tf[:, 0:256], tf[:, 256:512], ALU.add, [tfn], [(hsn, pc)])
            self.tt('dve', hdv[:, pc, :], tf[:, 256:512], tf[:, 0:256], ALU.subtract, [tfn], [(hdn, pc)])
        self.arena_reset(self.keep16 + 4096)
        hsn, hdn = 'hyhs', 'hyhd'
        hsv = self.ar16[:, self.keep16:self.keep16 + 2048].rearrange("p (j c) -> p j c", c=256)
        hdv = self.ar16[:, self.keep16 + 2048:self.keep16 + 4096].rearrange("p (j c) -> p j c", c=256)
        U, un = self.f32('hy_u', 128, 6 * NT)
        Cc, ccn = self.f32('hy_c', 128, 3 * NT)
        t1, n1 = self.next_wtile()
        t2, n2 = self.next_wtile()
        v1 = t1.rearrange("p (k c) -> p k c", c=512)
        v2 = t2.rearrange("p (k c) -> p k c", c=512)
        for g in range(6):
            wv, wn, c0 = (v1, n1, g * 128) if g < 4 else (v2, n2, (g - 4) * 128)
            self.proj(wv, wn, c0, 128, U[:, g * NT:(g + 1) * NT], (un, g))
        nw, nwn = self.f32('hy_nw', 128, 18)
        self.ts('dve', nw, self.w('hy_cw'), self.c('ncarry'), None, ALU.mult, None, ['lw', 'cw'], [nwn])
        cwv = self.w('hy_cw').rearrange("p (g k) -> p g k", k=3)
        nwv = nw.rearrange("p (g k) -> p g k", k=3)
        zb, zbn = self.b16('hy_zb', 128, 2 * NT)
        zT, ztn = self.b16('hy_zT', 128, 2048)
        zTv = zT.rearrange("p (t c) -> p t c", c=256)

        def conv(g, dst, dn):
            x = U[:, g * NT:(g + 1) * NT]
            xn = (un, g)
            self.act(dst, x, AF.Identity, [xn, 'lw'], [dn], bias=self.w('hy_cb')[:, g:g + 1], scale=cwv[:, g, 1:2])
            self.stt('dve', dst[:, 1:NT], x[:, 0:NT - 1], cwv[:, g, 0:1], dst[:, 1:NT], ALU.mult, ALU.add, [xn, 'lw', dn], [dn])
            self.stt('dve', dst[:, 0:NT - 1], x[:, 1:NT], cwv[:, g, 2:3], dst[:, 0:NT - 1], ALU.mult, ALU.add, [xn, 'lw', dn], [dn])
            xs = x.rearrange("p (s t) -> p s t", t=256)
            ds = dst.rearrange("p (s t) -> p s t", t=256)
            self.stt('dve', ds[:, 1:4, 0], xs[:, 0:3, 255], nwv[:, g, 0:1], ds[:, 1:4, 0], ALU.mult, ALU.add, [xn, nwn, dn], [dn])
            self.stt('dve', ds[:, 0:3, 255], xs[:, 1:4, 0], nwv[:, g, 2:3], ds[:, 0:3, 255], ALU.mult, ALU.add, [xn, nwn, dn], [dn])

        for cc in range(2):
            conv(cc, Cc[:, cc * NT:(cc + 1) * NT], (ccn, cc))
        for cc in range(2):
            sc2 = Cc[:, 2 * NT:3 * NT]
            conv(2 + cc, sc2, (ccn, 2))
            conv(4 + cc, U[:, cc * NT:(cc + 1) * NT], (un, cc))
            self.tt('dve', U[:, (2 + cc) * NT:(3 + cc) * NT], sc2, U[:, cc * NT:(cc + 1) * NT], ALU.mult, [(ccn, 2), (un, cc), (un, 2 + cc)], [(un, 2 + cc)])
            self.cp('act', zb[:, cc * NT:(cc + 1) * NT], U[:, (2 + cc) * NT:(3 + cc) * NT], [(un, 2 + cc)], [(zbn, cc)])
        for tc in range(8):
            b = self.bank()
            pb = self.ps[:, b, :].bitcast(BF16)
            for cc in range(2):
                self.tr(pb[:, cc * 128:(cc + 1) * 128], zb[:, cc * NT + tc * 128:cc * NT + (tc + 1) * 128], self.id16, [(zbn, cc), 'cb16'], [('ps', b)])
            self.cp('dve' if tc % 2 else 'act', zTv[:, tc, :], pb[:, 0:256], [('ps', b)], [(ztn, tc)])
        Y, yn = self.b16('hy_Y', 128, 4096)
        Yv = Y.rearrange("p (r f c) -> p r f c", r=2, c=256)
        zz, zzn = self.f32('hy_zz', 128, 1024)
        for half in range(2):
            ct, cn = self.next_wtile(self.dft_d[half], plain=True)
            stl, sn = self.next_wtile(self.dft_d[2 + half], plain=True)
            cv = ct.rearrange("p (k c) -> p k c", c=512)
            sv = stl.rearrange("p (k c) -> p k c", c=512)
            for fcl in range(4):
                fc = half * 4 + fcl
                b = self.bank()
                for tc in range(8):
                    self.mm(self.ps[:, b, 0:256], cv[:, tc, fcl * 128:(fcl + 1) * 128], zTv[:, tc, :], tc == 0, tc == 7, [cn, ztn], [('ps', b)])
                b2 = self.bank()
                for tc in range(8):
                    self.mm(self.ps[:, b2, 0:256], sv[:, tc, fcl * 128:(fcl + 1) * 128], zTv[:, tc, :], tc == 0, tc == 7, [sn, ztn], [('ps', b2)])
                bk1 = self.bank()
                for jc in range(8):
                    self.mm(self.ps[:, bk1, 0:256], cv[:, jc, fcl * 128:(fcl + 1) * 128], hsv[:, jc, :], jc == 0, jc == 7, [cn, hsn], [('ps', bk1)])
                bk2 = self.bank()
                for jc in range(8):
                    self.mm(self.ps[:, bk2, 0:256], sv[:, jc, fcl * 128:(fcl + 1) * 128], hdv[:, jc, :], jc == 0, jc == 7, [sn, hdn], [('ps', bk2)])
                kre, kim = self.ps[:, bk1, 0:256], self.ps[:, bk2, 0:256]
                self.cp('act', zz[:, 0:256], self.ps[:, b, 0:256], [('ps', b)], [zzn])
                self.cp('act', zz[:, 256:512], self.ps[:, b2, 0:256], [('ps', b2)], [zzn])
                self.tt('dve', zz[:, 512:768], zz[:, 0:256], kre, ALU.mult, [zzn, ('ps', bk1)], [zzn])
                self.tt('dve', zz[:, 768:1024], zz[:, 256:512], kim, ALU.mult, [zzn, ('ps', bk2)], [zzn])
                self.tt('dve', Yv[:, 0, fc, :], zz[:, 512:768], zz[:, 768:1024], ALU.add, [zzn], [(yn, 0, fc)])
                self.tt('dve', zz[:, 512:768], zz[:, 256:512], kre, ALU.mult, [zzn, ('ps', bk1)], [zzn])
                self.tt('dve', zz[:, 768:1024], zz[:, 0:256], kim, ALU.mult, [zzn, ('ps', bk2)], [zzn])
                self.tt('dve', Yv[:, 1, fc, :], zz[:, 512:768], zz[:, 768:1024], ALU.subtract, [zzn], [(yn, 1, fc)])
        for blk in range(2):
            ct, cn = self.next_wtile(self.dft_d[4 + blk], plain=True)
            stl, sn = self.next_wtile(self.dft_d[6 + blk], plain=True)
            cv = ct.rearrange("p (k c) -> p k c", c=512)
            sv = stl.rearrange("p (k c) -> p k c", c=512)
            for cc in range(2):
                b = self.bank()
                for fc in range(8):
                    self.mm(self.ps[:, b, :], Yv[:, 0, fc, cc * 128:(cc + 1) * 128], cv[:, fc, :], fc == 0, False, [cn, yn], [('ps', b)])
                    self.mm(self.ps[:, b, :], Yv[:, 1, fc, cc * 128:(cc + 1) * 128], sv[:, fc, :], False, fc == 7, [sn, yn], [('ps', b)])
                zsl = U[:, (2 + cc) * NT + blk * 512:(2 + cc) * NT + (blk + 1) * 512]
                tmp = zz[:, 0:512]
                self.stt('dve', tmp, zsl, self.w('hy_skip')[:, cc:cc + 1], self.ps[:, b, :], ALU.mult, ALU.add, [(un, 2 + cc), 'lw', ('ps', b)], [zzn])
                self.tt('dve', self.omix[:, 2 + cc, blk * 512:(blk + 1) * 512], tmp, Cc[:, cc * NT + blk * 512:cc * NT + (blk + 1) * 512], ALU.mult,
                        [zzn, (ccn, cc)], ['omix'])
        self.prefetch(3)

    def gdn(self, l):
        P = self.P
        self.arena_reset(self.keep16)
        tq, tqn = self.next_wtile()
        tvt, tvn = self.next_wtile()
        tab, tabn = self.next_wtile()
        self.abmla = (tab, tabn)
        tqv = tq.rearrange("p (k c) -> p k c", c=512)
        tvv = tvt.rearrange("p (k c) -> p k c", c=512)
        tabv = tab.rearrange("p (k c) -> p k c", c=512)
        G, gn_ = self.f32('gd_G', 16, NT)
        X1, x1n = self.f32('gd_X1', 16, NT)
        BE, ben = self.f32('gd_BE', 16, NT)
        tok, tokn = self.f32('gd_tok', 64, 1280)
        tokv = tok.rearrange("p (q n r) -> p q n r", q=5, r=16)
        R16, r16n = self.f32('gd_R16', 16, NT)
        LA, lan = self.f32('gd_LA', 16, NT)
        Pz, pzn = self.f32('gd_Pz', 16, NT + 64)
        X2, x2n = self.f32('gd_X2', 16, NT)
        X3, x3n = self.f32('gd_X3', 16, NT)
        tF, tfn = self.f32('gd_tF', 16, NT)
        tB, tbn = self.f32('gd_tB', 16, NT)
        nega, ngn = self.f32('gd_nega', 16, 2)
        mf, mb = self.c('mf'), self.c('mb')
        self.proj(tabv, tabn, 0, 16, R16, r16n)
        self.act(BE, R16, AF.Sigmoid, [r16n], [ben])
        self.act(LA, R16, AF.Exp, [r16n, 'lw'], [lan], bias=self.w('gd_dtb'))
        self.act(LA, LA, AF.Ln, [lan], [lan], bias=1.0)
        self.act(nega[:, 0:1], self.w('gd_alog'), AF.Exp, ['lw'], [ngn])
        self.ts('dve', nega[:, 1:2], nega[:, 0:1], -1.0, None, ALU.mult, None, [ngn], [ngn])
        self.ts('dve', LA, LA, nega[:, 1:2], None, ALU.mult, None, [lan, ngn], [lan])
        self.memset('dve', Pz, 0.0, [pzn])
        ones_b = self.c('ones')[0:16, 0:1].to_broadcast([16, NT])
        P.op('dve', lambda e: e.tensor_tensor_scan(out=Pz[:, 1:NT + 1], data0=ones_b, data1=LA, initial=0.0, op0=ALU.mult, op1=ALU.add),
             [lan, 'cw', pzn], [pzn])
        V = Pz[:, 1:NT + 1]
        W = Pz[:, 0:NT]
        c3 = lambda ap: ap.rearrange("p (n c) -> p n c", c=64)
        bcc = lambda ap: ap.unsqueeze(2).to_broadcast([16, 16, 64])
        P0 = Pz[:, 0:NT].rearrange("p (n c) -> p n c", c=64)[:, :, 0]
        Pl = Pz[:, 64:64 + NT].rearrange("p (n c) -> p n c", c=64)[:, :, 0]

        def combine(dst, dn):
            self.ts('dve', tF, tF, mf, None, ALU.mult, None, [tfn, 'cw'], [tfn])
            self.stt('dve', dst, tB, mb, tF, ALU.mult, ALU.add, [tbn, 'cw', tfn], [dn])

        self.cp('dve', tF, V, [pzn], [tfn])
        self.ts('dve', tB, W, -1.0, None, ALU.mult, None, [pzn], [tbn])
        combine(G, gn_)
        self.tt('dve', c3(tF), c3(V), bcc(P0), ALU.subtract, [pzn], [tfn])
        self.act(tF, tF, AF.Exp, [tfn], [tfn])
        self.tt('dve', c3(tB), c3(W), bcc(Pl), ALU.subtract, [pzn], [tbn])
        self.act(tB, tB, AF.Exp, [tbn], [tbn], scale=-1.0)
        combine(X1, x1n)
        self.tt('dve', c3(tF), c3(V), bcc(Pl), ALU.subtract, [pzn], [tfn])
        self.act(tF, tF, AF.Exp, [tfn], [tfn], scale=-1.0)
        self.tt('dve', c3(tB), c3(W), bcc(P0), ALU.subtract, [pzn], [tbn])
        self.act(tB, tB, AF.Exp, [tbn], [tbn])
        combine(X2, x2n)
        self.memset('dve', X3, 0.0, [x3n])
        self.tt('dve', c3(X3), c3(X3), bcc(Pl), ALU.add, [x3n, pzn], [x3n])
        self.tt('dve', c3(X3), c3(X3), bcc(P0), ALU.subtract, [x3n, pzn], [x3n])
        self.act(X3, X3, AF.Exp, [x3n], [x3n])
        for qi, (src, srcn) in enumerate(((G, gn_), (BE, ben), (X1, x1n), (X2, x2n), (X3, x3n))):
            b = self.bank()
            for n in range(16):
                self.tr(self.ps[0:64, b, n * 16:(n + 1) * 16], src[:, n * 64:(n + 1) * 64], self.c('ident')[0:16, 0:16], [srcn, 'cw'], [('ps', b)])
            self.cp('act' if qi % 2 else 'dve', tok[:, qi * 256:(qi + 1) * 256], self.ps[0:64, b, 0:256], [('ps', b)], [(tokn, qi)])
        self.arena_reset(self.keep16, keep32=3 * NT + 1280)
        gn_, x1n, ben, tokn = 'gdG', 'gdX1', 'gdBE', 'gdtok'
        st, stn = self.f32('gd_st', 64, 2048)
        P.dma('sp', lambda e: e.dma_start(out=st, in_=self.gdst_d[l]), 'gdin', writes=[stn])
        stv = st.rearrange("p (s d h v) -> p s d h v", s=4, d=2, h=4)
        KKs, kksn = self.f32('gd_KK', 64, NT)
        QKs, qksn = self.f32('gd_QK', 64, NT)
        E3, e3n = self.f32('gd_E3', 64, NT)
        Tm, tmn = self.f32('gd_Tm', 64, NT)
        Yt, ytn = self.f32('gd_Yt', 64, NT)
        Y, yn = self.f32('gd_Y', 64, NT)
        Pm, pmn = self.f32('gd_P', 64, NT)
        og, ogn = self.f32('gd_og', 64, NT)
        S, sn = self.f32('gd_S', 64, 64)
        nw, nwn = self.f32('gd_nw', 64, 36)
        qn16, qnn = self.b16('gd_qn', 64, NT)
        kn16, knn = self.b16('gd_kn', 64, NT)
        v16, v16n = self.b16('gd_v16', 64, NT)
        sg16, sgn_ = self.b16('gd_sg', 64, NT)
        ktok, ktn = self.b16('gd_ktok', 64, NT)
        vtok, vtn = self.b16('gd_vtok', 64, NT)
        attnT, atn = self.b16('gd_attnT', 64, NT)
        T16, t16n = self.b16('gd_T16', 64, NT)
        kg, kgn = self.b16('gd_kg', 64, NT)
        kout, kon = self.b16('gd_kout', 64, NT)
        nw0T, nw0n = self.b16('gd_nw0T', 64, NT)
        qin, qinn = self.b16('gd_qin', 64, NT)
        Sall, saln = self.b16('gd_Sall', 64, NT)
        vnew, vnn = self.b16('gd_vnew', 64, NT)
        sq, sqn = self.b16('gd_sq', 64, NT)
        u3 = lambda ap: ap.rearrange("p (n c) -> p n c", c=64)
        ub = lambda ap: ap.unsqueeze(2).to_broadcast([64, 16, 64])
        mb3 = lambda ap: ap.unsqueeze(1).to_broadcast([64, 16, 64])
        G = self.ar32[0:16, 0:NT]
        X1 = self.ar32[0:16, NT:2 * NT]
        BE = self.ar32[0:16, 2 * NT:3 * NT]
        tokv = self.ar32[0:64, 3 * NT:3 * NT + 1280].rearrange("p (q n r) -> p q n r", q=5, r=16)
        m64 = self.c('m64')
        ident = self.c('ident')
        self.ts('dve', nw, self.w('gd_cw'), self.c('ncarry')[0:64, :], None, ALU.mult, None, ['lw', 'cw'], [nwn])
        cwv = self.w('gd_cw').rearrange("p (g k) -> p g k", k=3)
        nwv = nw.rearrange("p (g k) -> p g k", k=3)
        R, rn = E3, e3n
        cx, cxn = Tm, tmn

        def conv_silu(gi):
            self.act(cx, R, AF.Copy, [rn, 'lw'], [cxn], scale=cwv[:, gi, 1:2])
            self.stt('dve', cx[:, 1:NT], R[:, 0:NT - 1], cwv[:, gi, 0:1], cx[:, 1:NT], ALU.mult, ALU.add, [rn, 'lw', cxn], [cxn])
            self.stt('dve', cx[:, 0:NT - 1], R[:, 1:NT], cwv[:, gi, 2:3], cx[:, 0:NT - 1], ALU.mult, ALU.add, [rn, 'lw', cxn], [cxn])
            xs = R.rearrange("p (s t) -> p s t", t=256)
            ds = cx.rearrange("p (s t) -> p s t", t=256)
            self.stt('dve', ds[:, 1:4, 0], xs[:, 0:3, 255], nwv[:, gi, 0:1], ds[:, 1:4, 0], ALU.mult, ALU.add, [rn, nwn, cxn], [cxn])
            self.stt('dve', ds[:, 0:3, 255], xs[:, 1:4, 0], nwv[:, gi, 2:3], ds[:, 0:3, 255], ALU.mult, ALU.add, [rn, nwn, cxn], [cxn])
            self.act(cx, cx, AF.Silu, [cxn], [cxn])

        def bcast_row(src, srcn, row):
            bs = []
            for blk in range(2):
                b = self.bank()
                self.mm(self.ps[0:64, b, :], ident[0:16, row:row + 1].to_broadcast([16, 64]), src[:, blk * 512:(blk + 1) * 512], True, True,
                        [srcn, 'cw'], [('ps', b)])
                bs.append(b)
            return bs

        attnT2 = [(attnT, atn), self.b16in32('gd_attnT1', 64, NT)]
        vtok2 = [(vtok, vtn), self.b16in32('gd_vtok1', 64, NT)]
        sg2 = [(sg16, sgn_), self.b16in32('gd_sg1', 64, NT)]
        sqA, sqAn = self.b16in32('gd_sqA', 64, NT)
        rsA, rsAn = self.f32('gd_rsA', 64, NT)

        def headstart(hh):
            par = hh % 2
            vtk, vtkn = vtok2[par]
            sgb, sgbn = sg2[par]
            for which, (tvw, tn_, c0) in enumerate(((tqv, tqn, hh * 64), (tqv, tqn, 256 + hh * 64), (tvv, tvn, hh * 64))):
                self.proj(tvw, tn_, c0, 64, R, rn)
                yield
                conv_silu(which * 4 + hh)
                yield
                if which < 2:
                    self.rms_fm(cx, cxn, 64, NT, sq, sqn, Yt, ytn, 1.0)
                    dst, dn = (qn16, qnn) if which == 0 else (kn16, knn)
                    self.stt('dve', dst, cx, 0.125 if which == 0 else 1.0, Yt, ALU.mult, ALU.mult, [cxn, ytn], [dn])
                else:
                    self.cp('act', v16, cx, [cxn], [v16n])
                yield
            self.proj(tvv, tvn, 256 + hh * 64, 64, R, rn)
            self.act(sgb, R, AF.Silu, [rn], [sgbn])
            yield
            for (src, srcn, dst, dn) in ((kn16, knn, ktok, ktn), (v16, v16n, vtk, vtkn)):
                b = self.bank()
                pb = self.ps[:, b, :].bitcast(BF16)
                for n in range(16):
                    self.tr(pb[0:64, n * 64:(n + 1) * 64], src[:, n * 64:(n + 1) * 64], self.id16[0:64, 0:64], [srcn, 'cb16'], [('ps', b)])
                self.cp('act', dst, pb[0:64, :], [('ps', b)], [dn])
                yield
            for (rhs16, rhsn, dst, dn) in ((kn16, knn, KKs, kksn), (qn16, qnn, QKs, qksn)):
                for half in range(2):
                    b = self.bank()
                    for nn in range(8):
                        n = half * 8 + nn
                        cs = slice(n * 64, (n + 1) * 64)
                        self.mm(self.ps[0:64, b, nn * 64:(nn + 1) * 64], kn16[:, cs], rhs16[:, cs], True, True, [knn, rhsn], [('ps', b)])
                    self.cp('act' if half else 'dve', dst[:, half * 512:(half + 1) * 512], self.ps[0:64, b, :], [('ps', b)], [(dn, half)])
                yield

        def prep_inv(hh, d):
            r = d * 4 + hh
            aT, aTn = attnT2[d]
            m_incl_t = m64[:, 0:64] if d == 0 else m64[:, 128:192]
            m_str_t = m64[:, 64:128] if d == 0 else m64[:, 192:256]
            m_str_2 = m64[:, 192:256] if d == 0 else m64[:, 64:128]
            gb = bcast_row(G, gn_, r)
            for blk in range(2):
                hs = slice(blk * 512, (blk + 1) * 512)
                self.tt('dve', u3(E3[:, hs]), u3(self.ps[0:64, gb[blk], :]), tokv[:, 0, blk * 8:(blk + 1) * 8, r].unsqueeze(2).to_broadcast([64, 8, 64]),
                        ALU.subtract, [('ps', gb[blk]), tokn], [(e3n, blk)])
            yield
            self.ts('dve', Tm, E3, 0.0, None, ALU.min, None, [e3n], [tmn])
            self.act(Tm, Tm, AF.Exp, [tmn], [tmn])
            yield
            self.tt('dve', u3(Yt), u3(Tm), mb3(m_incl_t), ALU.mult, [tmn, 'cw'], [ytn])
            self.tt('dve', aT, Yt, QKs, ALU.mult, [ytn, qksn], [aTn])
            yield
            self.tt('dve', u3(Yt), u3(Tm), mb3(m_str_t), ALU.mult, [tmn, 'cw'], [ytn])
            self.tt('dve', Yt, Yt, KKs, ALU.mult, [ytn, kksn], [ytn])
            self.stt('dve', u3(Yt), u3(Yt), -1.0, ub(tokv[:, 1, :, 8 + r]), ALU.mult, ALU.mult, [ytn, tokn], [ytn])
            yield
            self.ts('dve', Tm, E3, 0.0, None, ALU.max, None, [e3n], [tmn])
            self.act(Tm, Tm, AF.Exp, [tmn], [tmn], scale=-1.0)
            yield
            self.tt('dve', u3(Y), u3(Tm), mb3(m_str_2), ALU.mult, [tmn, 'cw'], [yn])
            self.tt('dve', Y, Y, KKs, ALU.mult, [yn, kksn], [yn])
            bb = bcast_row(BE, ben, 8 + r)
            for blk in range(2):
                hs = slice(blk * 512, (blk + 1) * 512)
                self.stt('dve', Y[:, hs], Y[:, hs], -1.0, self.ps[0:64, bb[blk], :], ALU.mult, ALU.mult, [yn, ('ps', bb[blk])], [(yn, blk)])
            self.tt('dve', u3(Pm), u3(Yt), mb3(ident[0:64, 0:64]), ALU.add, [ytn, 'cw'], [pmn])
            yield
            for lev in range(1, 6):
                ba = [self.bank(), self.bank()]
                if lev < 5:
                    for n in range(16):
                        cs = slice(n * 64, (n + 1) * 64)
                        self.mm(self.ps[0:64, ba[n // 8], (n % 8) * 64:(n % 8 + 1) * 64], Y[:, cs], Yt[:, cs], True, True, [yn, ytn], [('ps', ba[n // 8])])
                    yield
                bbk = [self.bank(), self.bank()]
                for n in range(16):
                    cs = slice(n * 64, (n + 1) * 64)
                    self.mm(self.ps[0:64, bbk[n // 8], (n % 8) * 64:(n % 8 + 1) * 64], Yt[:, cs], Y[:, cs], True, True, [yn, ytn], [('ps', bbk[n // 8])])
                yield
                for half in range(2):
                    hs = slice(half * 512, (half + 1) * 512)
                    if lev < 5:
                        self.cp('act', Yt[:, hs], self.ps[0:64, ba[half], :], [('ps', ba[half])], [(ytn, half)])
                    self.cp('act', Y[:, hs], self.ps[0:64, bbk[half], :], [('ps', bbk[half])], [(yn, half)])
                yield
                bp = [self.bank(), self.bank()]
                for n in range(16):
                    cs = slice(n * 64, (n + 1) * 64)
                    self.mm(self.ps[0:64, bp[n // 8], (n % 8) * 64:(n % 8 + 1) * 64], Y[:, cs], Pm[:, cs], True, True, [yn, pmn], [('ps', bp[n // 8])])
                yield
                for half in range(2):
                    hs = slice(half * 512, (half + 1) * 512)
                    self.tt('dve', Pm[:, hs], Pm[:, hs], self.ps[0:64, bp[half], :], ALU.add, [(pmn, half), ('ps', bp[half])], [(pmn, half)])
                yield

        def finish(hh, d):
            r = d * 4 + hh
            self.cp('act', T16, Pm, [pmn], [t16n])
            self.tt('dve', u3(kg), u3(ktok), ub(tokv[:, 2, :, r]), ALU.mult, [ktn, tokn], [kgn])
            self.tt('dve', u3(kout), u3(ktok), ub(tokv[:, 3, :, r]), ALU.mult, [ktn, tokn], [kon])
            xb = bcast_row(X1, x1n, r)
            for blk in range(2):
                hs = slice(blk * 512, (blk + 1) * 512)
                self.tt('dve', qin[:, hs], qn16[:, hs], self.ps[0:64, xb[blk], :], ALU.mult, [qnn, ('ps', xb[blk])], [(qinn, blk)])
            for half in range(2):
                b = self.bank()
                for nn in range(8):
                    n = half * 8 + nn
                    cs = slice(n * 64, (n + 1) * 64)
                    self.mm(self.ps[0:64, b, nn * 64:(nn + 1) * 64], kg[:, cs], T16[:, cs], True, True, [kgn, t16n], [('ps', b)])
                self.ts('dve', nw0T[:, half * 512:(half + 1) * 512], self.ps[0:64, b, :], -1.0, None, ALU.mult, None, [('ps', b)], [(nw0n, half)])
            yield

        def chain(hh, d):
            r = d * 4 + hh
            vtk, vtkn = vtok2[hh % 2]
            self.memset('dve', S, 0.0, [sn])
            order = range(16) if d == 0 else range(15, -1, -1)
            for n in order:
                seg = n // 4
                first = (n % 4 == 0) if d == 0 else (n % 4 == 3)
                last = (n % 4 == 3) if d == 0 else (n % 4 == 0)
                cs = slice(n * 64, (n + 1) * 64)
                if first:
                    self.stt('dve', S, S, self.c('carry')[0:64, :], stv[:, seg, d, hh, :], ALU.mult, ALU.add, [sn, 'cw', (stn, seg, d, hh)], [sn])
                self.cp('dve', Sall[:, cs], S, [sn], [(saln, n)])
                b = self.bank()
                self.mm(self.ps[0:64, b, 0:64], T16[:, cs], vtk[:, cs], True, False, [t16n, vtkn], [('ps', b)])
                self.mm(self.ps[0:64, b, 0:64], nw0T[:, cs], Sall[:, cs], False, True, [(nw0n, n // 8), (saln, n)], [('ps', b)])
                self.ts('dve', vnew[:, cs], self.ps[0:64, b, 0:64], tokv[:, 1, n, 8 + r:9 + r], None, ALU.mult, None, [('ps', b), tokn], [(vnn, n)])
                yield
                b2 = self.bank()
                self.mm(self.ps[0:64, b2, 0:64], kout[:, cs], vnew[:, cs], True, True, [kon, (vnn, n)], [('ps', b2)])
                self.stt('dve', S, S, tokv[:, 4, n, r:r + 1], self.ps[0:64, b2, 0:64], ALU.mult, ALU.add, [sn, tokn, ('ps', b2)], [sn])
                if last:
                    self.cp('dve', stv[:, seg, d, hh, :], S, [sn], [(stn, seg, d, hh)])
                yield

        def outp(hh, d):
            aT, aTn = attnT2[d]
            for half in range(2):
                b = self.bank()
                for nn in range(8):
                    n = half * 8 + nn
                    cs = slice(n * 64, (n + 1) * 64)
                    self.mm(self.ps[0:64, b, nn * 64:(nn + 1) * 64], Sall[:, cs], qin[:, cs], True, False, [(saln, n), qinn], [('ps', b)])
                    self.mm(self.ps[0:64, b, nn * 64:(nn + 1) * 64], vnew[:, cs], aT[:, cs], False, True, [(vnn, n), aTn], [('ps', b)])
                osl = og[:, half * 512:(half + 1) * 512]
                if d == 0:
                    self.cp('act', osl, self.ps[0:64, b, :], [('ps', b)], [(ogn, half)])
                else:
                    self.tt('dve', osl, osl, self.ps[0:64, b, :], ALU.add, [(ogn, half), ('ps', b)], [(ogn, half)])
                yield

        def normg(hh):
            sgb, sgbn = sg2[hh % 2]
            self.rms_fm(og, ogn, 64, NT, sqA, sqAn, rsA, rsAn, 1.0 / 64)
            self.stt('dve', og, og, self.w('gd_gn'), rsA, ALU.mult, ALU.mult, [ogn, 'lw', rsAn], [ogn])
            po = (hh % 2) * 64
            self.tt('dve', self.omix[po:po + 64, 6 + hh // 2, :], og, sgb, ALU.mult, [ogn, sgbn], ['omix'])
            yield

        def seq(*gens):
            for g in gens:
                yield from g

        def run(g):
            for _ in g:
                pass

        def interleave(a, b, ra=1, rb=1):
            alive = [a, b]
            rates = {id(a): ra, id(b): rb}
            while alive:
                for g in list(alive):
                    for _ in range(rates[id(g)]):
                        try:
                            next(g)
                        except StopIteration:
                            alive.remove(g)
                            break

        run(headstart(0))
        run(prep_inv(0, 0))
        run(finish(0, 0))
        for hh in range(4):
            interleave(seq(chain(hh, 0), outp(hh, 0)), prep_inv(hh, 1), 1, 1)
            run(finish(hh, 1))
            if hh < 3:
                interleave(seq(chain(hh, 1), outp(hh, 1), normg(hh)), seq(headstart(hh + 1), prep_inv(hh + 1, 0)), 1, 1)
                run(finish(hh + 1, 0))
            else:
                run(seq(chain(hh, 1), outp(hh, 1), normg(hh)))
        P.dma('sp', lambda e: e.dma_start(out=self.gdo_d[l], in_=st), 'gdo', reads=[stn])

    def rms_fm(self, x, xn, parts, n, sq, sqn, rstd, rn, scale, ones=None):
        self.act(sq[0:parts, 0:n], x, AF.Square, [xn], [sqn])
        c0 = 0
        while c0 < n:
            w = min(512, n - c0)
            b = self.bank()
            self.mm(self.ps[0:parts, b, 0:w], (self.ones16 if ones is None else ones)[0:parts, 0:parts], sq[0:parts, c0:c0 + w], True, True, [sqn, 'cb16'], [('ps', b)])
            self.rsqrt(rstd[0:parts, c0:c0 + w], self.ps[0:parts, b, 0:w], scale, self.epsc[0:parts, :], [('ps', b), 'misc'], [(rn, c0)])
            c0 += w

    def mla(self, l):
        P = self.P
        self.arena_reset(self.keep16)
        self.ada_tmp = self.f32('ada_tmp', 128, 1024)
        wt, wn = self.abmla
        wv = wt.rearrange("p (k c) -> p k c", c=512)
        cq, cqn = self.f32('m_cq', 128, 2 * NT)
        ckv, ckvn = self.f32('m_ckv', 128, NT)
        kr, krn = self.f32('m_kr', 32, NT)
        cosq, cosn = self.f32('m_cos', 96, NT)
        sinq, sinn = self.f32('m_sin', 96, NT)
        rstd, rsn = self.f32('m_rstd', 128, 1280)
        qh, qhn = self.f32('m_qh', 96, NT)
        tmp, tmn = self.f32('m_tmp', 96, NT)
        kh, khn = self.f32('m_kh', 96, 1280)
        sq, sqn = self.b16('m_sq', 128, 1280)
        cqn16, cq16n = self.b16('m_cqn', 128, 2 * NT)
        kvin, kvn = self.b16('m_kvin', 128, 1280)
        kr16, kr16n = self.b16('m_kr16', 32, 1280)
        vtok, vtn = self.b16('m_vtok', 128, 2560)
        vtv = vtok.rearrange("p (k c) -> p k c", c=256)
        wq16, wqn = self.b16('m_wq', 128, 768)
        wk16, wkn = self.b16('m_wk', 128, 384)
        wv16, wvn = self.b16('m_wv', 128, 256)
        rm16, rmn = self.b16('m_rm', 96, 96)
        e16, e16n = self.b16('m_e32', 32, 96)
        xn16, xn16n = self.b16('m_xn16', 96, 1280)
        qrot, qrn = self.b16('m_qrot', 104, NT)
        krot, krotn = self.b16('m_krot', 104, 1280)
        vext, vxn = self.b16('m_vext', 128, 1280)
        vxv = vext.rearrange("p (k c) -> p k c", c=128)
        pT, ptn = self.b16('m_pT', 128, 1024)
        self.cbig('cosq', cosq, cosn)
        self.cbig('sinq', sinq, sinn)
        o, p_, c_ = CB.items['ek']
        P.dma('pool', lambda e: e.dma_start(out=krot[96:104, :], in_=self.cb_d[0:8, o:o + 1280]), 'cb2', writes=[(krotn, 'm')])
        o2 = CB.items['eq'][0]
        P.dma('pool', lambda e: e.dma_start(out=qrot[96:104, :], in_=self.cb_d[0:8, o2:o2 + 1024]), 'cb2', writes=[(qrn, 'm')])
        self.memset('pool', vext, 1.0, [vxn])
        P.dma('pool', lambda e: e.dma_start(out=kvin[:, 1024:1280], in_=self.ctxkv_d[l]), 'cb2', writes=[(kvn, 1)])
        P.dma('pool', lambda e: e.dma_start(out=kr16[:, 1024:1280], in_=self.ctxkr_d[l]), 'cb2', writes=[(kr16n, 1)])
        for (dst, dn, src) in ((wq16, wqn, 'm_wq'), (wk16, wkn, 'm_wk'), (wv16, wvn, 'm_wv')):
            self.cp('dve', dst, self.w(src), ['lw'], [dn])
        self.cp('dve', rm16, self.c('rm'), ['cw'], [rmn])
        self.cp('dve', e16, self.c('e32'), ['cw'], [e16n])
        self.proj(wv, wn, 16, 128, cq[:, 0:NT], (cqn, 0))
        self.proj(wv, wn, 144, 128, cq[:, NT:2 * NT], (cqn, 1))
        self.proj(wv, wn, 272, 128, ckv, ckvn)
        self.proj(wv, wn, 400, 32, kr, krn)
        P.dma('sp', lambda e: e.dma_start(out=self.krT_d[l], in_=kr), 'kro', reads=[krn])
        self.cp('act', kr16[:, 0:NT], kr, [krn], [(kr16n, 0)])
        yield
        banks = [self.bank(), self.bank()]
        for c in range(2):
            self.act(sq[:, 0:NT], cq[:, c * NT:(c + 1) * NT], AF.Square, [(cqn, c)], [sqn])
            for blk in range(2):
                self.mm(self.ps[:, banks[blk], :], self.ones16, sq[:, blk * 512:(blk + 1) * 512], c == 0, c == 1, [sqn, 'cb16'], [('ps', banks[blk])])
        for blk in range(2):
            self.rsqrt(rstd[:, blk * 512:(blk + 1) * 512], self.ps[:, banks[blk], :], 1.0 / 256, self.epsc, [('ps', banks[blk]), 'misc'], [(rsn, blk)])
        for c in range(2):
            self.stt('dve', cqn16[:, c * NT:(c + 1) * NT], cq[:, c * NT:(c + 1) * NT], self.w('m_qa')[:, c:c + 1], rstd[:, 0:NT], ALU.mult, ALU.mult,
                     [(cqn, c), 'lw', rsn], [(cq16n, c)])
        self.rms_fm(ckv, ckvn, 128, NT, sq, sqn, rstd, rsn, 1.0 / 128)
        self.stt('dve', ckv, ckv, self.w('m_kva'), rstd[:, 0:NT], ALU.mult, ALU.mult, [ckvn, 'lw', rsn], [ckvn])
        P.dma('sp', lambda e: e.dma_start(out=self.ckvT_d[l], in_=ckv), 'ckvo', reads=[ckvn])
        self.cp('act', kvin[:, 0:NT], ckv, [ckvn], [(kvn, 0)])
        for kc in range(10):
            b = self.bank()
            self.mm(self.ps[:, b, 0:256], kvin[:, kc * 128:(kc + 1) * 128], wv16, True, True, [kvn, wvn], [('ps', b)])
            self.cp('act' if kc % 2 else 'dve', vtv[:, kc, :], self.ps[:, b, 0:256], [('ps', b)], [(vtn, kc)])
        yield
        self.nbank = 7
        scale = 96.0 ** -0.5
        qrot1, qr1n = self.b16in32('m_qrot1', 104, NT)
        krot1, kr1n = self.b16in32('m_krot1', 104, 1280)
        vext1, vx1n = self.b16in32('m_vext1', 128, 1280)
        rec, recn = self.f32('m_rec', 64, 512)
        P.dma('pool', lambda e: e.dma_start(out=krot1[96:104, :], in_=self.cb_d[0:8, o:o + 1280]), 'cb2', writes=[(kr1n, 'm')])
        P.dma('pool', lambda e: e.dma_start(out=qrot1[96:104, :], in_=self.cb_d[0:8, o2:o2 + 1024]), 'cb2', writes=[(qr1n, 'm')])
        self.memset('pool', vext1, 1.0, [vx1n])
        qrot2 = [(qrot, qrn), (qrot1, qr1n)]
        krot2 = [(krot, krotn), (krot1, kr1n)]
        vext2 = [(vext, vxn), (vext1, vx1n)]

        def prep(hh):
            qro, qron = qrot2[hh % 2]
            kro, kron = krot2[hh % 2]
            vx, vxn_ = vext2[hh % 2]
            vxv_ = vx.rearrange("p (k c) -> p k c", c=128)
            self.cp('pool', vxv_[:, :, 0:64], vtv[:, :, hh * 64:(hh + 1) * 64], [vtn], [vxn_])
            for blk in range(2):
                b = self.bank()
                for c in range(2):
                    self.mm(self.ps[0:96, b, :], wq16[:, c * 384 + hh * 96:c * 384 + (hh + 1) * 96], cqn16[:, c * NT + blk * 512:c * NT + (blk + 1) * 512],
                            c == 0, c == 1, [wqn, cq16n], [('ps', b)])
                self.cp('act', qh[:, blk * 512:(blk + 1) * 512], self.ps[0:96, b, :], [('ps', b)], [(qhn, blk)])
            yield
            self.rms_fm(qh, qhn, 96, NT, sq, sqn, rstd, rsn, 1.0 / 96)
            self.stt('dve', qh, qh, self.w('m_gq'), rstd[0:96, 0:NT], ALU.mult, ALU.mult, [qhn, 'lw', rsn], [qhn])
            self.cp('act', xn16[:, 0:NT], qh, [qhn], [xn16n])
            yield
            for blk in range(2):
                b = self.bank()
                sl = slice(blk * 512, (blk + 1) * 512)
                self.mm(self.ps[0:96, b, :], rm16, xn16[:, sl], True, True, [rmn, xn16n], [('ps', b)])
                self.tt('dve', tmp[:, sl], self.ps[0:96, b, :], sinq[:, sl], ALU.mult, [('ps', b), sinn], [(tmn, blk)])
                self.tt('dve', qh[:, sl], qh[:, sl], cosq[:, sl], ALU.mult, [qhn, cosn], [(qhn, blk)])
                self.tt('dve', qro[0:96, sl], tmp[:, sl], qh[:, sl], ALU.add, [(tmn, blk), (qhn, blk)], [(qron, blk)])
            yield
            for (c0, w) in ((0, 512), (512, 512), (1024, 256)):
                b = self.bank()
                self.mm(self.ps[0:96, b, 0:w], wk16[:, hh * 96:(hh + 1) * 96], kvin[:, c0:c0 + w], True, False, [wkn, kvn], [('ps', b)])
                self.mm(self.ps[0:96, b, 0:w], e16, kr16[:, c0:c0 + w], False, True, [e16n, kr16n], [('ps', b)])
                self.cp('act', kh[:, c0:c0 + w], self.ps[0:96, b, 0:w], [('ps', b)], [(khn, c0)])
            yield
            self.rms_fm(kh, khn, 96, 1280, sq, sqn, rstd, rsn, 1.0 / 96)
            self.stt('dve', kh, kh, self.w('m_gk'), rstd[0:96, 0:1280], ALU.mult, ALU.mult, [khn, 'lw', rsn], [khn])
            self.cp('act', xn16, kh, [khn], [xn16n])
            yield
            self.cp('dve', kro[0:96, 1024:1280], kh[:, 1024:1280], [khn], [(kron, 2)])
            for blk in range(2):
                b = self.bank()
                sl = slice(blk * 512, (blk + 1) * 512)
                self.mm(self.ps[0:96, b, :], rm16, xn16[:, sl], True, True, [rmn, xn16n], [('ps', b)])
                self.tt('dve', tmp[:, sl], self.ps[0:96, b, :], sinq[:, sl], ALU.mult, [('ps', b), sinn], [(tmn, blk)])
                self.tt('dve', kh[:, sl], kh[:, sl], cosq[:, sl], ALU.mult, [khn, cosn], [(khn, blk)])
                self.tt('dve', kro[0:96, sl], tmp[:, sl], kh[:, sl], ALU.add, [(tmn, blk), (khn, blk)], [(kron, blk)])
            yield

        def attn(hh):
            qro, qron = qrot2[hh % 2]
            kro, kron = krot2[hh % 2]
            vx, vxn_ = vext2[hh % 2]
            vxv_ = vx.rearrange("p (k c) -> p k c", c=128)
            for qb in range(2):
                qs = slice(qb * 512, (qb + 1) * 512)

                def score(kc):
                    b = self.bank()
                    ks = slice(kc * 128, (kc + 1) * 128)
                    self.mm(self.ps[:, b, :], kro[:, ks], qro[:, qs], True, True, [kron, qron], [('ps', b)])
                    return b
                bnext = score(0)
                for kc in range(10):
                    b = bnext
                    pt = pT[:, (kc % 2) * 512:(kc % 2 + 1) * 512]
                    self.act(pt, self.ps[:, b, :], AF.Exp, [('ps', b)], [(ptn, kc % 2)], scale=scale)
                    if kc < 9:
                        bnext = score(kc + 1)
                    self.mm(self.ps[:, 7, :], vxv_[:, kc, :], pt, kc == 0, kc == 9, [vxn_, (ptn, kc % 2)], [('ps', 7)])
                    if kc % 2:
                        yield
                self.recip(rec, self.ps[64:128, 7, :], [('ps', 7)], [recn])
                po = (hh % 2) * 64
                self.tt('dve', self.omix[po:po + 64, 4 + hh // 2, qs], self.ps[0:64, 7, :], rec, ALU.mult, [('ps', 7), recn], ['omix'])
                yield

        def inter(a_, b_):
            alive = [a_, b_]
            while alive:
                for gg in list(alive):
                    try:
                        next(gg)
                        yield
                    except StopIteration:
                        alive.remove(gg)

        yield from prep(0)
        for hh in range(4):
            if hh < 3:
                yield from inter(attn(hh), prep(hh + 1))
            else:
                yield from attn(hh)
        self.nbank = 8

PROMPT_ASSIGN = [[0, 1, 2], [3, 4, 5], [6, 7, 8], [9, 10, 11], [12, 13], [14, 15]]


def core_plan():
    plan = [(True, [0] * 4), (True, [1] * 4)]
    for seqs in PROMPT_ASSIGN:
        s4 = list(seqs) + [seqs[0]] * (4 - len(seqs))
        plan.append((False, s4))
    return plan


def make_in_maps(inp, depth=DEPTH):
    inp = {k: np.asarray(v) for k, v in inp.items()}
    wstream = pack_wstream(inp)
    wada = np.ascontiguousarray(inp['w_ada'].reshape(DEPTH, 8, 128, 18, 512).transpose(0, 3, 2, 1, 4).reshape(DEPTH, 18, 128, 4096))
    lw = np.stack([pack_layer_small(inp, l) for l in range(DEPTH)])
    import ml_dtypes
    dft_s, dft_p = dft_tiles(True).astype(ml_dtypes.bfloat16), dft_tiles(False).astype(ml_dtypes.bfloat16)
    maps = []
    for ci, (is_s, segs) in enumerate(core_plan()):
        m = {}
        if is_s:
            b = segs[0]
            x = inp['x_sample'][b]
            cond = inp['c'][b]
            ckv = inp['cache_mla_ckv'][b].transpose(0, 2, 1)
            ckr = inp['cache_mla_krope'][b].transpose(0, 2, 1)
            hg = np.zeros((DEPTH, 64, 4, 2, 4, 64), np.float32)
            gd = np.zeros((DEPTH, 64, 4, 2, 4, 64), np.float32)
            hg[:, :, 0, 0] = inp['state_hgrn'][b][:, 0].transpose(0, 2, 1, 3)
            hg[:, :, 3, 1] = inp['state_hgrn'][b][:, 1].transpose(0, 2, 1, 3)
            gd[:, :, 0, 0] = inp['state_gdn'][b][:, 0].transpose(0, 2, 1, 3)
            gd[:, :, 3, 1] = inp['state_gdn'][b][:, 1].transpose(0, 2, 1, 3)
        else:
            x = np.concatenate([inp['x_prompt'][s] for s in segs], axis=0)
            cond = inp['c_ctx']
            ckv = np.zeros((DEPTH, 128, 256), np.float32)
            ckr = np.zeros((DEPTH, 32, 256), np.float32)
            hg = np.zeros((DEPTH, 64, 4, 2, 4, 64), np.float32)
            gd = np.zeros((DEPTH, 64, 4, 2, 4, 64), np.float32)
        m['xT'] = np.ascontiguousarray(x.T.reshape(8, 128, NT))
        m['cw'], m['cb'] = core_consts(is_s, cond)
        m['lw'] = lw
        m['wstream'] = wstream
        m['wada'] = wada
        m['dft'] = dft_s if is_s else dft_p
        m['ctxkv'] = np.ascontiguousarray(ckv)
        m['ctxkr'] = np.ascontiguousarray(ckr)
        m['hgst'] = np.ascontiguousarray(hg.reshape(DEPTH, 64, 2048))
        m['gdst'] = np.ascontiguousarray(gd.reshape(DEPTH, 64, 2048))
        maps.append(m)
    return maps


def assemble(results):
    y_prompt = np.zeros((16, 256, D), np.float32)
    y_sample = np.zeros((2, 1024, D), np.float32)
    n_ckv = np.zeros((16, DEPTH, 256, 128), np.float32)
    n_kr = np.zeros((16, DEPTH, 256, 32), np.float32)
    n_hg = np.zeros((16, DEPTH, 2, 4, 64, 64), np.float32)
    n_gd = np.zeros((16, DEPTH, 2, 4, 64, 64), np.float32)
    for ci, (is_s, segs) in enumerate(core_plan()):
        r = results[ci]
        y = r['yT'].reshape(D, NT).T
        if is_s:
            y_sample[segs[0]] = y
            continue
        hgo = r['hgo'].reshape(DEPTH, 64, 4, 2, 4, 64)
        gdo = r['gdo'].reshape(DEPTH, 64, 4, 2, 4, 64)
        for g, s in enumerate(segs):
            if g > 0 and s == segs[0]:
                continue
            y_prompt[s] = y[g * 256:(g + 1) * 256]
            n_ckv[s] = r['ckvT'][:, :, g * 256:(g + 1) * 256].transpose(0, 2, 1)
            n_kr[s] = r['krT'][:, :, g * 256:(g + 1) * 256].transpose(0, 2, 1)
            n_hg[s] = hgo[:, :, g].transpose(0, 2, 3, 1, 4)
            n_gd[s] = gdo[:, :, g].transpose(0, 2, 3, 1, 4)
    return (y_prompt, y_sample, n_ckv, n_kr, n_hg, n_gd)


def build_nc(depth=DEPTH, stages=None, dbg=None):
    nc = bass.Bass("TRN2", target_bir_lowering=False)
    with contextlib.ExitStack() as st:
        b = B(nc, st, depth, stages, dbg)
        b.build()
    return nc


def kernel(**inputs):
    maps = make_in_maps(inputs)
    nc = build_nc()
    res = run_bass_kernel_spmd(nc, maps, core_ids=list(range(NCORES)))
    return assemble(res.results)
```
